# Optimizing a Trainium2 kernel written in Bass

```python
import math
import jax, jax.numpy as jnp
from jax import lax
import numpy as np

D_MODEL = 2048
BATCH = 4
SEQ = 4096
DEPTH = 2

EXPAND = 2
D_INNER = EXPAND * D_MODEL
HEAD_DIM = 128
N_HEADS_SB = (D_INNER // 2) // HEAD_DIM
N_HEADS_DIL = (D_INNER // 2) // HEAD_DIM
D_SB = N_HEADS_SB * HEAD_DIM
D_DIL = N_HEADS_DIL * HEAD_DIM
DILATED_GROUPS = ((128, 1), (512, 4), (2048, 16))
SB_BLOCK = 128
N_HEADS_HGRN = D_INNER // HEAD_DIM
HGRN_CHUNK = 32
N_EVEN = (DEPTH + 1) // 2
N_ODD = DEPTH // 2
EVEN_IN_COLS = 3 * D_SB + 3 * D_DIL + D_INNER
ODD_IN_COLS = 4 * D_INNER
RMS_EPS = 1e-6

kernel_name = "hybrid_stickbreak_dilated_hgrn2"


def rms_norm(x, gain):
    xf = x.astype(jnp.float32)
    y = xf * lax.rsqrt(jnp.mean(xf * xf, axis=-1, keepdims=True) + RMS_EPS)
    return (y * gain.astype(jnp.float32)).astype(x.dtype)


def split_cols(t, sizes):
    idx = [int(c) for c in np.cumsum(sizes)[:-1]]
    return jnp.split(t, idx, axis=-1)


def to_heads(t):
    b, s, _ = t.shape
    return t.reshape(b, s, -1, HEAD_DIM)


def stick_breaking_attention(q, k, v):
    b, s_len, h, dh = q.shape
    scale = 1.0 / math.sqrt(dh)
    qf = q.astype(jnp.float32).transpose(0, 2, 1, 3)
    kf = k.astype(jnp.float32).transpose(0, 2, 1, 3)
    vf = v.astype(jnp.float32).transpose(0, 2, 1, 3)
    outs = []
    for blk in range(s_len // SB_BLOCK):
        q0 = blk * SB_BLOCK
        kend = q0 + SB_BLOCK
        z = jnp.einsum('bhqd,bhkd->bhqk', qf[:, :, q0:kend], kf[:, :, :kend]) * scale
        t_pos = q0 + jnp.arange(SB_BLOCK)[:, None]
        s_pos = jnp.arange(kend)[None, :]
        mask = s_pos < t_pos
        log_1m = jnp.where(mask, jax.nn.log_sigmoid(-z), 0.0)
        tail = lax.cumsum(log_1m, axis=3, reverse=True) - log_1m
        a = jnp.where(mask, jnp.exp(jax.nn.log_sigmoid(z) + tail), 0.0)
        outs.append(jnp.einsum('bhqk,bhkd->bhqd', a, vf[:, :, :kend]))
    return jnp.concatenate(outs, axis=2).transpose(0, 2, 1, 3)


def dilated_branch(q, k, v, window, dilation):
    b, s_len, h, dh = q.shape
    scale = 1.0 / math.sqrt(dh)
    blk = window // dilation
    L = s_len // dilation
    nblk = -(-L // blk)
    Lp = nblk * blk

    def to_blocks(t):
        t = t.reshape(b, L, dilation, h, dh).transpose(0, 2, 1, 3, 4)
        t = jnp.pad(t, ((0, 0), (0, 0), (0, Lp - L), (0, 0), (0, 0)))
        return t.reshape(b, dilation, nblk, blk, h, dh)

    def with_prev(t):
        prev = jnp.pad(t, ((0, 0), (0, 0), (1, 0), (0, 0), (0, 0), (0, 0)))[:, :, :-1]
        return jnp.concatenate([prev, t], axis=3)

    qb = to_blocks(q)
    kc = with_prev(to_blocks(k))
    vc = with_prev(to_blocks(v))
    sc = jnp.einsum('bpnqhd,bpnkhd->bpnhqk', qb, kc) * scale
    n_i = jnp.arange(nblk)[:, None, None]
    q_i = jnp.arange(blk)[None, :, None]
    k_j = jnp.arange(2 * blk)[None, None, :]
    dist = blk + q_i - k_j
    valid = (dist >= 0) & (dist <= blk) & ((n_i > 0) | (k_j >= blk))
    sc = jnp.where(valid[:, None], sc, -jnp.inf)
    lse = jax.nn.logsumexp(sc, axis=-1)
    p = jnp.exp(sc - lse[..., None])
    o = jnp.einsum('bpnhqk,bpnkhd->bpnqhd', p, vc)
    o = o.reshape(b, dilation, Lp, h, dh)[:, :, :L].transpose(0, 2, 1, 3, 4).reshape(b, s_len, h, dh)
    lse = lse.transpose(0, 1, 2, 4, 3).reshape(b, dilation, Lp, h)[:, :, :L]
    lse = lse.transpose(0, 2, 1, 3).reshape(b, s_len, h)
    return o, lse


def dilated_attention(q, k, v, q_gain, k_gain):
    qf = rms_norm(q.astype(jnp.float32), q_gain)
    kf = rms_norm(k.astype(jnp.float32), k_gain)
    vf = v.astype(jnp.float32)
    outs, lses = [], []
    for window, dilation in DILATED_GROUPS:
        o, lse = dilated_branch(qf, kf, vf, window, dilation)
        outs.append(o)
        lses.append(lse)
    wts = jax.nn.softmax(jnp.stack(lses, axis=0), axis=0)
    return jnp.sum(wts[..., None] * jnp.stack(outs, axis=0), axis=0)


def hgrn2(q, f_pre, inp, lb):
    b, s_len, h, d = q.shape
    lb = lb.reshape(h, d)
    fp = f_pre.astype(jnp.float32)
    log_f = jnp.logaddexp(jnp.log(lb), jnp.log1p(-lb) + jax.nn.log_sigmoid(fp))
    key = (1.0 - lb) * jax.nn.sigmoid(-fp)
    n_c = s_len // HGRN_CHUNK

    def chunks(t):
        return t.astype(jnp.float32).reshape(b, n_c, HGRN_CHUNK, h, d).transpose(1, 0, 3, 2, 4)

    causal = jnp.arange(HGRN_CHUNK)[:, None] >= jnp.arange(HGRN_CHUNK)[None, :]

    def step(state, xs):
        qc, kc, vc, gc = xs
        cum = jnp.cumsum(gc, axis=2)
        inter = jnp.einsum('bhtc,bhcv->bhtv', qc * jnp.exp(cum), state)
        diff = cum[:, :, :, None, :] - cum[:, :, None, :, :]
        decay = jnp.exp(jnp.where(causal[:, :, None], diff, -jnp.inf))
        attn = jnp.einsum('bhtc,bhsc,bhtsc->bhts', qc, kc, decay)
        intra = jnp.einsum('bhts,bhsv->bhtv', attn, vc)
        last = cum[:, :, -1:, :]
        new_state = jnp.exp(last[:, :, 0, :])[..., None] * state + jnp.einsum(
            'bhsc,bhsv->bhcv', kc * jnp.exp(last - cum), vc)
        return new_state, inter + intra

    init = jnp.zeros((b, h, d, d), jnp.float32)
    _, ys = lax.scan(step, init, (chunks(q), chunks(key), chunks(inp), chunks(log_f)))
    return ys.transpose(1, 0, 3, 2, 4).reshape(b, s_len, h, d)


def even_layer(x, norm_g, w_in, q_gain, k_gain, w_out):
    b, s_len, _ = x.shape
    hdn = rms_norm(x, norm_g)
    proj = jnp.einsum('bsd,de->bse', hdn, w_in)
    qa, ka, va, qb, kb, vb, gate = split_cols(
        proj, [D_SB, D_SB, D_SB, D_DIL, D_DIL, D_DIL, D_INNER])
    oa = stick_breaking_attention(to_heads(qa), to_heads(ka), to_heads(va))
    ob = dilated_attention(to_heads(qb), to_heads(kb), to_heads(vb), q_gain, k_gain)
    mix = jnp.concatenate([oa.reshape(b, s_len, D_SB), ob.reshape(b, s_len, D_DIL)], axis=-1)
    mix = mix * jax.nn.silu(gate.astype(jnp.float32))
    return x + jnp.einsum('bse,ed->bsd', mix.astype(x.dtype), w_out)


def odd_layer(x, norm_g, w_in, lb, o_gain, w_out):
    b, s_len, _ = x.shape
    hdn = rms_norm(x, norm_g)
    proj = jnp.einsum('bsd,de->bse', hdn, w_in)
    q, f_pre, inp, gate = split_cols(proj, [D_INNER, D_INNER, D_INNER, D_INNER])
    o = hgrn2(to_heads(q), to_heads(f_pre), to_heads(inp), lb)
    o = rms_norm(o, o_gain).reshape(b, s_len, D_INNER)
    mix = o * jax.nn.silu(gate.astype(jnp.float32))
    return x + jnp.einsum('bse,ed->bsd', mix.astype(x.dtype), w_out)


def setup_inputs(seed: int = 0) -> dict:
    key = jax.random.key(seed)
    ks = jax.random.split(key, 12)
    f32 = jnp.float32
    x = jax.random.normal(ks[0], (BATCH, SEQ, D_MODEL), f32)
    norm_even = 1.0 + 0.02 * jax.random.normal(ks[1], (N_EVEN, D_MODEL), f32)
    w_in_even = jax.random.normal(ks[2], (N_EVEN, D_MODEL, EVEN_IN_COLS), f32) * D_MODEL ** -0.5
    q_norm_even = 1.0 + 0.02 * jax.random.normal(ks[3], (N_EVEN, HEAD_DIM), f32)
    k_norm_even = 1.0 + 0.02 * jax.random.normal(ks[4], (N_EVEN, HEAD_DIM), f32)
    w_out_even = jax.random.normal(ks[5], (N_EVEN, D_INNER, D_MODEL), f32) * D_INNER ** -0.5
    norm_odd = 1.0 + 0.02 * jax.random.normal(ks[6], (N_ODD, D_MODEL), f32)
    w_in_odd = jax.random.normal(ks[7], (N_ODD, D_MODEL, ODD_IN_COLS), f32) * D_MODEL ** -0.5
    lb_logits = 0.1 * jax.random.normal(ks[8], (DEPTH, D_INNER), f32)
    o_norm_odd = 1.0 + 0.02 * jax.random.normal(ks[9], (N_ODD, HEAD_DIM), f32)
    w_out_odd = jax.random.normal(ks[10], (N_ODD, D_INNER, D_MODEL), f32) * D_INNER ** -0.5
    return {"x": x, "norm_even": norm_even, "w_in_even": w_in_even,
            "q_norm_even": q_norm_even, "k_norm_even": k_norm_even, "w_out_even": w_out_even,
            "norm_odd": norm_odd, "w_in_odd": w_in_odd, "lb_logits": lb_logits,
            "o_norm_odd": o_norm_odd, "w_out_odd": w_out_odd}


def reference(x, norm_even, w_in_even, q_norm_even, k_norm_even, w_out_even,
              norm_odd, w_in_odd, lb_logits, o_norm_odd, w_out_odd):
    p = jax.nn.softmax(lb_logits.astype(jnp.float32), axis=0)
    lb_all = jnp.cumsum(p, axis=0) - p[0:1]
    for layer in range(DEPTH):
        j = layer // 2
        if layer % 2 == 0:
            x = even_layer(x, norm_even[j], w_in_even[j], q_norm_even[j], k_norm_even[j], w_out_even[j])
        else:
            x = odd_layer(x, norm_odd[j], w_in_odd[j], lb_all[layer], o_norm_odd[j], w_out_odd[j])
    return x
```

```python
import math
import numpy as np
from contextlib import ExitStack
import concourse.bass as bass
import concourse.mybir as mybir
from concourse.bass_utils import run_bass_kernel_spmd


F32 = mybir.dt.float32
BF16 = mybir.dt.bfloat16
AF = mybir.ActivationFunctionType
ALU = mybir.AluOpType
AX = mybir.AxisListType

NDMA_SLOTS = 8
_UNIQ = [0]


def _sbt(nc, name, shape, dtype):
    _UNIQ[0] += 1
    return nc.sbuf_tensor("%s_u%d" % (name, _UNIQ[0]), shape, dtype)


class Sched:
    def __init__(self, nc, es):
        self.nc = nc
        self.eng = {"pe": nc.tensor, "act": nc.scalar, "dve": nc.vector,
                    "pool": nc.gpsimd, "sp": nc.sync}
        self.sem = {e: es.enter_context(nc.semaphore("s_" + e)) for e in self.eng}
        self.cnt = {e: 0 for e in self.eng}
        self.dq = {"sp": "sp", "pool": "pool", "act": "act"}
        self.dsem = {q: [es.enter_context(nc.semaphore("d_%s%d" % (q, i)))
                         for i in range(NDMA_SLOTS)] for q in self.dq}
        self.dcnt = {q: 0 for q in self.dq}
        self.waited = {e: {} for e in self.eng}
        self.state = {}
        self.out_tokens = []
        self.nwaits = 0

    def _need(self, e, tok):
        if tok is None:
            return
        sem, val, key = tok
        if e == "pe" and key == "pe":
            return
        w = self.waited[e]
        if w.get(key, 0) >= val:
            return
        w[key] = val
        self.eng[e].wait_ge(sem, val)
        self.nwaits += 1

    def _st(self, root):
        s = self.state.get(root)
        if s is None:
            s = {"w": None, "r": [], "subs": {}}
            self.state[root] = s
        return s

    @staticmethod
    def _split(res):
        if isinstance(res, tuple):
            return res[0], res[1]
        return res, None

    def _deps(self, e, reads, writes):
        for res in reads:
            root, sub = self._split(res)
            s = self._st(root)
            self._need(e, s["w"])
            if sub is None:
                for ss in s["subs"].values():
                    self._need(e, ss["w"])
            else:
                ss = s["subs"].get(sub)
                if ss:
                    self._need(e, ss["w"])
        for res in writes:
            root, sub = self._split(res)
            s = self._st(root)
            self._need(e, s["w"])
            for t in s["r"]:
                self._need(e, t)
            if sub is None:
                for ss in s["subs"].values():
                    self._need(e, ss["w"])
                    for t in ss["r"]:
                        self._need(e, t)
            else:
                ss = s["subs"].get(sub)
                if ss:
                    self._need(e, ss["w"])
                    for t in ss["r"]:
                        self._need(e, t)

    def _record(self, tok, reads, writes):
        for res in reads:
            root, sub = self._split(res)
            s = self._st(root)
            if sub is None:
                s["r"] = [t for t in s["r"] if t[2] != tok[2]] + [tok]
            else:
                ss = s["subs"].setdefault(sub, {"w": None, "r": []})
                ss["r"] = [t for t in ss["r"] if t[2] != tok[2]] + [tok]
        for res in writes:
            root, sub = self._split(res)
            s = self._st(root)
            if sub is None:
                s["w"] = tok
                s["r"] = []
                s["subs"] = {}
            else:
                s["subs"][sub] = {"w": tok, "r": []}

    def op(self, e, fn, reads=(), writes=()):
        self._deps(e, reads, writes)
        ins = fn(self.eng[e])
        self.cnt[e] += 1
        ins.then_inc(self.sem[e], 1)
        tok = (self.sem[e], self.cnt[e], e)
        self._record(tok, reads, writes)
        return tok

    def dma(self, q, out, in_, reads=(), writes=(), is_output=False, **kw):
        e = self.dq[q]
        i = self.dcnt[q]
        slot = i % NDMA_SLOTS
        rnd = i // NDMA_SLOTS
        key = "d_%s%d" % (q, slot)
        sem = self.dsem[q][slot]
        if rnd > 0:
            self._need(e, (sem, 16 * rnd, key))
        self._deps(e, reads, writes)
        ins = self.eng[e].dma_start(out=out, in_=in_, **kw)
        ins.then_inc(sem, 16)
        self.dcnt[q] += 1
        tok = (sem, 16 * (rnd + 1), key)
        self._record(tok, reads, writes)
        if is_output:
            self.out_tokens.append(tok)
        return tok

    def collective(self, kind, ins, outs, groups, reads=(), writes=()):
        q, e = "pool", "pool"
        i = self.dcnt[q]
        slot, rnd = i % NDMA_SLOTS, i // NDMA_SLOTS
        key = "d_%s%d" % (q, slot)
        sem = self.dsem[q][slot]
        if rnd > 0:
            self._need(e, (sem, 16 * rnd, key))
        self._deps(e, reads, writes)
        ins_ = self.eng[e].collective_compute(kind, ALU.bypass, replica_groups=groups, ins=ins, outs=outs)
        ins_.then_inc(sem, 16)
        self.dcnt[q] += 1
        tok = (sem, 16 * (rnd + 1), key)
        self._record(tok, reads, writes)
        return tok

    def barrier(self):
        toks = [(self.sem[e], self.cnt[e], e) for e in self.eng if self.cnt[e] > 0]
        for q in self.dq:
            n = self.dcnt[q]
            for slot in range(NDMA_SLOTS):
                if n > slot:
                    rounds = (n - 1 - slot) // NDMA_SLOTS + 1
                    toks.append((self.dsem[q][slot], 16 * rounds, "d_%s%d" % (q, slot)))
        for e in self.eng:
            for t in toks:
                if not (t[2] == e):
                    self._need(e, t)
                elif e != "pe":
                    self._need(e, t)

    def finish(self):
        for t in self.out_tokens:
            self._need("sp", t)
        self.barrier()


S = 4096
D = 2048
NKC = D // 128
NTT = S // 512
EPS = 1e-6
MASKV = -200.0


class PsumPool:
    def __init__(self, nc, es, n=8, tiles=None, names=None):
        if tiles is None:
            self.t = [es.enter_context(nc.psum_tensor("ps%d" % i, [128, 512], F32)) for i in range(n)]
            self.names = ["ps%d" % i for i in range(n)]
        else:
            self.t, self.names = tiles, names
        self.i = 0

    def sub(self, idxs):
        return PsumPool(None, None, tiles=[self.t[i] for i in idxs], names=[self.names[i] for i in idxs])

    def next(self):
        k = self.i % len(self.t)
        self.i += 1
        return self.t[k], self.names[k]


class Rot:
    def __init__(self, nc, es, name, shape, dtype, n):
        self.t = [es.enter_context(_sbt(nc, "%s%d" % (name, i), shape, dtype)) for i in range(n)]
        self.names = ["%s%d" % (name, i) for i in range(n)]
        self.i = 0

    def next(self):
        k = self.i % len(self.t)
        self.i += 1
        return self.t[k], self.names[k]


def load_consts(nc, es, Sc, cst):
    ncols = 128 * 4 + 2048 + 256
    ct = es.enter_context(_sbt(nc, "consts", [128, ncols], BF16))
    Sc.dma("pool", ct[:], cst, writes=["consts"])
    c = {}
    c["ident"] = ct[:, 0:128]
    c["negtri"] = ct[:, 128:256]
    c["negones"] = ct[:, 256:384]
    c["ones"] = ct[:, 384:512]
    c["sbmask"] = [ct[:, 512 + m * 512: 512 + (m + 1) * 512] for m in range(4)]
    c["dmask"] = ct[:, 2560:2816]
    return c


def make_consts_np():
    ncols = 128 * 4 + 2048 + 256
    c = np.zeros((128, ncols), np.float32)
    j = np.arange(128)[:, None]
    s = np.arange(128)[None, :]
    c[:, 0:128] = np.eye(128)
    c[:, 128:256] = -1.0 * (j >= s)
    c[:, 256:384] = -1.0
    c[:, 384:512] = 1.0
    col = np.arange(512)[None, :]
    for m in range(4):
        valid = col > (m * 128 + j)
        c[:, 512 + m * 512: 512 + (m + 1) * 512] = np.where(valid, 0.0, MASKV)
    c[:, 2560:2688] = np.where(j <= s, 0.0, MASKV)
    c[:, 2688:2816] = np.where(j >= s, 0.0, MASKV)
    return c


def phase12(nc, Sc, PS, x, gain_bc, W, scr, heads, head_kinds, cst_extra, ntt_tok=32):
    with ExitStack() as es:
        hdnT = es.enter_context(_sbt(nc, "hdnT", [128, NKC, S], BF16))
        with ExitStack() as e1:
            gbc = e1.enter_context(_sbt(nc, "gbc", [128, D], F32))
            ss = e1.enter_context(_sbt(nc, "ss", [128, 32], F32))
            rstd = e1.enter_context(_sbt(nc, "rstd", [128, 32], F32))
            junk = e1.enter_context(_sbt(nc, "junk", [128, D], BF16))
            xt_r = Rot(nc, e1, "xt", [128, D], F32, 2)
            xs_r = Rot(nc, e1, "xs", [128, D], BF16, 2)
            Sc.dma("sp", gbc[:], gain_bc, writes=["gbc"])
            Sc.op("dve", lambda e: e.memset(ss[:], 0.0), writes=["ss"])
            for tt in range(ntt_tok):
                xt, xtn = xt_r.next()
                Sc.dma("sp", xt[:], x[tt * 128:(tt + 1) * 128, :], writes=[xtn])
                Sc.op("act", lambda e: e.activation(junk[:], xt[:], AF.Square, accum_out=ss[:, tt:tt + 1]),
                      reads=[xtn], writes=[("ss", tt)])
                Sc.op("act", lambda e: e.activation(rstd[:, tt:tt + 1], ss[:, tt:tt + 1], AF.Ln,
                                                    scale=1.0 / D, bias=EPS),
                      reads=[("ss", tt)], writes=[("rstd", tt)])
                Sc.op("act", lambda e: e.activation(rstd[:, tt:tt + 1], rstd[:, tt:tt + 1], AF.Exp, scale=-0.5),
                      reads=[("rstd", tt)], writes=[("rstd", tt)])
                xs, xsn = xs_r.next()
                Sc.op("dve", lambda e: e.scalar_tensor_tensor(xs[:], xt[:], rstd[:, tt:tt + 1], gbc[:],
                                                              ALU.mult, ALU.mult),
                      reads=[xtn, ("rstd", tt), "gbc"], writes=[xsn])
                for half in range(2):
                    ps, psn = PS.next()
                    psb = ps[:].bitcast(BF16)
                    for j in range(8):
                        kc = half * 8 + j
                        Sc.op("pe", lambda e: e.transpose(psb[:, j * 128:(j + 1) * 128],
                                                          xs[:, kc * 128:(kc + 1) * 128], cst_extra["ident"]),
                              reads=[xsn, "consts"], writes=[psn])
                    eng = "act" if half == 0 else "dve"
                    src = psb.rearrange("p (j t) -> p j t", j=8)
                    dst = hdnT[:, half * 8:(half + 1) * 8, tt * 128:(tt + 1) * 128]
                    if eng == "act":
                        Sc.op("act", lambda e: e.copy(dst, src), reads=[psn], writes=[("hdnT", (tt, half))])
                    else:
                        Sc.op("dve", lambda e: e.tensor_copy(dst, src), reads=[psn], writes=[("hdnT", (tt, half))])
        Sc.barrier()
        w_r = Rot(nc, es, "wt", [128, NKC, 512], BF16, 2)
        st_r = Rot(nc, es, "st", [128, 4, 512], BF16, 3)
        sq_r = Rot(nc, es, "sq", [128, 512], BF16, 2)
        rs_r = Rot(nc, es, "rs", [128, 512], F32, 2)
        eg_r = Rot(nc, es, "eg", [128, 512], F32, 2)
        for hd in heads:
            kind = head_kinds[hd]
            wt, wtn = w_r.next()
            for g4 in range(4):
                Sc.dma("pool", wt[:, g4 * 4:(g4 + 1) * 4, :], W[hd, :, g4 * 4:(g4 + 1) * 4, :],
                       writes=[(wtn, g4)])
            for tt in range(NTT):
                st, stn = st_r.next()
                pss = []
                for c in range(4):
                    ps, psn = PS.next()
                    pss.append((ps, psn))
                    for kc in range(NKC):
                        Sc.op("pe", lambda e: e.matmul(ps[:], wt[:, kc, c * 128:(c + 1) * 128],
                                                       hdnT[:, kc, tt * 512:(tt + 1) * 512],
                                                       start=(kc == 0), stop=(kc == NKC - 1)),
                              reads=[(wtn, kc // 4)], writes=[psn])
                (pq, pqn), (pk, pkn), (pv, pvn), (pg, pgn) = pss
                if kind == "sb":
                    sc = 1.0 / math.sqrt(128.0)
                    Sc.op("act", lambda e: e.mul(st[:, 0, :], pq[:], sc), reads=[pqn], writes=[(stn, 0)])
                    Sc.op("dve", lambda e: e.tensor_copy(st[:, 1, :], pk[:]), reads=[pkn], writes=[(stn, 1)])
                elif kind == "dil":
                    for ci, (pp, ppn, gcol) in enumerate(((pq, pqn, cst_extra["qg"]), (pk, pkn, cst_extra["kg"]))):
                        sq, sqn = sq_r.next()
                        Sc.op("act", lambda e: e.activation(sq[:], pp[:], AF.Square), reads=[ppn], writes=[sqn])
                        p2, p2n = PS.next()
                        Sc.op("pe", lambda e: e.matmul(p2[:], cst_extra["ones"], sq[:], start=True, stop=True),
                              reads=[sqn, "consts"], writes=[p2n])
                        rs, rsn = rs_r.next()
                        Sc.op("act", lambda e: e.activation(rs[:], p2[:], AF.Ln, bias=128.0 * EPS),
                              reads=[p2n], writes=[rsn])
                        Sc.op("act", lambda e: e.activation(rs[:], rs[:], AF.Exp, scale=-0.5),
                              reads=[rsn], writes=[rsn])
                        Sc.op("dve", lambda e: e.scalar_tensor_tensor(st[:, ci, :], pp[:], gcol, rs[:],
                                                                      ALU.mult, ALU.mult),
                              reads=[ppn, rsn, "gcols"], writes=[(stn, ci)])
                else:
                    Sc.op("act", lambda e: e.copy(st[:, 0, :], pq[:]), reads=[pqn], writes=[(stn, 0)])
                    Sc.op("dve", lambda e: e.tensor_copy(st[:, 1, :], pk[:]), reads=[pkn], writes=[(stn, 1)])
                Sc.op("act", lambda e: e.copy(st[:, 2, :], pv[:]), reads=[pvn], writes=[(stn, 2)])
                eg, egn = eg_r.next()
                Sc.op("act", lambda e: e.activation(eg[:], pg[:], AF.Exp, scale=-1.0), reads=[pgn], writes=[egn])
                Sc.op("dve", lambda e: e.tensor_scalar(eg[:], eg[:], 1.0, None, ALU.add), reads=[egn], writes=[egn])
                Sc.op("dve", lambda e: e.reciprocal(eg[:], eg[:]), reads=[egn], writes=[egn])
                Sc.op("dve", lambda e: e.tensor_tensor(st[:, 3, :], pg[:], eg[:], ALU.mult),
                      reads=[pgn, egn], writes=[(stn, 3)])
                dst = scr[hd, :, :, tt * 512:(tt + 1) * 512].rearrange("c p t -> p c t")
                Sc.dma("sp", dst, st[:], reads=[stn])
    Sc.barrier()


def load_head(nc, Sc, hb, hbn, scr, hd):
    for c in range(4):
        Sc.dma("sp", hb[:, c, :], scr[hd, c, :, :], writes=[(hbn, c)])


def make_vtok(nc, Sc, PS, cst, hb, hbn, vtok, vtokn, dil, srcT=None, srckey=None, fill=None):
    nb = 32 // dil
    if srcT is None:
        srcT, srckey = hb[:, 2, :], (hbn, 2)
    for g in range(8):
        ps, psn = PS.next()
        psb = ps[:].bitcast(BF16)
        for j in range(4):
            blk = g * 4 + j
            p, n = blk // nb, blk % nb
            start = p + dil * 128 * n
            src = srcT[:, start:start + dil * 127 + 1:dil]
            Sc.op("pe", lambda e: e.transpose(psb[:, j * 128:(j + 1) * 128], src, cst["ident"]),
                  reads=[srckey, "consts"], writes=[psn])
        dst = vtok[:, g * 4:(g + 1) * 4, :]
        srcp = psb[:, 0:512].rearrange("p (j t) -> p j t", j=4)
        if g % 2 == 0:
            Sc.op("dve", lambda e: e.tensor_copy(dst, srcp), reads=[psn], writes=[(vtokn, g)])
        else:
            Sc.op("act", lambda e: e.copy(dst, srcp), reads=[psn], writes=[(vtokn, g)])
        if fill:
            fill(1)


def sb_head(nc, Sc, PS, PSO, cst, R, hb, hbn, vtok, vtokn, out_dram, is_output=True):
    qT, kT, gT = hb[:, 0, :], hb[:, 1, :], hb[:, 3, :]
    for qt in range(NTT):
        nkb = 4 * (qt + 1)
        qs = qT[:, qt * 512:(qt + 1) * 512]
        o_ps, o_psn = PSO.next()
        carry = None
        carryn = None
        for idx, kb in enumerate(range(nkb - 1, -1, -1)):
            m = kb - 4 * qt
            diag = m >= 0
            ks = kT[:, kb * 128:(kb + 1) * 128]
            z_ps, z_psn = PS.next()
            Sc.op("pe", lambda e: e.matmul(z_ps[:], ks, qs, start=True, stop=not diag),
                  reads=[(hbn, 0), (hbn, 1)], writes=[z_psn])
            if diag:
                Sc.op("pe", lambda e: e.matmul(z_ps[:], cst["ident"], cst["sbmask"][m], start=False, stop=True),
                      reads=["consts"], writes=[z_psn])
            ee, een = R["e"].next()
            Sc.op("act", lambda e: e.activation(ee[:], z_ps[:], AF.Exp), reads=[z_psn], writes=[een])
            sp, spn = R["sp"].next()
            Sc.op("act", lambda e: e.activation(sp[:], ee[:], AF.Ln, bias=1.0), reads=[een], writes=[spn])
            a_ps, a_psn = PS.next()
            Sc.op("pe", lambda e: e.matmul(a_ps[:], ks, qs, start=True, stop=False),
                  reads=[(hbn, 0), (hbn, 1)], writes=[a_psn])
            if diag:
                Sc.op("pe", lambda e: e.matmul(a_ps[:], cst["ident"], cst["sbmask"][m], start=False, stop=False),
                      reads=["consts"], writes=[a_psn])
            Sc.op("pe", lambda e: e.matmul(a_ps[:], cst["negtri"], sp[:], start=False, stop=(carry is None)),
                  reads=["consts", spn], writes=[a_psn])
            if carry is not None:
                Sc.op("pe", lambda e: e.matmul(a_ps[:], cst["negones"], carry[:], start=False, stop=True),
                      reads=["consts", carryn], writes=[a_psn])
            at, atn = R["a"].next()
            Sc.op("act", lambda e: e.activation(at[:], a_ps[:], AF.Exp), reads=[a_psn], writes=[atn])
            Sc.op("pe", lambda e: e.matmul(o_ps[:], vtok[:, kb, :], at[:], start=(idx == 0), stop=(kb == 0)),
                  reads=[(vtokn, kb // 4), atn], writes=[o_psn])
            if kb > 0:
                if carry is None:
                    carry, carryn = sp, spn
                else:
                    nc_, ncn = R["carry"].next()
                    Sc.op("dve", lambda e: e.tensor_tensor(nc_[:], carry[:], sp[:], ALU.add),
                          reads=[carryn, spn], writes=[ncn])
                    carry, carryn = nc_, ncn
        ot, otn = R["o"].next()
        Sc.op("dve", lambda e: e.tensor_tensor(ot[:], o_ps[:], gT[:, qt * 512:(qt + 1) * 512], ALU.mult),
              reads=[o_psn, (hbn, 3)], writes=[otn])
        Sc.dma("sp", out_dram[:, qt * 512:(qt + 1) * 512], ot[:], reads=[otn], is_output=is_output)


def dil_cols(p, n, r):
    start = p + r * 128 * n
    return slice(start, start + r * 127 + 1, r)


def dil_head(nc, Sc, PS, cst, R, hb, hbn, vtoks, vtokns, UZ, out_dram, is_output=True):
    qT, kT, gT = hb[:, 0, :], hb[:, 1, :], hb[:, 3, :]
    sc = math.sqrt(128.0)
    for gi, r in enumerate((1, 4, 16)):
        make_vtok(nc, Sc, PS, cst, hb, hbn, vtoks[gi], vtokns[gi], r)
    for gi, r in enumerate((1, 4, 16)):
        nb = 32 // r
        vt, vtn = vtoks[gi], vtokns[gi]
        for p in range(r):
            for n in range(nb):
                blk = p * nb + n
                cq = dil_cols(p, n, r)
                w = 256 if n >= 1 else 128
                s_ps, s_psn = PS.next()
                Sc.op("pe", lambda e: e.matmul(s_ps[:, 0:128], kT[:, cq], qT[:, cq], start=True, stop=False),
                      reads=[(hbn, 0), (hbn, 1)], writes=[s_psn])
                Sc.op("pe", lambda e: e.matmul(s_ps[:, 0:128], cst["ident"], cst["dmask"][:, 0:128],
                                               start=False, stop=True), reads=["consts"], writes=[s_psn])
                if n >= 1:
                    ck = dil_cols(p, n - 1, r)
                    Sc.op("pe", lambda e: e.matmul(s_ps[:, 128:256], kT[:, ck], qT[:, cq], start=True, stop=False),
                          reads=[(hbn, 0), (hbn, 1)], writes=[s_psn])
                    Sc.op("pe", lambda e: e.matmul(s_ps[:, 128:256], cst["ident"], cst["dmask"][:, 128:256],
                                                   start=False, stop=True), reads=["consts"], writes=[s_psn])
                pt, ptn = R["p"].next()
                Sc.op("act", lambda e: e.activation(pt[:, 0:w], s_ps[:, 0:w], AF.Exp, scale=sc),
                      reads=[s_psn], writes=[ptn])
                uz_ps, uz_psn = PS.next()
                for half, lhs in enumerate((None, cst["ones"])):
                    o = uz_ps[:, half * 128:(half + 1) * 128]
                    l0 = vt[:, blk, :] if half == 0 else lhs
                    Sc.op("pe", lambda e: e.matmul(o, l0, pt[:, 0:128], start=True, stop=(n == 0)),
                          reads=[(vtn, blk // 4), ptn, "consts"], writes=[uz_psn])
                    if n >= 1:
                        l1 = vt[:, blk - 1, :] if half == 0 else lhs
                        Sc.op("pe", lambda e: e.matmul(o, l1, pt[:, 128:256], start=False, stop=True),
                              reads=[(vtn, (blk - 1) // 4), ptn, "consts"], writes=[uz_psn])
                acc = UZ[:, :, cq]
                src = uz_ps[:, 0:256].rearrange("p (a t) -> p a t", a=2)
                if gi == 0:
                    wr = [("uz0", n)]
                    if n % 2 == 0:
                        Sc.op("dve", lambda e: e.tensor_copy(acc, src), reads=[uz_psn], writes=wr)
                    else:
                        Sc.op("act", lambda e: e.copy(acc, src), reads=[uz_psn], writes=wr)
                else:
                    if gi == 1:
                        rd = [("uz0", 4 * n + i) for i in range(4)]
                        wr = [("uz1", (n, p))]
                    else:
                        rd = [("uz1", (c, p % 4)) for c in range(4 * n, 4 * n + 4)]
                        wr = [("uz2", (n, p))]
                    Sc.op("dve", lambda e: e.tensor_tensor(acc, acc, src, ALU.add),
                          reads=[uz_psn] + rd, writes=wr)
    for ch in range(8):
        cs = slice(ch * 512, (ch + 1) * 512)
        rz, rzn = R["rz"].next()
        Sc.op("dve", lambda e: e.reciprocal(rz[:], UZ[:, 1, cs]), reads=["uz0", "uz1", "uz2"], writes=[rzn])
        Sc.op("pool", lambda e: e.tensor_tensor(rz[:], rz[:], UZ[:, 0, cs], ALU.mult),
              reads=[rzn, "uz0", "uz1", "uz2"], writes=[rzn])
        ot, otn = R["o"].next()
        Sc.op("pool", lambda e: e.tensor_tensor(ot[:], rz[:], gT[:, cs], ALU.mult),
              reads=[rzn, (hbn, 3)], writes=[otn])
        Sc.dma("sp", out_dram[:, cs], ot[:], reads=[otn], is_output=is_output)


def hgrn_head(nc, Sc, PS, PSO, cst, R, T, hb, hbn, lbc, omlc, ogc, out_dram, is_output=True):
    qT, fT, vT, gT = hb[:, 0, :], hb[:, 1, :], hb[:, 2, :], hb[:, 3, :]
    E, B2, CUM, KT, QT, KTt, EL = T["E"], T["B2"], T["CUM"], T["KT"], T["QT"], T["KTt"], T["EL"]
    ktok, vtok = T["ktok"], T["vtok"]
    Sc.op("act", lambda e: e.activation(E[:], fT, AF.Exp, scale=-1.0), reads=[(hbn, 1)], writes=["E"])
    Sc.op("dve", lambda e: e.tensor_scalar(B2[:], E[:], 1.0, None, ALU.add), reads=["E"], writes=["B2"])
    Sc.op("dve", lambda e: e.reciprocal(B2[:], B2[:]), reads=["B2"], writes=["B2"])
    Sc.op("dve", lambda e: e.scalar_tensor_tensor(KT[:], E[:], omlc, B2[:], ALU.mult, ALU.mult),
          reads=["E", "B2", "hconst"], writes=["KT"])
    Sc.op("dve", lambda e: e.tensor_scalar(B2[:], B2[:], omlc, lbc, ALU.mult, ALU.add),
          reads=["B2", "hconst"], writes=["B2"])
    Sc.op("act", lambda e: e.activation(B2[:], B2[:], AF.Ln), reads=["B2"], writes=["B2"])
    Sc.op("dve", lambda e: e.tensor_tensor_scan(CUM[:], T["rmask"], B2[:], 0.0, ALU.mult, ALU.add),
          reads=["B2", "consts2"], writes=["CUM"])
    Sc.op("act", lambda e: e.activation(E[:], CUM[:], AF.Exp), reads=["CUM"], writes=["E"])
    Sc.op("dve", lambda e: e.tensor_tensor(QT[:], qT, E[:], ALU.mult), reads=["E", (hbn, 0)], writes=["QT"])
    Sc.op("act", lambda e: e.activation(B2[:], CUM[:], AF.Exp, scale=-1.0), reads=["CUM"], writes=["B2"])
    Sc.op("dve", lambda e: e.tensor_tensor(KTt[:], KT[:], B2[:], ALU.mult), reads=["KT", "B2"], writes=["KTt"])
    Sc.op("act", lambda e: e.activation(EL[:], CUM[:, 127:4096:128], AF.Exp), reads=["CUM"], writes=["EL"])
    make_vtok(nc, Sc, PS, cst, hb, hbn, ktok, "ktok", 1, srcT=KTt[:], srckey="KTt")
    make_vtok(nc, Sc, PS, cst, hb, hbn, vtok, "vtok", 1)
    state = None
    for c in range(32):
        tsl = slice(c * 128, (c + 1) * 128)
        at_ps, at_psn = PS.next()
        Sc.op("pe", lambda e: e.matmul(at_ps[:, 0:128], KTt[:, tsl], QT[:, tsl], start=True, stop=True),
              reads=["KTt", "QT"], writes=[at_psn])
        am, amn = R["am"].next()
        Sc.op("dve", lambda e: e.tensor_tensor(am[:], at_ps[:, 0:128], T["mask01"], ALU.mult),
              reads=[at_psn, "consts2"], writes=[amn])
        if c % 4 == 0:
            o_ps, o_psn = PSO.next()
        oc = slice((c % 4) * 128, (c % 4 + 1) * 128)
        if state is not None:
            st_t, st_n = state
            Sc.op("pe", lambda e: e.matmul(o_ps[:, oc], st_t[:], QT[:, tsl], start=True, stop=False),
                  reads=[st_n, "QT"], writes=[o_psn])
        Sc.op("pe", lambda e: e.matmul(o_ps[:, oc], vtok[:, c, :], am[:], start=(state is None), stop=True),
              reads=[("vtok", c // 4), amn], writes=[o_psn])
        if c < 31:
            p2, p2n = PS.next()
            Sc.op("pe", lambda e: e.matmul(p2[:, 0:128], ktok[:, c, :], vtok[:, c, :], start=True, stop=True),
                  reads=[("ktok", c // 4), ("vtok", c // 4)], writes=[p2n])
            ns, nsn = R["state"].next()
            if state is None:
                Sc.op("act", lambda e: e.mul(ns[:], p2[:, 0:128], EL[:, c:c + 1]), reads=[p2n, "EL"], writes=[nsn])
            else:
                pe_, pen = R["pse"].next()
                Sc.op("act", lambda e: e.mul(pe_[:], p2[:, 0:128], EL[:, c:c + 1]), reads=[p2n, "EL"], writes=[pen])
                Sc.op("dve", lambda e: e.scalar_tensor_tensor(ns[:], st_t[:], EL[:, c:c + 1], pe_[:],
                                                              ALU.mult, ALU.add),
                      reads=[st_n, "EL", pen], writes=[nsn])
            state = (ns, nsn)
        if c % 4 == 3:
            cs = slice((c // 4) * 512, (c // 4 + 1) * 512)
            sq, sqn = R["sq"].next()
            Sc.op("act", lambda e: e.activation(sq[:], o_ps[:], AF.Square), reads=[o_psn], writes=[sqn])
            ss_ps, ss_psn = PS.next()
            Sc.op("pe", lambda e: e.matmul(ss_ps[:], cst["ones"], sq[:], start=True, stop=True),
                  reads=[sqn, "consts"], writes=[ss_psn])
            rs, rsn = R["rs"].next()
            Sc.op("act", lambda e: e.activation(rs[:], ss_ps[:], AF.Ln, scale=1.0 / 128.0, bias=EPS),
                  reads=[ss_psn], writes=[rsn])
            Sc.op("act", lambda e: e.activation(rs[:], rs[:], AF.Exp, scale=-0.5), reads=[rsn], writes=[rsn])
            Sc.op("dve", lambda e: e.tensor_tensor(rs[:], o_ps[:], rs[:], ALU.mult), reads=[o_psn, rsn], writes=[rsn])
            ot, otn = R["o"].next()
            Sc.op("dve", lambda e: e.scalar_tensor_tensor(ot[:], rs[:], ogc, gT[:, cs], ALU.mult, ALU.mult),
                  reads=[rsn, "hconst", "ogt", (hbn, 3)], writes=[otn])
            Sc.dma("sp", out_dram[:, cs], ot[:], reads=[otn], is_output=is_output)


def sb_head_v2(nc, Sc, PS, PSO, cst, R, hb, hbn, vtok, vtokn, out_dram, is_output=True, fill=None):
    qT, kT, gT = hb[:, 0, :], hb[:, 1, :], hb[:, 3, :]
    blocks = []
    for qt in range(NTT):
        nkb = 4 * (qt + 1)
        for idx, kb in enumerate(range(nkb - 1, -1, -1)):
            blocks.append({"qt": qt, "kb": kb, "first": idx == 0, "last": kb == 0})
    qstate = {}

    def stage_z(b):
        qt, kb = b["qt"], b["kb"]
        m = kb - 4 * qt
        qs = qT[:, qt * 512:(qt + 1) * 512]
        ks = kT[:, kb * 128:(kb + 1) * 128]
        z_ps, z_psn = PS.next()
        Sc.op("pe", lambda e: e.matmul(z_ps[:], ks, qs, start=True, stop=False, skip_group_check=True),
              reads=[(hbn, 0), (hbn, 1)], writes=[z_psn])
        if m >= 0:
            Sc.op("pe", lambda e: e.matmul(z_ps[:], cst["ident"], cst["sbmask"][m], start=False, stop=False,
                                           skip_group_check=True),
                  reads=["consts"], writes=[z_psn])
        b["z"] = (z_ps, z_psn)
        ee, een = R["e"].next()
        Sc.op("act", lambda e: e.activation(ee[:], z_ps[:], AF.Exp), reads=[z_psn], writes=[een])
        sp, spn = R["sp"].next()
        Sc.op("act", lambda e: e.activation(sp[:], ee[:], AF.Ln, bias=1.0), reads=[een], writes=[spn])
        b["sp"] = (sp, spn)

    def stage_arg(b):
        qt, kb = b["qt"], b["kb"]
        m = kb - 4 * qt
        qs = qT[:, qt * 512:(qt + 1) * 512]
        ks = kT[:, kb * 128:(kb + 1) * 128]
        sp, spn = b["sp"]
        if b["first"]:
            qstate[qt] = None
        carry = qstate[qt]
        a_ps, a_psn = b["z"]
        Sc.op("pe", lambda e: e.matmul(a_ps[:], cst["negtri"], sp[:], start=False, stop=(carry is None),
                                       skip_group_check=True),
              reads=["consts", spn], writes=[a_psn])
        if carry is not None:
            Sc.op("pe", lambda e: e.matmul(a_ps[:], cst["negones"], carry[0][:], start=False, stop=True,
                                           skip_group_check=True),
                  reads=["consts", carry[1]], writes=[a_psn])
        at, atn = R["a"].next()
        Sc.op("act", lambda e: e.activation(at[:], a_ps[:], AF.Exp), reads=[a_psn], writes=[atn])
        b["at"] = (at, atn)
        if kb > 0:
            if carry is None:
                qstate[qt] = (sp, spn)
            else:
                nc_, ncn = R["carry"].next()
                Sc.op("dve", lambda e: e.tensor_tensor(nc_[:], carry[0][:], sp[:], ALU.add),
                      reads=[carry[1], spn], writes=[ncn])
                qstate[qt] = (nc_, ncn)

    ops = {}

    def stage_av(b):
        qt, kb = b["qt"], b["kb"]
        at, atn = b["at"]
        if b["first"]:
            ops[qt] = PSO.next()
        o_ps, o_psn = ops[qt]
        Sc.op("pe", lambda e: e.matmul(o_ps[:], vtok[:, kb, :], at[:], start=b["first"], stop=b["last"]),
              reads=[(vtokn, kb // 4), atn], writes=[o_psn])
        if b["last"]:
            ot, otn = R["o"].next()
            Sc.op("dve", lambda e: e.tensor_tensor(ot[:], o_ps[:], gT[:, qt * 512:(qt + 1) * 512], ALU.mult),
                  reads=[o_psn, (hbn, 3)], writes=[otn])
            Sc.dma("sp", out_dram[:, qt * 512:(qt + 1) * 512], ot[:], reads=[otn], is_output=is_output)

    n = len(blocks)
    for k in range(n + 2):
        if k < n:
            stage_z(blocks[k])
        if 0 <= k - 1 < n:
            stage_arg(blocks[k - 1])
        if 0 <= k - 2 < n:
            stage_av(blocks[k - 2])
        if fill:
            fill(1)


def hgrn_head_v2(nc, Sc, PS, PSO, cst, R, T, hb, hbn, lbc, omlc, ogc, out_dram, is_output=True, fill=None):
    qT, fT, vT, gT = hb[:, 0, :], hb[:, 1, :], hb[:, 2, :], hb[:, 3, :]
    E, B2, CUM, KT, QT, EL = T["E"], T["B2"], T["CUM"], T["KT"], T["QT"], T["EL"]
    ktok, vtok, AM, ST = T["ktok"], T["vtok"], T["AM"], T["ST"]
    f_ = fill if fill else (lambda n: None)
    H = S // 2
    for hf in range(2):
        cs = slice(hf * H, (hf + 1) * H)
        Sc.op("act", lambda e: e.activation(E[:], fT[:, cs], AF.Exp, scale=-1.0), reads=[(hbn, 1)], writes=["E"])
        f_(5)
        Sc.op("dve", lambda e: e.tensor_scalar(B2[:], E[:], 1.0, None, ALU.add), reads=["E"], writes=["B2"])
        Sc.op("dve", lambda e: e.reciprocal(B2[:], B2[:]), reads=["B2"], writes=["B2"])
        f_(5)
        Sc.op("dve", lambda e: e.scalar_tensor_tensor(KT[:, cs], E[:], omlc, B2[:], ALU.mult, ALU.mult),
              reads=["E", "B2", "hconst"], writes=[("KT", hf)])
        Sc.op("dve", lambda e: e.tensor_scalar(B2[:], B2[:], omlc, lbc, ALU.mult, ALU.add),
              reads=["B2", "hconst"], writes=["B2"])
        f_(5)
        Sc.op("act", lambda e: e.activation(B2[:], B2[:], AF.Ln), reads=["B2"], writes=["B2"])
        Sc.op("dve", lambda e: e.tensor_tensor_scan(CUM[:], T["rmask"][:, cs], B2[:], 0.0, ALU.mult, ALU.add),
              reads=["B2", "consts2"], writes=["CUM"])
        f_(5)
        Sc.op("act", lambda e: e.activation(E[:], CUM[:], AF.Exp), reads=["CUM"], writes=["E"])
        Sc.op("dve", lambda e: e.tensor_tensor(QT[:, cs], qT[:, cs], E[:], ALU.mult),
              reads=["E", (hbn, 0)], writes=[("QT", hf)])
        f_(5)
        Sc.op("act", lambda e: e.activation(B2[:], CUM[:], AF.Exp, scale=-1.0), reads=["CUM"], writes=["B2"])
        Sc.op("dve", lambda e: e.tensor_tensor(KT[:, cs], KT[:, cs], B2[:], ALU.mult),
              reads=[("KT", hf), "B2"], writes=[("KT", hf)])
        Sc.op("act", lambda e: e.activation(EL[:, hf * 16:(hf + 1) * 16], CUM[:, 127:H:128], AF.Exp),
              reads=["CUM"], writes=[("EL", hf)])
        f_(5)
    make_vtok(nc, Sc, PS, cst, hb, hbn, ktok, "ktok", 1, srcT=KT[:], srckey="KT", fill=fill)
    make_vtok(nc, Sc, PS, cst, hb, hbn, vtok, "vtok", 1, fill=fill)
    for g in range(8):
        at_ps, at_psn = PS.next()
        for j in range(4):
            c = g * 4 + j
            tsl = slice(c * 128, (c + 1) * 128)
            Sc.op("pe", lambda e: e.matmul(at_ps[:, j * 128:(j + 1) * 128], KT[:, tsl], QT[:, tsl],
                                           start=True, stop=True), reads=["KT", "QT"], writes=[at_psn])
        Sc.op("dve", lambda e: e.tensor_tensor(AM[:, g * 4:(g + 1) * 4, :],
                                               at_ps[:].rearrange("p (j t) -> p j t", j=4),
                                               T["mask4"].rearrange("p (j t) -> p j t", j=4), ALU.mult),
              reads=[at_psn, "consts2"], writes=[("AM", g)])
        p2, p2n = PS.next()
        for j in range(4):
            c = g * 4 + j
            if c == 31:
                continue
            Sc.op("pe", lambda e: e.matmul(p2[:, j * 128:(j + 1) * 128], ktok[:, c, :], vtok[:, c, :],
                                           start=True, stop=True),
                  reads=[("ktok", c // 4), ("vtok", c // 4)], writes=[p2n])
        for j in range(4):
            c = g * 4 + j
            if c == 31:
                continue
            Sc.op("act", lambda e: e.mul(ST[:, c + 1, :], p2[:, j * 128:(j + 1) * 128], EL[:, c:c + 1]),
                  reads=[p2n, "EL"], writes=[("ST", c + 1)])
        f_(3)
    for c in range(1, 31):
        Sc.op("dve", lambda e: e.scalar_tensor_tensor(ST[:, c + 1, :], ST[:, c, :], EL[:, c:c + 1], ST[:, c + 1, :],
                                                      ALU.mult, ALU.add),
              reads=[("ST", c), "EL", ("ST", c + 1)], writes=[("ST", c + 1)])
        f_(1)
    for g in range(8):
        o_ps, o_psn = PSO.next()
        for j in range(4):
            c = g * 4 + j
            tsl = slice(c * 128, (c + 1) * 128)
            oc = slice(j * 128, (j + 1) * 128)
            if c > 0:
                Sc.op("pe", lambda e: e.matmul(o_ps[:, oc], ST[:, c, :], QT[:, tsl], start=True, stop=False),
                      reads=[("ST", c), "QT"], writes=[o_psn])
            Sc.op("pe", lambda e: e.matmul(o_ps[:, oc], vtok[:, c, :], AM[:, c, :], start=(c == 0), stop=True),
                  reads=[("vtok", c // 4), ("AM", g)], writes=[o_psn])
        cs = slice(g * 512, (g + 1) * 512)
        sq, sqn = R["sq"].next()
        Sc.op("act", lambda e: e.activation(sq[:], o_ps[:], AF.Square), reads=[o_psn], writes=[sqn])
        ss_ps, ss_psn = PS.next()
        Sc.op("pe", lambda e: e.matmul(ss_ps[:], cst["ones"], sq[:], start=True, stop=True),
              reads=[sqn, "consts"], writes=[ss_psn])
        rs, rsn = R["rs"].next()
        Sc.op("act", lambda e: e.activation(rs[:], ss_ps[:], AF.Ln, scale=1.0 / 128.0, bias=EPS),
              reads=[ss_psn], writes=[rsn])
        Sc.op("act", lambda e: e.activation(rs[:], rs[:], AF.Exp, scale=-0.5), reads=[rsn], writes=[rsn])
        Sc.op("dve", lambda e: e.tensor_tensor(rs[:], o_ps[:], rs[:], ALU.mult), reads=[o_psn, rsn], writes=[rsn])
        ot, otn = R["o"].next()
        Sc.op("dve", lambda e: e.scalar_tensor_tensor(ot[:], rs[:], ogc, gT[:, cs], ALU.mult, ALU.mult),
              reads=[rsn, "hconst", "ogt", (hbn, 3)], writes=[otn])
        Sc.dma("sp", out_dram[:, cs], ot[:], reads=[otn], is_output=is_output)
        f_(2)


def dil_head_v2(nc, Sc, PS, cst, R, hb, hbn, vtoks, vtokns, UZ, out_dram, is_output=True, fill=None):
    qT, kT, gT = hb[:, 0, :], hb[:, 1, :], hb[:, 3, :]
    sc = math.sqrt(128.0)
    for gi, r in enumerate((1, 4, 16)):
        make_vtok(nc, Sc, PS, cst, hb, hbn, vtoks[gi], vtokns[gi], r)
    blocks = []
    for gi, r in enumerate((1, 4, 16)):
        nb = 32 // r
        for p in range(r):
            for n in range(nb):
                blocks.append({"gi": gi, "r": r, "p": p, "n": n, "blk": p * nb + n})

    def stage_s(b):
        r, p, n = b["r"], b["p"], b["n"]
        cq = dil_cols(p, n, r)
        w = 256 if n >= 1 else 128
        s_ps, s_psn = PS.next()
        Sc.op("pe", lambda e: e.matmul(s_ps[:, 0:w], cst["ident"], cst["dmask"][:, 0:w], start=True, stop=False,
                                       skip_group_check=True),
              reads=["consts"], writes=[s_psn])
        Sc.op("pe", lambda e: e.matmul(s_ps[:, 0:128], kT[:, cq], qT[:, cq], start=False, stop=(n == 0),
                                       skip_group_check=True),
              reads=[(hbn, 0), (hbn, 1)], writes=[s_psn])
        if n >= 1:
            ck = dil_cols(p, n - 1, r)
            Sc.op("pe", lambda e: e.matmul(s_ps[:, 128:256], kT[:, ck], qT[:, cq], start=False, stop=True,
                                           skip_group_check=True),
                  reads=[(hbn, 0), (hbn, 1)], writes=[s_psn])
        pt, ptn = R["p"].next()
        Sc.op("act", lambda e: e.activation(pt[:, 0:w], s_ps[:, 0:w], AF.Exp, scale=sc),
              reads=[s_psn], writes=[ptn])
        b["pt"] = (pt, ptn)

    def stage_uz(b):
        gi, r, p, n, blk = b["gi"], b["r"], b["p"], b["n"], b["blk"]
        vt, vtn = vtoks[gi], vtokns[gi]
        pt, ptn = b["pt"]
        cq = dil_cols(p, n, r)
        uz_ps, uz_psn = PS.next()
        for half in range(2):
            o = uz_ps[:, half * 128:(half + 1) * 128]
            l0 = vt[:, blk, :] if half == 0 else cst["ones"]
            Sc.op("pe", lambda e: e.matmul(o, l0, pt[:, 0:128], start=True, stop=(n == 0)),
                  reads=[(vtn, blk // 4), ptn, "consts"], writes=[uz_psn])
            if n >= 1:
                l1 = vt[:, blk - 1, :] if half == 0 else cst["ones"]
                Sc.op("pe", lambda e: e.matmul(o, l1, pt[:, 128:256], start=False, stop=True),
                      reads=[(vtn, (blk - 1) // 4), ptn, "consts"], writes=[uz_psn])
        acc = UZ[:, :, cq]
        src = uz_ps[:, 0:256].rearrange("p (a t) -> p a t", a=2)
        if gi == 0:
            wr = [("uz0", n)]
            if n % 2 == 0:
                Sc.op("dve", lambda e: e.tensor_copy(acc, src), reads=[uz_psn], writes=wr)
            else:
                Sc.op("act", lambda e: e.copy(acc, src), reads=[uz_psn], writes=wr)
        else:
            if gi == 1:
                rd = [("uz0", 4 * n + i) for i in range(4)]
                wr = [("uz1", (n, p))]
            else:
                rd = [("uz1", (c, p % 4)) for c in range(4 * n, 4 * n + 4)]
                wr = [("uz2", (n, p))]
            Sc.op("dve", lambda e: e.tensor_tensor(acc, acc, src, ALU.add), reads=[uz_psn] + rd, writes=wr)

    nb_ = len(blocks)
    for k in range(nb_ + 1):
        if k < nb_:
            stage_s(blocks[k])
        if k >= 1:
            stage_uz(blocks[k - 1])
        if fill:
            fill(2 if k % 2 == 0 else 1)
    for ch in range(8):
        cs = slice(ch * 512, (ch + 1) * 512)
        rz, rzn = R["rz"].next()
        Sc.op("dve", lambda e: e.reciprocal(rz[:], UZ[:, 1, cs]), reads=["uz0", "uz1", "uz2"], writes=[rzn])
        Sc.op("pool", lambda e: e.tensor_tensor(rz[:], rz[:], UZ[:, 0, cs], ALU.mult),
              reads=[rzn, "uz0", "uz1", "uz2"], writes=[rzn])
        ot, otn = R["o"].next()
        Sc.op("pool", lambda e: e.tensor_tensor(ot[:], rz[:], gT[:, cs], ALU.mult),
              reads=[rzn, (hbn, 3)], writes=[otn])
        Sc.dma("sp", out_dram[:, cs], ot[:], reads=[otn], is_output=is_output)


def phase1_stream(nc, Sc, PS, x, gain_bc, hd_d, cst):
    with ExitStack() as e1:
        gbc = e1.enter_context(_sbt(nc, "gbc", [128, D], F32))
        ss = e1.enter_context(_sbt(nc, "ss", [128, 32], F32))
        rstd = e1.enter_context(_sbt(nc, "rstd", [128, 32], F32))
        junk = e1.enter_context(_sbt(nc, "junk", [128, D], BF16))
        xt_r = Rot(nc, e1, "xt", [128, D], F32, 4)
        xs_r = Rot(nc, e1, "xs", [128, D], BF16, 4)
        hs_r = Rot(nc, e1, "hs1_", [128, NKC, 512], BF16, 2)
        Sc.dma("sp", gbc[:], gain_bc, writes=["gbc"])
        Sc.op("dve", lambda e: e.memset(ss[:], 0.0), writes=["ss"])
        for t5 in range(8):
            hs, hsn = hs_r.next()
            for ti in range(4):
                tt = t5 * 4 + ti
                xt, xtn = xt_r.next()
                Sc.dma("pool", xt[:], x[tt * 128:(tt + 1) * 128, :], writes=[xtn])
                Sc.op("act", lambda e: e.activation(junk[:], xt[:], AF.Square, accum_out=ss[:, tt:tt + 1]),
                      reads=[xtn], writes=[("ss", tt)])
                Sc.op("act", lambda e: e.activation(rstd[:, tt:tt + 1], ss[:, tt:tt + 1], AF.Ln,
                                                    scale=1.0 / D, bias=EPS),
                      reads=[("ss", tt)], writes=[("rstd", tt)])
                Sc.op("act", lambda e: e.activation(rstd[:, tt:tt + 1], rstd[:, tt:tt + 1], AF.Exp, scale=-0.5),
                      reads=[("rstd", tt)], writes=[("rstd", tt)])
                xs, xsn = xs_r.next()
                Sc.op("dve", lambda e: e.scalar_tensor_tensor(xs[:], xt[:], rstd[:, tt:tt + 1], gbc[:],
                                                              ALU.mult, ALU.mult),
                      reads=[xtn, ("rstd", tt), "gbc"], writes=[xsn])
                for half in range(2):
                    ps, psn = PS.next()
                    psb = ps[:].bitcast(BF16)
                    for j in range(8):
                        kc = half * 8 + j
                        Sc.op("pe", lambda e: e.transpose(psb[:, j * 128:(j + 1) * 128],
                                                          xs[:, kc * 128:(kc + 1) * 128], cst["ident"]),
                              reads=[xsn, "consts"], writes=[psn])
                    src = psb.rearrange("p (j t) -> p j t", j=8)
                    dst = hs[:, half * 8:(half + 1) * 8, ti * 128:(ti + 1) * 128]
                    if half == 0:
                        Sc.op("act", lambda e: e.copy(dst, src), reads=[psn], writes=[(hsn, (ti, half))])
                    else:
                        Sc.op("dve", lambda e: e.tensor_copy(dst, src), reads=[psn], writes=[(hsn, (ti, half))])
            Sc.dma("sp", hd_d[t5], hs[:], reads=[hsn])
        Sc.barrier()


class ProjStream:
    def __init__(self, nc, es, Sc, PSP, cst, heads, kinds, W, hd_d, scr, nhs=3):
        self.nc, self.Sc, self.PSP, self.cst = nc, Sc, PSP, cst
        self.heads, self.kinds, self.W, self.hd_d, self.scr = heads, kinds, W, hd_d, scr
        self.wt_r = Rot(nc, es, "wt", [128, NKC, 512], BF16, 2)
        self.hs_r = Rot(nc, es, "hs", [128, NKC, 512], BF16, nhs)
        self.nhs = nhs
        self.st_r = Rot(nc, es, "st", [128, 4, 512], BF16, 2)
        if any(k == "dil" for k in kinds.values()):
            self.sq_r = Rot(nc, es, "sq", [128, 512], BF16, 2)
            self.rs_r = Rot(nc, es, "rs", [128, 512], F32, 2)
        self.eg_r = Rot(nc, es, "eg", [128, 512], F32, 2)
        self.pref = "dve"
        self.done = set()
        self.wts = {}
        self.hss = {}
        self.gen = self._run()
        self.exhausted = False

    def step(self, n=1):
        for _ in range(n):
            if self.exhausted:
                return
            try:
                next(self.gen)
            except StopIteration:
                self.exhausted = True

    def finish_head(self, hd):
        while hd not in self.done and not self.exhausted:
            self.step()

    def _load_w(self, hd):
        if hd in self.wts or hd not in self.kinds:
            return
        wt, wtn = self.wt_r.next()
        for g4 in range(4):
            self.Sc.dma("pool", wt[:, g4 * 4:(g4 + 1) * 4, :], self.W[hd, :, g4 * 4:(g4 + 1) * 4, :],
                        writes=[(wtn, g4)])
        self.wts[hd] = (wt, wtn)

    def _load_hs(self, i):
        if i in self.hss or i >= 8 * len(self.heads):
            return
        hs, hsn = self.hs_r.next()
        tt = i % 8
        for hf in range(2):
            self.Sc.dma("sp", hs[:, hf * 8:(hf + 1) * 8, :], self.hd_d[tt, :, hf * 8:(hf + 1) * 8, :],
                        writes=[(hsn, hf)])
        self.hss[i] = (hs, hsn)

    def _run(self):
        Sc, cst = self.Sc, self.cst
        self._load_w(self.heads[0])
        for j in range(self.nhs - 1):
            self._load_hs(j)
        for hi, hd in enumerate(self.heads):
            kind = self.kinds[hd]
            wt, wtn = self.wts[hd]
            if hi + 1 < len(self.heads):
                self._load_w(self.heads[hi + 1])
            for tt in range(NTT):
                i = hi * 8 + tt
                hs, hsn = self.hss.pop(i)
                self._load_hs(i + self.nhs - 1)
                st, stn = self.st_r.next()
                for c in range(4):
                    ps, psn = self.PSP.next()
                    for kc in range(NKC):
                        Sc.op("pe", lambda e: e.matmul(ps[:], wt[:, kc, c * 128:(c + 1) * 128], hs[:, kc, :],
                                                       start=(kc == 0), stop=(kc == NKC - 1)),
                              reads=[(wtn, kc // 4), (hsn, kc // 8)], writes=[psn])
                        if kc % 4 == 3 and kc != NKC - 1:
                            yield
                    self._evac(kind, c, ps, psn, st, stn)
                    yield
                dst = self.scr[hd, :, :, tt * 512:(tt + 1) * 512].rearrange("c p t -> p c t")
                Sc.dma("sp", dst, st[:], reads=[stn], writes=[("scr", hd)])
            self.done.add(hd)
            yield

    def _evac(self, kind, c, ps, psn, st, stn):
        Sc, cst = self.Sc, self.cst
        act_heavy = self.pref == "act"

        def cp(dst, key):
            if act_heavy:
                Sc.op("act", lambda e: e.copy(dst, ps[:]), reads=[psn], writes=[key])
            else:
                Sc.op("dve", lambda e: e.tensor_copy(dst, ps[:]), reads=[psn], writes=[key])

        if c == 3:
            eg, egn = self.eg_r.next()
            Sc.op("act", lambda e: e.activation(eg[:], ps[:], AF.Exp, scale=-1.0), reads=[psn], writes=[egn])
            if act_heavy:
                Sc.op("act", lambda e: e.activation(eg[:], eg[:], AF.Ln, bias=1.0), reads=[egn], writes=[egn])
                Sc.op("act", lambda e: e.activation(eg[:], eg[:], AF.Exp, scale=-1.0), reads=[egn], writes=[egn])
            else:
                Sc.op("dve", lambda e: e.tensor_scalar(eg[:], eg[:], 1.0, None, ALU.add), reads=[egn], writes=[egn])
                Sc.op("dve", lambda e: e.reciprocal(eg[:], eg[:]), reads=[egn], writes=[egn])
            Sc.op("dve", lambda e: e.tensor_tensor(st[:, 3, :], ps[:], eg[:], ALU.mult),
                  reads=[psn, egn], writes=[(stn, 3)])
        elif c == 2:
            cp(st[:, 2, :], (stn, 2))
        elif kind == "dil":
            gcol = cst["qg"] if c == 0 else cst["kg"]
            sq, sqn = self.sq_r.next()
            Sc.op("act", lambda e: e.activation(sq[:], ps[:], AF.Square), reads=[psn], writes=[sqn])
            p2, p2n = self.PSP.next()
            Sc.op("pe", lambda e: e.matmul(p2[:], cst["ones"], sq[:], start=True, stop=True),
                  reads=[sqn, "consts"], writes=[p2n])
            rs, rsn = self.rs_r.next()
            Sc.op("act", lambda e: e.activation(rs[:], p2[:], AF.Ln, bias=128.0 * EPS), reads=[p2n], writes=[rsn])
            Sc.op("act", lambda e: e.activation(rs[:], rs[:], AF.Exp, scale=-0.5), reads=[rsn], writes=[rsn])
            Sc.op("dve", lambda e: e.scalar_tensor_tensor(st[:, c, :], ps[:], gcol, rs[:], ALU.mult, ALU.mult),
                  reads=[psn, rsn, "gcols"], writes=[(stn, c)])
        elif c == 0:
            if kind == "sb":
                sc = 1.0 / math.sqrt(128.0)
                Sc.op("dve", lambda e: e.tensor_scalar(st[:, 0, :], ps[:], sc, None, ALU.mult),
                      reads=[psn], writes=[(stn, 0)])
            else:
                cp(st[:, 0, :], (stn, 0))
        else:
            cp(st[:, 1, :], (stn, 1))


def load_head_tracked(nc, Sc, hb, hbn, scr, hd):
    for c in range(4):
        Sc.dma("sp", hb[:, c, :], scr[hd, c, :, :], reads=[("scr", hd)], writes=[(hbn, c)])


def mixer_stream(nc, es, Sc, PSA, layer, x, g, W, extra, hd_d, scr, out, nheads, c, is_output=False):
    PS, PSO, PSP = PSA.sub([0, 1, 2]), PSA.sub([3, 4]), PSA.sub([5, 6, 7])
    bank_split = {"sb": ([0, 1, 2, 3], [4, 5], [6, 7]),
                  "dil": ([0, 1, 2, 3], [4], [4, 5, 6, 7]),
                  "hgrn": ([0, 1, 2, 3], [4, 5], [6, 7])}
    heads = list(range(nheads))
    L = "L%d" % layer
    if layer == 0:
        gct = es.enter_context(_sbt(nc, L + "gct", [128, 2], F32))
        Sc.dma("sp", gct[:], extra, writes=["gcols"])
        c["qg"], c["kg"] = gct[:, 0:1], gct[:, 1:2]
        kinds = {hd: ("sb" if hd < nheads // 2 else "dil") for hd in heads}
    else:
        cst2, lbl, ogd = extra
        kinds = {hd: "hgrn" for hd in heads}
        c2 = es.enter_context(_sbt(nc, "c2", [128, 4608], BF16))
        Sc.dma("pool", c2[:], cst2, writes=["consts2"])
        hct = es.enter_context(_sbt(nc, "hct", [128, 2, nheads], F32))
        lbt = es.enter_context(_sbt(nc, "lbt", [128, nheads], F32))
        omlt = es.enter_context(_sbt(nc, "omlt", [128, nheads], F32))
        ogt = es.enter_context(_sbt(nc, "ogt", [128, 1], F32))
        Sc.dma("sp", hct[:], lbl, writes=["hct"])
        Sc.dma("sp", ogt[:], ogd, writes=["ogt"])
        Sc.op("dve", lambda e: e.tensor_tensor(omlt[:], hct[:, 0, :], hct[:, 1, :], ALU.subtract),
              reads=["hct"], writes=["omlt"])
        Sc.op("act", lambda e: e.activation(omlt[:], omlt[:], AF.Exp), reads=["omlt"], writes=["omlt"])
        Sc.op("dve", lambda e: e.tensor_scalar(lbt[:], omlt[:], 1.0, None, ALU.add), reads=["omlt"], writes=["lbt"])
        Sc.op("dve", lambda e: e.reciprocal(lbt[:], lbt[:]), reads=["lbt"], writes=["lbt"])
        Sc.op("dve", lambda e: e.tensor_tensor(omlt[:], omlt[:], lbt[:], ALU.mult),
              reads=["omlt", "lbt"], writes=["hconst"])
    phase1_stream(nc, Sc, PSA, x, g, hd_d, c)
    with ExitStack() as es3:
        P = ProjStream(nc, es3, Sc, PSP, c, heads, kinds, W, hd_d, scr, nhs=(3 if layer == 0 else 2))
        fill = P.step
        if layer == 0:
            groups = [[h for h in heads if kinds[h] == "sb"], [h for h in heads if kinds[h] == "dil"]]
        else:
            groups = [heads]
        for grp in groups:
            kind = kinds[grp[0]]
            P.pref = "act" if kind == "hgrn" else "dve"
            bs = bank_split[kind]
            PS, PSO = PSA.sub(bs[0]), PSA.sub(bs[1])
            P.PSP = PSA.sub(bs[2])
            with ExitStack() as es4:
                R = {"o": Rot(nc, es4, "o", [128, 512], BF16, 2)}
                nhb = 2 if kind == "sb" else 1
                hbs = [es4.enter_context(_sbt(nc, "hb%d" % i, [128, 4, S], BF16)) for i in range(nhb)]
                hbn = ["hb%d" % i for i in range(nhb)]
                if kind == "sb":
                    vtok0 = es4.enter_context(_sbt(nc, "vtok0", [128, 32, 128], BF16))
                    R.update({"e": Rot(nc, es4, "e", [128, 512], F32, 2), "sp": Rot(nc, es4, "sp", [128, 512], BF16, 5),
                              "a": Rot(nc, es4, "a", [128, 512], BF16, 4),
                              "carry": Rot(nc, es4, "carry", [128, 512], BF16, 4)})
                elif kind == "dil":
                    vtoks = [es4.enter_context(_sbt(nc, "vtok%d" % i, [128, 32, 128], BF16)) for i in range(3)]
                    vtokn = ["vtok0", "vtok1", "vtok2"]
                    UZ = es4.enter_context(_sbt(nc, "UZ", [128, 2, S], F32))
                    R.update({"p": Rot(nc, es4, "p", [128, 256], BF16, 4), "rz": Rot(nc, es4, "rz", [128, 512], F32, 2)})
                else:
                    T = {}
                    for nm in ("E", "B2", "CUM"):
                        T[nm] = es4.enter_context(_sbt(nc, nm, [128, S // 2], F32))
                    for nm in ("KT", "QT"):
                        T[nm] = es4.enter_context(_sbt(nc, nm, [128, S], BF16))
                    for nm in ("AM", "ST"):
                        T[nm] = es4.enter_context(_sbt(nc, nm, [128, 32, 128], BF16))
                    T["EL"] = es4.enter_context(_sbt(nc, "EL", [128, 32], F32))
                    T["ktok"] = es4.enter_context(_sbt(nc, "ktok", [128, 32, 128], BF16))
                    T["vtok"] = es4.enter_context(_sbt(nc, "vtok", [128, 32, 128], BF16))
                    T["rmask"], T["mask4"] = c2[:, 0:4096], c2[:, 4096:4608]
                    R.update({"sq": Rot(nc, es4, "sq3", [128, 512], BF16, 2), "rs": Rot(nc, es4, "rs3", [128, 512], F32, 2)})
                P.finish_head(grp[0])
                load_head_tracked(nc, Sc, hbs[0], hbn[0], scr, grp[0])
                if kind == "dil":
                    P.step(40)
                for i, hd in enumerate(grp):
                    hb, hn = hbs[i % nhb], hbn[i % nhb]
                    if kind == "sb":
                        make_vtok(nc, Sc, PS, c, hb, hn, vtok0, "vtok0", 1)
                        sb_head_v2(nc, Sc, PS, PSO, c, R, hb, hn, vtok0, "vtok0", out[hd], is_output, fill=fill)
                    elif kind == "dil":
                        dil_head_v2(nc, Sc, PS, c, R, hb, hn, vtoks, vtokn, UZ, out[hd], is_output, fill=fill)
                    else:
                        hgrn_head_v2(nc, Sc, PS, PSO, c, R, T, hb, hn, lbt[:, hd:hd + 1], omlt[:, hd:hd + 1],
                                     ogt[:, 0:1], out[hd], is_output, fill=fill)
                    if i + 1 < len(grp):
                        P.finish_head(grp[i + 1])
                        load_head_tracked(nc, Sc, hbs[(i + 1) % nhb], hbn[(i + 1) % nhb], scr, grp[i + 1])
                        if kind == "dil":
                            P.step(40)
                Sc.barrier()


def build_outproj():
    nc = bass.Bass("TRN2", target_bir_lowering=False)
    mixT = nc.dram_tensor("mixT", [32, 128, 2048], BF16, kind="ExternalInput").ap()
    w = nc.dram_tensor("w", [128, 32, 2048], F32, kind="ExternalInput").ap()
    x = nc.dram_tensor("x", [2048, 2048], F32, kind="ExternalInput").ap()
    y = nc.dram_tensor("y", [2048, 2048], F32, kind="ExternalOutput").ap()
    with ExitStack() as es:
        Sc = Sched(nc, es)
        PS = PsumPool(nc, es)
        outproj_body(nc, es, Sc, PS, mixT, w, x, y, 2048)
        Sc.finish()
    return nc


def outproj_body(nc, es_unused, Sc, PS, mixT, w, x, y, ntok, is_output=True):
    with ExitStack() as es:
        wsb = es.enter_context(_sbt(nc, "wsb", [128, 32, 2048], BF16))
        for g in range(8):
            Sc.dma("pool", wsb[:, g * 4:(g + 1) * 4, :], w[:, g * 4:(g + 1) * 4, :], writes=[("wsb", g)])
        mx_r = Rot(nc, es, "mx", [128, 32, 256], BF16, 2)
        xt_r = Rot(nc, es, "xo", [128, 2048], F32, 2)
        for t5 in range(ntok // 256):
            mx, mxn = mx_r.next()
            for g in range(4):
                src = mixT[g * 8:(g + 1) * 8, :, t5 * 256:(t5 + 1) * 256].rearrange("c p t -> p c t")
                Sc.dma("sp", mx[:, g * 8:(g + 1) * 8, :], src, writes=[(mxn, g)])
            for ti in range(2):
                tt = t5 * 2 + ti
                xt, xtn = xt_r.next()
                Sc.dma("sp", xt[:], x[tt * 128:(tt + 1) * 128, :], writes=[xtn])
                for n in range(4):
                    ps, psn = PS.next()
                    for kc in range(32):
                        Sc.op("pe", lambda e: e.matmul(ps[:], mx[:, kc, ti * 128:(ti + 1) * 128],
                                                       wsb[:, kc, n * 512:(n + 1) * 512],
                                                       start=(kc == 0), stop=(kc == 31)),
                              reads=[(mxn, kc // 8), ("wsb", kc // 4)], writes=[psn])
                    Sc.op("dve", lambda e: e.tensor_tensor(xt[:, n * 512:(n + 1) * 512], ps[:],
                                                           xt[:, n * 512:(n + 1) * 512], ALU.add),
                          reads=[psn, xtn], writes=[(xtn, n)])
                Sc.dma("sp", y[tt * 128:(tt + 1) * 128, :], xt[:], reads=[xtn], is_output=is_output)
        Sc.barrier()


def build_mixer(layer, nheads=16):
    nc = bass.Bass("TRN2", target_bir_lowering=False)
    x = nc.dram_tensor("x", [S, D], F32, kind="ExternalInput").ap()
    g = nc.dram_tensor("g", [128, D], F32, kind="ExternalInput").ap()
    W = nc.dram_tensor("W", [nheads, 128, NKC, 512], F32, kind="ExternalInput").ap()
    cst = nc.dram_tensor("cst", [128, 2816], F32, kind="ExternalInput").ap()
    if layer == 0:
        gc = nc.dram_tensor("gc", [128, 2], F32, kind="ExternalInput").ap()
    else:
        cst2 = nc.dram_tensor("cst2", [128, 4608], F32, kind="ExternalInput").ap()
        lbl = nc.dram_tensor("lbl", [128, 2, nheads], F32, kind="ExternalInput").ap()
        ogd = nc.dram_tensor("og", [128, 1], F32, kind="ExternalInput").ap()
    scr = nc.dram_tensor("scr", [nheads, 4, 128, S], BF16, kind="Internal").ap()
    out = nc.dram_tensor("mixT", [nheads, 128, S], BF16, kind="ExternalOutput").ap()
    with ExitStack() as es:
        Sc = Sched(nc, es)
        PSA = PsumPool(nc, es)
        mixer_body(nc, es, Sc, PSA, layer, x, g, W, cst, gc if layer == 0 else (cst2, lbl, ogd), scr, out, nheads)
        Sc.finish()
    return nc


def mixer_body(nc, es, Sc, PSA, layer, x, g, W, cst, extra, scr, out, nheads=16, c=None, is_output=True):
    PS, PSO = PSA.sub([0, 1, 2, 3, 4, 5]), PSA.sub([6, 7])
    if c is None:
        c = load_consts(nc, es, Sc, cst)
    heads = list(range(nheads))
    L = "L%d" % layer
    if layer == 0:
        gct = es.enter_context(_sbt(nc, L + "gct", [128, 2], F32))
        Sc.dma("sp", gct[:], extra, writes=["gcols"])
        c["qg"], c["kg"] = gct[:, 0:1], gct[:, 1:2]
        kinds = {hd: ("sb" if hd < nheads // 2 else "dil") for hd in heads}
    else:
        cst2, lbl, ogd = extra
        kinds = {hd: "hgrn" for hd in heads}
        c2 = es.enter_context(_sbt(nc, "c2", [128, 4608], BF16))
        Sc.dma("pool", c2[:], cst2, writes=["consts2"])
        hct = es.enter_context(_sbt(nc, "hct", [128, 2, nheads], F32))
        lbt = es.enter_context(_sbt(nc, "lbt", [128, nheads], F32))
        omlt = es.enter_context(_sbt(nc, "omlt", [128, nheads], F32))
        ogt = es.enter_context(_sbt(nc, "ogt", [128, 1], F32))
        Sc.dma("sp", hct[:], lbl, writes=["hct"])
        Sc.dma("sp", ogt[:], ogd, writes=["ogt"])
        Sc.op("dve", lambda e: e.tensor_tensor(omlt[:], hct[:, 0, :], hct[:, 1, :], ALU.subtract),
              reads=["hct"], writes=["omlt"])
        Sc.op("act", lambda e: e.activation(omlt[:], omlt[:], AF.Exp), reads=["omlt"], writes=["omlt"])
        Sc.op("dve", lambda e: e.tensor_scalar(lbt[:], omlt[:], 1.0, None, ALU.add), reads=["omlt"], writes=["lbt"])
        Sc.op("dve", lambda e: e.reciprocal(lbt[:], lbt[:]), reads=["lbt"], writes=["lbt"])
        Sc.op("dve", lambda e: e.tensor_tensor(omlt[:], omlt[:], lbt[:], ALU.mult),
              reads=["omlt", "lbt"], writes=["hconst"])
    phase12(nc, Sc, PSA, x, g, W, scr, heads, kinds, c)
    es = ExitStack()
    es.__enter__()
    hbs = [es.enter_context(_sbt(nc, "hb%d" % i, [128, 4, S], BF16)) for i in range(2)]
    hbn = ["hb0", "hb1"]
    R = {"o": Rot(nc, es, "o", [128, 512], BF16, 2)}
    if layer == 0:
        vtoks = [es.enter_context(_sbt(nc, "vtok%d" % i, [128, 32, 128], BF16)) for i in range(3)]
        vtokn = ["vtok0", "vtok1", "vtok2"]
        UZ = es.enter_context(_sbt(nc, "UZ", [128, 2, S], F32))
        R.update({"e": Rot(nc, es, "e", [128, 512], F32, 3), "sp": Rot(nc, es, "sp", [128, 512], BF16, 5),
                  "a": Rot(nc, es, "a", [128, 512], BF16, 4), "carry": Rot(nc, es, "carry", [128, 512], BF16, 4),
                  "p": Rot(nc, es, "p", [128, 256], BF16, 4), "rz": Rot(nc, es, "rz", [128, 512], F32, 2)})
    else:
        T = {}
        for nm in ("E", "B2", "CUM"):
            T[nm] = es.enter_context(_sbt(nc, nm, [128, S], F32))
        for nm in ("KT", "QT"):
            T[nm] = es.enter_context(_sbt(nc, nm, [128, S], BF16))
        for nm in ("AM", "PSE", "ST"):
            T[nm] = es.enter_context(_sbt(nc, nm, [128, 32, 128], BF16))
        T["EL"] = es.enter_context(_sbt(nc, "EL", [128, 32], F32))
        T["ktok"] = es.enter_context(_sbt(nc, "ktok", [128, 32, 128], BF16))
        T["vtok"] = es.enter_context(_sbt(nc, "vtok", [128, 32, 128], BF16))
        T["rmask"], T["mask4"] = c2[:, 0:4096], c2[:, 4096:4608]
        R.update({"am": Rot(nc, es, "am", [128, 128], BF16, 3), "state": Rot(nc, es, "state", [128, 128], BF16, 3),
                  "pse": Rot(nc, es, "pse", [128, 128], F32, 2), "sq": Rot(nc, es, "sq3", [128, 512], BF16, 2),
                  "rs": Rot(nc, es, "rs3", [128, 512], F32, 2)})
    load_head(nc, Sc, hbs[0], hbn[0], scr, heads[0])
    for i, hd in enumerate(heads):
        hb, hn = hbs[i % 2], hbn[i % 2]
        if i + 1 < len(heads):
            load_head(nc, Sc, hbs[(i + 1) % 2], hbn[(i + 1) % 2], scr, heads[i + 1])
        if kinds[hd] == "sb":
            make_vtok(nc, Sc, PS, c, hb, hn, vtoks[0], vtokn[0], 1)
            sb_head_v2(nc, Sc, PS, PSO, c, R, hb, hn, vtoks[0], vtokn[0], out[hd], is_output)
        elif kinds[hd] == "dil":
            dil_head_v2(nc, Sc, PS, c, R, hb, hn, vtoks, vtokn, UZ, out[hd], is_output)
        else:
            hgrn_head_v2(nc, Sc, PS, PSO, c, R, T, hb, hn, lbt[:, hd:hd + 1], omlt[:, hd:hd + 1], ogt[:, 0:1], out[hd], is_output)
    Sc.barrier()
    es.close()


def build_fused():
    nc = bass.Bass("TRN2", target_bir_lowering=False)
    x = nc.dram_tensor("x", [S, D], F32, kind="ExternalInput").ap()
    g0 = nc.dram_tensor("g0", [128, D], F32, kind="ExternalInput").ap()
    g1 = nc.dram_tensor("g1", [128, D], F32, kind="ExternalInput").ap()
    W0 = nc.dram_tensor("W0", [32, 128, NKC, 512], F32, kind="ExternalInput").ap()
    W1 = nc.dram_tensor("W1", [32, 128, NKC, 512], F32, kind="ExternalInput").ap()
    wo0 = nc.dram_tensor("wo0", [128, 32, 2048], F32, kind="ExternalInput").ap()
    wo1 = nc.dram_tensor("wo1", [128, 32, 2048], F32, kind="ExternalInput").ap()
    cst = nc.dram_tensor("cst", [128, 2816], F32, kind="ExternalInput").ap()
    gc = nc.dram_tensor("gc", [128, 2], F32, kind="ExternalInput").ap()
    cst2 = nc.dram_tensor("cst2", [128, 4608], F32, kind="ExternalInput").ap()
    lbl = nc.dram_tensor("lbl", [128, 2, 32], F32, kind="ExternalInput").ap()
    ogd = nc.dram_tensor("og", [128, 1], F32, kind="ExternalInput").ap()
    scr = nc.dram_tensor("scr", [32, 4, 128, S], BF16, kind="Internal").ap()
    hd_d = nc.dram_tensor("hdn_d", [8, 128, NKC, 512], BF16, kind="Internal").ap()
    mixT = nc.dram_tensor("mixT", [32, 128, S], BF16, kind="Internal").ap()
    x1 = nc.dram_tensor("x1", [S, D], F32, kind="Internal").ap()
    y = nc.dram_tensor("y", [S, D], F32, kind="ExternalOutput").ap()
    with ExitStack() as es:
        Sc = Sched(nc, es)
        PSA = PsumPool(nc, es)
        c = load_consts(nc, es, Sc, cst)
        mixer_stream(nc, es, Sc, PSA, 0, x, g0, W0, gc, hd_d, scr, mixT, 32, c)
        outproj_body(nc, None, Sc, PSA, mixT, wo0, x, x1, S, is_output=False)
        mixer_stream(nc, es, Sc, PSA, 1, x1, g1, W1, (cst2, lbl, ogd), hd_d, scr, mixT, 32, c)
        outproj_body(nc, None, Sc, PSA, mixT, wo1, x1, y, S, is_output=True)
        Sc.finish()
        print("instr counts", Sc.cnt, "dma", Sc.dcnt, "waits", Sc.nwaits)
    return nc


N_CORES = 4
ACTIVE = (0, 1, 2, 3)


def _prep_w(w_in, cols):
    return np.ascontiguousarray(w_in[:, cols].reshape(NKC, 128, 512).transpose(1, 0, 2))


def _cols_l0():
    out = []
    for hd in range(32):
        if hd < 16:
            h, base = hd, (0, 2048, 4096, 12288)
        else:
            h, base = hd - 16, (6144, 8192, 10240, 12288 + 2048)
        out.append(np.concatenate([b + np.arange(h * 128, (h + 1) * 128) for b in base]))
    return out


def _cols_l1():
    return [np.concatenate([b + np.arange(h * 128, (h + 1) * 128) for b in (0, 4096, 8192, 12288)])
            for h in range(32)]


def _make_cst2():
    c2 = np.zeros((128, 4608), np.float32)
    c2[:, :4096] = (np.arange(4096) % 128 != 0)[None, :]
    c2[:, 4096:] = np.tile(np.arange(128)[:, None] <= np.arange(128)[None, :], (1, 4))
    return c2


def kernel(x, norm_even, w_in_even, q_norm_even, k_norm_even, w_out_even,
           norm_odd, w_in_odd, lb_logits, o_norm_odd, w_out_odd):
    f32 = np.float32
    x = np.asarray(x, f32)
    w0 = np.asarray(w_in_even[0], f32)
    w1 = np.asarray(w_in_odd[0], f32)
    shared = {
        "g0": np.ascontiguousarray(np.broadcast_to(np.asarray(norm_even[0], f32), (128, D))),
        "g1": np.ascontiguousarray(np.broadcast_to(np.asarray(norm_odd[0], f32), (128, D))),
        "W0": np.stack([_prep_w(w0, cl) for cl in _cols_l0()]),
        "W1": np.stack([_prep_w(w1, cl) for cl in _cols_l1()]),
        "wo0": np.ascontiguousarray(np.asarray(w_out_even[0], f32).reshape(32, 128, 2048).transpose(1, 0, 2)),
        "wo1": np.ascontiguousarray(np.asarray(w_out_odd[0], f32).reshape(32, 128, 2048).transpose(1, 0, 2)),
        "cst": make_consts_np(),
        "gc": np.ascontiguousarray(np.stack([np.asarray(q_norm_even[0], f32), np.asarray(k_norm_even[0], f32)], 1)),
        "cst2": _make_cst2(),
        "lbl": np.ascontiguousarray(np.asarray(lb_logits, f32).reshape(2, 32, 128).transpose(2, 0, 1)),
        "og": np.ascontiguousarray(np.asarray(o_norm_odd[0], f32).reshape(128, 1)),
    }
    big = ("W0", "W1", "wo0", "wo1")
    idle = dict(shared, x=np.zeros((S, D), f32))
    for k in big:
        idle[k] = np.zeros_like(shared[k])
    in_maps = []
    for core in range(N_CORES):
        if core in ACTIVE:
            in_maps.append(dict(shared, x=np.ascontiguousarray(x[ACTIVE.index(core)])))
        else:
            in_maps.append(idle)
    nc = build_fused()
    res = run_bass_kernel_spmd(nc, in_maps, core_ids=list(range(N_CORES)))
    return np.stack([res.results[core]["y"] for core in ACTIVE])
```

```python
import math
import numpy as np
from contextlib import ExitStack
import concourse.bass as bass
import concourse.mybir as mybir
from concourse.bass_utils import run_bass_kernel_spmd


F32 = mybir.dt.float32
BF16 = mybir.dt.bfloat16
AF = mybir.ActivationFunctionType
ALU = mybir.AluOpType
AX = mybir.AxisListType

NDMA_SLOTS = 8
_UNIQ = [0]


def _sbt(nc, name, shape, dtype):
    _UNIQ[0] += 1
    return nc.sbuf_tensor("%s_u%d" % (name, _UNIQ[0]), shape, dtype)


class Sched:
    def __init__(self, nc, es):
        self.nc = nc
        self.eng = {"pe": nc.tensor, "act": nc.scalar, "dve": nc.vector,
                    "pool": nc.gpsimd, "sp": nc.sync}
        self.sem = {e: es.enter_context(nc.semaphore("s_" + e)) for e in self.eng}
        self.cnt = {e: 0 for e in self.eng}
        self.dq = {"sp": "sp", "pool": "pool", "act": "act"}
        self.dsem = {q: [es.enter_context(nc.semaphore("d_%s%d" % (q, i)))
                         for i in range(NDMA_SLOTS)] for q in self.dq}
        self.dcnt = {q: 0 for q in self.dq}
        self.waited = {e: {} for e in self.eng}
        self.state = {}
        self.out_tokens = []
        self.nwaits = 0

    def _need(self, e, tok):
        if tok is None:
            return
        sem, val, key = tok
        if e == "pe" and key == "pe":
            return
        w = self.waited[e]
        if w.get(key, 0) >= val:
            return
        w[key] = val
        self.eng[e].wait_ge(sem, val)
        self.nwaits += 1

    def _st(self, root):
        s = self.state.get(root)
        if s is None:
            s = {"w": None, "r": [], "subs": {}}
            self.state[root] = s
        return s

    @staticmethod
    def _split(res):
        if isinstance(res, tuple):
            return res[0], res[1]
        return res, None

    def _deps(self, e, reads, writes):
        for res in reads:
            root, sub = self._split(res)
            s = self._st(root)
            self._need(e, s["w"])
            if sub is None:
                for ss in s["subs"].values():
                    self._need(e, ss["w"])
            else:
                ss = s["subs"].get(sub)
                if ss:
                    self._need(e, ss["w"])
        for res in writes:
            root, sub = self._split(res)
            s = self._st(root)
            self._need(e, s["w"])
            for t in s["r"]:
                self._need(e, t)
            if sub is None:
                for ss in s["subs"].values():
                    self._need(e, ss["w"])
                    for t in ss["r"]:
                        self._need(e, t)
            else:
                ss = s["subs"].get(sub)
                if ss:
                    self._need(e, ss["w"])
                    for t in ss["r"]:
                        self._need(e, t)

    def _record(self, tok, reads, writes):
        for res in reads:
            root, sub = self._split(res)
            s = self._st(root)
            if sub is None:
                s["r"] = [t for t in s["r"] if t[2] != tok[2]] + [tok]
            else:
                ss = s["subs"].setdefault(sub, {"w": None, "r": []})
                ss["r"] = [t for t in ss["r"] if t[2] != tok[2]] + [tok]
        for res in writes:
            root, sub = self._split(res)
            s = self._st(root)
            if sub is None:
                s["w"] = tok
                s["r"] = []
                s["subs"] = {}
            else:
                s["subs"][sub] = {"w": tok, "r": []}

    def op(self, e, fn, reads=(), writes=()):
        self._deps(e, reads, writes)
        ins = fn(self.eng[e])
        self.cnt[e] += 1
        ins.then_inc(self.sem[e], 1)
        tok = (self.sem[e], self.cnt[e], e)
        self._record(tok, reads, writes)
        return tok

    def dma(self, q, out, in_, reads=(), writes=(), is_output=False, **kw):
        e = self.dq[q]
        i = self.dcnt[q]
        slot = i % NDMA_SLOTS
        rnd = i // NDMA_SLOTS
        key = "d_%s%d" % (q, slot)
        sem = self.dsem[q][slot]
        if rnd > 0:
            self._need(e, (sem, 16 * rnd, key))
        self._deps(e, reads, writes)
        ins = self.eng[e].dma_start(out=out, in_=in_, **kw)
        ins.then_inc(sem, 16)
        self.dcnt[q] += 1
        tok = (sem, 16 * (rnd + 1), key)
        self._record(tok, reads, writes)
        if is_output:
            self.out_tokens.append(tok)
        return tok

    def collective(self, kind, ins, outs, groups, reads=(), writes=()):
        q, e = "pool", "pool"
        i = self.dcnt[q]
        slot, rnd = i % NDMA_SLOTS, i // NDMA_SLOTS
        key = "d_%s%d" % (q, slot)
        sem = self.dsem[q][slot]
        if rnd > 0:
            self._need(e, (sem, 16 * rnd, key))
        self._deps(e, reads, writes)
        ins_ = self.eng[e].collective_compute(kind, ALU.bypass, replica_groups=groups, ins=ins, outs=outs)
        ins_.then_inc(sem, 16)
        self.dcnt[q] += 1
        tok = (sem, 16 * (rnd + 1), key)
        self._record(tok, reads, writes)
        return tok

    def barrier(self):
        toks = [(self.sem[e], self.cnt[e], e) for e in self.eng if self.cnt[e] > 0]
        for q in self.dq:
            n = self.dcnt[q]
            for slot in range(NDMA_SLOTS):
                if n > slot:
                    rounds = (n - 1 - slot) // NDMA_SLOTS + 1
                    toks.append((self.dsem[q][slot], 16 * rounds, "d_%s%d" % (q, slot)))
        for e in self.eng:
            for t in toks:
                if not (t[2] == e):
                    self._need(e, t)
                elif e != "pe":
                    self._need(e, t)

    def finish(self):
        for t in self.out_tokens:
            self._need("sp", t)
        self.barrier()


S = 4096
D = 2048
NKC = D // 128
NTT = S // 512
EPS = 1e-6
MASKV = -200.0


class PsumPool:
    def __init__(self, nc, es, n=8, tiles=None, names=None):
        if tiles is None:
            self.t = [es.enter_context(nc.psum_tensor("ps%d" % i, [128, 512], F32)) for i in range(n)]
            self.names = ["ps%d" % i for i in range(n)]
        else:
            self.t, self.names = tiles, names
        self.i = 0

    def sub(self, idxs):
        return PsumPool(None, None, tiles=[self.t[i] for i in idxs], names=[self.names[i] for i in idxs])

    def next(self):
        k = self.i % len(self.t)
        self.i += 1
        return self.t[k], self.names[k]


class Rot:
    def __init__(self, nc, es, name, shape, dtype, n):
        self.t = [es.enter_context(_sbt(nc, "%s%d" % (name, i), shape, dtype)) for i in range(n)]
        self.names = ["%s%d" % (name, i) for i in range(n)]
        self.i = 0

    def next(self):
        k = self.i % len(self.t)
        self.i += 1
        return self.t[k], self.names[k]


def load_consts(nc, es, Sc, cst):
    ncols = 128 * 4 + 2048 + 256
    ct = es.enter_context(_sbt(nc, "consts", [128, ncols], BF16))
    Sc.dma("pool", ct[:], cst, writes=["consts"])
    c = {}
    c["ident"] = ct[:, 0:128]
    c["negtri"] = ct[:, 128:256]
    c["negones"] = ct[:, 256:384]
    c["ones"] = ct[:, 384:512]
    c["sbmask"] = [ct[:, 512 + m * 512: 512 + (m + 1) * 512] for m in range(4)]
    c["dmask"] = ct[:, 2560:2816]
    return c


def make_consts_np():
    ncols = 128 * 4 + 2048 + 256
    c = np.zeros((128, ncols), np.float32)
    j = np.arange(128)[:, None]
    s = np.arange(128)[None, :]
    c[:, 0:128] = np.eye(128)
    c[:, 128:256] = -1.0 * (j >= s)
    c[:, 256:384] = -1.0
    c[:, 384:512] = 1.0
    col = np.arange(512)[None, :]
    for m in range(4):
        valid = col > (m * 128 + j)
        c[:, 512 + m * 512: 512 + (m + 1) * 512] = np.where(valid, 0.0, MASKV)
    c[:, 2560:2688] = np.where(j <= s, 0.0, MASKV)
    c[:, 2688:2816] = np.where(j >= s, 0.0, MASKV)
    return c


def phase12(nc, Sc, PS, x, gain_bc, W, scr, heads, head_kinds, cst_extra, ntt_tok=32):
    with ExitStack() as es:
        hdnT = es.enter_context(_sbt(nc, "hdnT", [128, NKC, S], BF16))
        with ExitStack() as e1:
            gbc = e1.enter_context(_sbt(nc, "gbc", [128, D], F32))
            ss = e1.enter_context(_sbt(nc, "ss", [128, 32], F32))
            rstd = e1.enter_context(_sbt(nc, "rstd", [128, 32], F32))
            junk = e1.enter_context(_sbt(nc, "junk", [128, D], BF16))
            xt_r = Rot(nc, e1, "xt", [128, D], F32, 2)
            xs_r = Rot(nc, e1, "xs", [128, D], BF16, 2)
            Sc.dma("sp", gbc[:], gain_bc, writes=["gbc"])
            Sc.op("dve", lambda e: e.memset(ss[:], 0.0), writes=["ss"])
            for tt in range(ntt_tok):
                xt, xtn = xt_r.next()
                Sc.dma("sp", xt[:], x[tt * 128:(tt + 1) * 128, :], writes=[xtn])
                Sc.op("act", lambda e: e.activation(junk[:], xt[:], AF.Square, accum_out=ss[:, tt:tt + 1]),
                      reads=[xtn], writes=[("ss", tt)])
                Sc.op("act", lambda e: e.activation(rstd[:, tt:tt + 1], ss[:, tt:tt + 1], AF.Ln,
                                                    scale=1.0 / D, bias=EPS),
                      reads=[("ss", tt)], writes=[("rstd", tt)])
                Sc.op("act", lambda e: e.activation(rstd[:, tt:tt + 1], rstd[:, tt:tt + 1], AF.Exp, scale=-0.5),
                      reads=[("rstd", tt)], writes=[("rstd", tt)])
                xs, xsn = xs_r.next()
                Sc.op("dve", lambda e: e.scalar_tensor_tensor(xs[:], xt[:], rstd[:, tt:tt + 1], gbc[:],
                                                              ALU.mult, ALU.mult),
                      reads=[xtn, ("rstd", tt), "gbc"], writes=[xsn])
                for half in range(2):
                    ps, psn = PS.next()
                    psb = ps[:].bitcast(BF16)
                    for j in range(8):
                        kc = half * 8 + j
                        Sc.op("pe", lambda e: e.transpose(psb[:, j * 128:(j + 1) * 128],
                                                          xs[:, kc * 128:(kc + 1) * 128], cst_extra["ident"]),
                              reads=[xsn, "consts"], writes=[psn])
                    eng = "act" if half == 0 else "dve"
                    src = psb.rearrange("p (j t) -> p j t", j=8)
                    dst = hdnT[:, half * 8:(half + 1) * 8, tt * 128:(tt + 1) * 128]
                    if eng == "act":
                        Sc.op("act", lambda e: e.copy(dst, src), reads=[psn], writes=[("hdnT", (tt, half))])
                    else:
                        Sc.op("dve", lambda e: e.tensor_copy(dst, src), reads=[psn], writes=[("hdnT", (tt, half))])
        Sc.barrier()
        w_r = Rot(nc, es, "wt", [128, NKC, 512], BF16, 2)
        st_r = Rot(nc, es, "st", [128, 4, 512], BF16, 3)
        sq_r = Rot(nc, es, "sq", [128, 512], BF16, 2)
        rs_r = Rot(nc, es, "rs", [128, 512], F32, 2)
        eg_r = Rot(nc, es, "eg", [128, 512], F32, 2)
        for hd in heads:
            kind = head_kinds[hd]
            wt, wtn = w_r.next()
            for g4 in range(4):
                Sc.dma("pool", wt[:, g4 * 4:(g4 + 1) * 4, :], W[hd, :, g4 * 4:(g4 + 1) * 4, :],
                       writes=[(wtn, g4)])
            for tt in range(NTT):
                st, stn = st_r.next()
                pss = []
                for c in range(4):
                    ps, psn = PS.next()
                    pss.append((ps, psn))
                    for kc in range(NKC):
                        Sc.op("pe", lambda e: e.matmul(ps[:], wt[:, kc, c * 128:(c + 1) * 128],
                                                       hdnT[:, kc, tt * 512:(tt + 1) * 512],
                                                       start=(kc == 0), stop=(kc == NKC - 1)),
                              reads=[(wtn, kc // 4)], writes=[psn])
                (pq, pqn), (pk, pkn), (pv, pvn), (pg, pgn) = pss
                if kind == "sb":
                    sc = 1.0 / math.sqrt(128.0)
                    Sc.op("act", lambda e: e.mul(st[:, 0, :], pq[:], sc), reads=[pqn], writes=[(stn, 0)])
                    Sc.op("dve", lambda e: e.tensor_copy(st[:, 1, :], pk[:]), reads=[pkn], writes=[(stn, 1)])
                elif kind == "dil":
                    for ci, (pp, ppn, gcol) in enumerate(((pq, pqn, cst_extra["qg"]), (pk, pkn, cst_extra["kg"]))):
                        sq, sqn = sq_r.next()
                        Sc.op("act", lambda e: e.activation(sq[:], pp[:], AF.Square), reads=[ppn], writes=[sqn])
                        p2, p2n = PS.next()
                        Sc.op("pe", lambda e: e.matmul(p2[:], cst_extra["ones"], sq[:], start=True, stop=True),
                              reads=[sqn, "consts"], writes=[p2n])
                        rs, rsn = rs_r.next()
                        Sc.op("act", lambda e: e.activation(rs[:], p2[:], AF.Ln, bias=128.0 * EPS),
                              reads=[p2n], writes=[rsn])
                        Sc.op("act", lambda e: e.activation(rs[:], rs[:], AF.Exp, scale=-0.5),
                              reads=[rsn], writes=[rsn])
                        Sc.op("dve", lambda e: e.scalar_tensor_tensor(st[:, ci, :], pp[:], gcol, rs[:],
                                                                      ALU.mult, ALU.mult),
                              reads=[ppn, rsn, "gcols"], writes=[(stn, ci)])
                else:
                    Sc.op("act", lambda e: e.copy(st[:, 0, :], pq[:]), reads=[pqn], writes=[(stn, 0)])
                    Sc.op("dve", lambda e: e.tensor_copy(st[:, 1, :], pk[:]), reads=[pkn], writes=[(stn, 1)])
                Sc.op("act", lambda e: e.copy(st[:, 2, :], pv[:]), reads=[pvn], writes=[(stn, 2)])
                eg, egn = eg_r.next()
                Sc.op("act", lambda e: e.activation(eg[:], pg[:], AF.Exp, scale=-1.0), reads=[pgn], writes=[egn])
                Sc.op("dve", lambda e: e.tensor_scalar(eg[:], eg[:], 1.0, None, ALU.add), reads=[egn], writes=[egn])
                Sc.op("dve", lambda e: e.reciprocal(eg[:], eg[:]), reads=[egn], writes=[egn])
                Sc.op("dve", lambda e: e.tensor_tensor(st[:, 3, :], pg[:], eg[:], ALU.mult),
                      reads=[pgn, egn], writes=[(stn, 3)])
                dst = scr[hd, :, :, tt * 512:(tt + 1) * 512].rearrange("c p t -> p c t")
                Sc.dma("sp", dst, st[:], reads=[stn])
    Sc.barrier()


def load_head(nc, Sc, hb, hbn, scr, hd):
    for c in range(4):
        Sc.dma("sp", hb[:, c, :], scr[hd, c, :, :], writes=[(hbn, c)])


def make_vtok(nc, Sc, PS, cst, hb, hbn, vtok, vtokn, dil, srcT=None, srckey=None, fill=None):
    nb = 32 // dil
    if srcT is None:
        srcT, srckey = hb[:, 2, :], (hbn, 2)
    for g in range(8):
        ps, psn = PS.next()
        psb = ps[:].bitcast(BF16)
        for j in range(4):
            blk = g * 4 + j
            p, n = blk // nb, blk % nb
            start = p + dil * 128 * n
            src = srcT[:, start:start + dil * 127 + 1:dil]
            Sc.op("pe", lambda e: e.transpose(psb[:, j * 128:(j + 1) * 128], src, cst["ident"]),
                  reads=[srckey, "consts"], writes=[psn])
        dst = vtok[:, g * 4:(g + 1) * 4, :]
        srcp = psb[:, 0:512].rearrange("p (j t) -> p j t", j=4)
        if g % 2 == 0:
            Sc.op("dve", lambda e: e.tensor_copy(dst, srcp), reads=[psn], writes=[(vtokn, g)])
        else:
            Sc.op("act", lambda e: e.copy(dst, srcp), reads=[psn], writes=[(vtokn, g)])
        if fill:
            fill(1)


def sb_head(nc, Sc, PS, PSO, cst, R, hb, hbn, vtok, vtokn, out_dram, is_output=True):
    qT, kT, gT = hb[:, 0, :], hb[:, 1, :], hb[:, 3, :]
    for qt in range(NTT):
        nkb = 4 * (qt + 1)
        qs = qT[:, qt * 512:(qt + 1) * 512]
        o_ps, o_psn = PSO.next()
        carry = None
        carryn = None
        for idx, kb in enumerate(range(nkb - 1, -1, -1)):
            m = kb - 4 * qt
            diag = m >= 0
            ks = kT[:, kb * 128:(kb + 1) * 128]
            z_ps, z_psn = PS.next()
            Sc.op("pe", lambda e: e.matmul(z_ps[:], ks, qs, start=True, stop=not diag),
                  reads=[(hbn, 0), (hbn, 1)], writes=[z_psn])
            if diag:
                Sc.op("pe", lambda e: e.matmul(z_ps[:], cst["ident"], cst["sbmask"][m], start=False, stop=True),
                      reads=["consts"], writes=[z_psn])
            ee, een = R["e"].next()
            Sc.op("act", lambda e: e.activation(ee[:], z_ps[:], AF.Exp), reads=[z_psn], writes=[een])
            sp, spn = R["sp"].next()
            Sc.op("act", lambda e: e.activation(sp[:], ee[:], AF.Ln, bias=1.0), reads=[een], writes=[spn])
            a_ps, a_psn = PS.next()
            Sc.op("pe", lambda e: e.matmul(a_ps[:], ks, qs, start=True, stop=False),
                  reads=[(hbn, 0), (hbn, 1)], writes=[a_psn])
            if diag:
                Sc.op("pe", lambda e: e.matmul(a_ps[:], cst["ident"], cst["sbmask"][m], start=False, stop=False),
                      reads=["consts"], writes=[a_psn])
            Sc.op("pe", lambda e: e.matmul(a_ps[:], cst["negtri"], sp[:], start=False, stop=(carry is None)),
                  reads=["consts", spn], writes=[a_psn])
            if carry is not None:
                Sc.op("pe", lambda e: e.matmul(a_ps[:], cst["negones"], carry[:], start=False, stop=True),
                      reads=["consts", carryn], writes=[a_psn])
            at, atn = R["a"].next()
            Sc.op("act", lambda e: e.activation(at[:], a_ps[:], AF.Exp), reads=[a_psn], writes=[atn])
            Sc.op("pe", lambda e: e.matmul(o_ps[:], vtok[:, kb, :], at[:], start=(idx == 0), stop=(kb == 0)),
                  reads=[(vtokn, kb // 4), atn], writes=[o_psn])
            if kb > 0:
                if carry is None:
                    carry, carryn = sp, spn
                else:
                    nc_, ncn = R["carry"].next()
                    Sc.op("dve", lambda e: e.tensor_tensor(nc_[:], carry[:], sp[:], ALU.add),
                          reads=[carryn, spn], writes=[ncn])
                    carry, carryn = nc_, ncn
        ot, otn = R["o"].next()
        Sc.op("dve", lambda e: e.tensor_tensor(ot[:], o_ps[:], gT[:, qt * 512:(qt + 1) * 512], ALU.mult),
              reads=[o_psn, (hbn, 3)], writes=[otn])
        Sc.dma("sp", out_dram[:, qt * 512:(qt + 1) * 512], ot[:], reads=[otn], is_output=is_output)


def dil_cols(p, n, r):
    start = p + r * 128 * n
    return slice(start, start + r * 127 + 1, r)


def dil_head(nc, Sc, PS, cst, R, hb, hbn, vtoks, vtokns, UZ, out_dram, is_output=True):
    qT, kT, gT = hb[:, 0, :], hb[:, 1, :], hb[:, 3, :]
    sc = math.sqrt(128.0)
    for gi, r in enumerate((1, 4, 16)):
        make_vtok(nc, Sc, PS, cst, hb, hbn, vtoks[gi], vtokns[gi], r)
    for gi, r in enumerate((1, 4, 16)):
        nb = 32 // r
        vt, vtn = vtoks[gi], vtokns[gi]
        for p in range(r):
            for n in range(nb):
                blk = p * nb + n
                cq = dil_cols(p, n, r)
                w = 256 if n >= 1 else 128
                s_ps, s_psn = PS.next()
                Sc.op("pe", lambda e: e.matmul(s_ps[:, 0:128], kT[:, cq], qT[:, cq], start=True, stop=False),
                      reads=[(hbn, 0), (hbn, 1)], writes=[s_psn])
                Sc.op("pe", lambda e: e.matmul(s_ps[:, 0:128], cst["ident"], cst["dmask"][:, 0:128],
                                               start=False, stop=True), reads=["consts"], writes=[s_psn])
                if n >= 1:
                    ck = dil_cols(p, n - 1, r)
                    Sc.op("pe", lambda e: e.matmul(s_ps[:, 128:256], kT[:, ck], qT[:, cq], start=True, stop=False),
                          reads=[(hbn, 0), (hbn, 1)], writes=[s_psn])
                    Sc.op("pe", lambda e: e.matmul(s_ps[:, 128:256], cst["ident"], cst["dmask"][:, 128:256],
                                                   start=False, stop=True), reads=["consts"], writes=[s_psn])
                pt, ptn = R["p"].next()
                Sc.op("act", lambda e: e.activation(pt[:, 0:w], s_ps[:, 0:w], AF.Exp, scale=sc),
                      reads=[s_psn], writes=[ptn])
                uz_ps, uz_psn = PS.next()
                for half, lhs in enumerate((None, cst["ones"])):
                    o = uz_ps[:, half * 128:(half + 1) * 128]
                    l0 = vt[:, blk, :] if half == 0 else lhs
                    Sc.op("pe", lambda e: e.matmul(o, l0, pt[:, 0:128], start=True, stop=(n == 0)),
                          reads=[(vtn, blk // 4), ptn, "consts"], writes=[uz_psn])
                    if n >= 1:
                        l1 = vt[:, blk - 1, :] if half == 0 else lhs
                        Sc.op("pe", lambda e: e.matmul(o, l1, pt[:, 128:256], start=False, stop=True),
                              reads=[(vtn, (blk - 1) // 4), ptn, "consts"], writes=[uz_psn])
                acc = UZ[:, :, cq]
                src = uz_ps[:, 0:256].rearrange("p (a t) -> p a t", a=2)
                if gi == 0:
                    wr = [("uz0", n)]
                    if n % 2 == 0:
                        Sc.op("dve", lambda e: e.tensor_copy(acc, src), reads=[uz_psn], writes=wr)
                    else:
                        Sc.op("act", lambda e: e.copy(acc, src), reads=[uz_psn], writes=wr)
                else:
                    if gi == 1:
                        rd = [("uz0", 4 * n + i) for i in range(4)]
                        wr = [("uz1", (n, p))]
                    else:
                        rd = [("uz1", (c, p % 4)) for c in range(4 * n, 4 * n + 4)]
                        wr = [("uz2", (n, p))]
                    Sc.op("dve", lambda e: e.tensor_tensor(acc, acc, src, ALU.add),
                          reads=[uz_psn] + rd, writes=wr)
    for ch in range(8):
        cs = slice(ch * 512, (ch + 1) * 512)
        rz, rzn = R["rz"].next()
        Sc.op("dve", lambda e: e.reciprocal(rz[:], UZ[:, 1, cs]), reads=["uz0", "uz1", "uz2"], writes=[rzn])
        Sc.op("pool", lambda e: e.tensor_tensor(rz[:], rz[:], UZ[:, 0, cs], ALU.mult),
              reads=[rzn, "uz0", "uz1", "uz2"], writes=[rzn])
        ot, otn = R["o"].next()
        Sc.op("pool", lambda e: e.tensor_tensor(ot[:], rz[:], gT[:, cs], ALU.mult),
              reads=[rzn, (hbn, 3)], writes=[otn])
        Sc.dma("sp", out_dram[:, cs], ot[:], reads=[otn], is_output=is_output)


def hgrn_head(nc, Sc, PS, PSO, cst, R, T, hb, hbn, lbc, omlc, ogc, out_dram, is_output=True):
    qT, fT, vT, gT = hb[:, 0, :], hb[:, 1, :], hb[:, 2, :], hb[:, 3, :]
    E, B2, CUM, KT, QT, KTt, EL = T["E"], T["B2"], T["CUM"], T["KT"], T["QT"], T["KTt"], T["EL"]
    ktok, vtok = T["ktok"], T["vtok"]
    Sc.op("act", lambda e: e.activation(E[:], fT, AF.Exp, scale=-1.0), reads=[(hbn, 1)], writes=["E"])
    Sc.op("dve", lambda e: e.tensor_scalar(B2[:], E[:], 1.0, None, ALU.add), reads=["E"], writes=["B2"])
    Sc.op("dve", lambda e: e.reciprocal(B2[:], B2[:]), reads=["B2"], writes=["B2"])
    Sc.op("dve", lambda e: e.scalar_tensor_tensor(KT[:], E[:], omlc, B2[:], ALU.mult, ALU.mult),
          reads=["E", "B2", "hconst"], writes=["KT"])
    Sc.op("dve", lambda e: e.tensor_scalar(B2[:], B2[:], omlc, lbc, ALU.mult, ALU.add),
          reads=["B2", "hconst"], writes=["B2"])
    Sc.op("act", lambda e: e.activation(B2[:], B2[:], AF.Ln), reads=["B2"], writes=["B2"])
    Sc.op("dve", lambda e: e.tensor_tensor_scan(CUM[:], T["rmask"], B2[:], 0.0, ALU.mult, ALU.add),
          reads=["B2", "consts2"], writes=["CUM"])
    Sc.op("act", lambda e: e.activation(E[:], CUM[:], AF.Exp), reads=["CUM"], writes=["E"])
    Sc.op("dve", lambda e: e.tensor_tensor(QT[:], qT, E[:], ALU.mult), reads=["E", (hbn, 0)], writes=["QT"])
    Sc.op("act", lambda e: e.activation(B2[:], CUM[:], AF.Exp, scale=-1.0), reads=["CUM"], writes=["B2"])
    Sc.op("dve", lambda e: e.tensor_tensor(KTt[:], KT[:], B2[:], ALU.mult), reads=["KT", "B2"], writes=["KTt"])
    Sc.op("act", lambda e: e.activation(EL[:], CUM[:, 127:4096:128], AF.Exp), reads=["CUM"], writes=["EL"])
    make_vtok(nc, Sc, PS, cst, hb, hbn, ktok, "ktok", 1, srcT=KTt[:], srckey="KTt")
    make_vtok(nc, Sc, PS, cst, hb, hbn, vtok, "vtok", 1)
    state = None
    for c in range(32):
        tsl = slice(c * 128, (c + 1) * 128)
        at_ps, at_psn = PS.next()
        Sc.op("pe", lambda e: e.matmul(at_ps[:, 0:128], KTt[:, tsl], QT[:, tsl], start=True, stop=True),
              reads=["KTt", "QT"], writes=[at_psn])
        am, amn = R["am"].next()
        Sc.op("dve", lambda e: e.tensor_tensor(am[:], at_ps[:, 0:128], T["mask01"], ALU.mult),
              reads=[at_psn, "consts2"], writes=[amn])
        if c % 4 == 0:
            o_ps, o_psn = PSO.next()
        oc = slice((c % 4) * 128, (c % 4 + 1) * 128)
        if state is not None:
            st_t, st_n = state
            Sc.op("pe", lambda e: e.matmul(o_ps[:, oc], st_t[:], QT[:, tsl], start=True, stop=False),
                  reads=[st_n, "QT"], writes=[o_psn])
        Sc.op("pe", lambda e: e.matmul(o_ps[:, oc], vtok[:, c, :], am[:], start=(state is None), stop=True),
              reads=[("vtok", c // 4), amn], writes=[o_psn])
        if c < 31:
            p2, p2n = PS.next()
            Sc.op("pe", lambda e: e.matmul(p2[:, 0:128], ktok[:, c, :], vtok[:, c, :], start=True, stop=True),
                  reads=[("ktok", c // 4), ("vtok", c // 4)], writes=[p2n])
            ns, nsn = R["state"].next()
            if state is None:
                Sc.op("act", lambda e: e.mul(ns[:], p2[:, 0:128], EL[:, c:c + 1]), reads=[p2n, "EL"], writes=[nsn])
            else:
                pe_, pen = R["pse"].next()
                Sc.op("act", lambda e: e.mul(pe_[:], p2[:, 0:128], EL[:, c:c + 1]), reads=[p2n, "EL"], writes=[pen])
                Sc.op("dve", lambda e: e.scalar_tensor_tensor(ns[:], st_t[:], EL[:, c:c + 1], pe_[:],
                                                              ALU.mult, ALU.add),
                      reads=[st_n, "EL", pen], writes=[nsn])
            state = (ns, nsn)
        if c % 4 == 3:
            cs = slice((c // 4) * 512, (c // 4 + 1) * 512)
            sq, sqn = R["sq"].next()
            Sc.op("act", lambda e: e.activation(sq[:], o_ps[:], AF.Square), reads=[o_psn], writes=[sqn])
            ss_ps, ss_psn = PS.next()
            Sc.op("pe", lambda e: e.matmul(ss_ps[:], cst["ones"], sq[:], start=True, stop=True),
                  reads=[sqn, "consts"], writes=[ss_psn])
            rs, rsn = R["rs"].next()
            Sc.op("act", lambda e: e.activation(rs[:], ss_ps[:], AF.Ln, scale=1.0 / 128.0, bias=EPS),
                  reads=[ss_psn], writes=[rsn])
            Sc.op("act", lambda e: e.activation(rs[:], rs[:], AF.Exp, scale=-0.5), reads=[rsn], writes=[rsn])
            Sc.op("dve", lambda e: e.tensor_tensor(rs[:], o_ps[:], rs[:], ALU.mult), reads=[o_psn, rsn], writes=[rsn])
            ot, otn = R["o"].next()
            Sc.op("dve", lambda e: e.scalar_tensor_tensor(ot[:], rs[:], ogc, gT[:, cs], ALU.mult, ALU.mult),
                  reads=[rsn, "hconst", "ogt", (hbn, 3)], writes=[otn])
            Sc.dma("sp", out_dram[:, cs], ot[:], reads=[otn], is_output=is_output)


def sb_head_v2(nc, Sc, PS, PSO, cst, R, hb, hbn, vtok, vtokn, out_dram, is_output=True, fill=None):
    qT, kT, gT = hb[:, 0, :], hb[:, 1, :], hb[:, 3, :]
    blocks = []
    for qt in range(NTT):
        nkb = 4 * (qt + 1)
        for idx, kb in enumerate(range(nkb - 1, -1, -1)):
            blocks.append({"qt": qt, "kb": kb, "first": idx == 0, "last": kb == 0})
    qstate = {}

    def stage_z(b):
        qt, kb = b["qt"], b["kb"]
        m = kb - 4 * qt
        qs = qT[:, qt * 512:(qt + 1) * 512]
        ks = kT[:, kb * 128:(kb + 1) * 128]
        z_ps, z_psn = PS.next()
        Sc.op("pe", lambda e: e.matmul(z_ps[:], ks, qs, start=True, stop=False, skip_group_check=True),
              reads=[(hbn, 0), (hbn, 1)], writes=[z_psn])
        if m >= 0:
            Sc.op("pe", lambda e: e.matmul(z_ps[:], cst["ident"], cst["sbmask"][m], start=False, stop=False,
                                           skip_group_check=True),
                  reads=["consts"], writes=[z_psn])
        b["z"] = (z_ps, z_psn)
        ee, een = R["e"].next()
        Sc.op("act", lambda e: e.activation(ee[:], z_ps[:], AF.Exp), reads=[z_psn], writes=[een])
        sp, spn = R["sp"].next()
        Sc.op("act", lambda e: e.activation(sp[:], ee[:], AF.Ln, bias=1.0), reads=[een], writes=[spn])
        b["sp"] = (sp, spn)

    def stage_arg(b):
        qt, kb = b["qt"], b["kb"]
        m = kb - 4 * qt
        qs = qT[:, qt * 512:(qt + 1) * 512]
        ks = kT[:, kb * 128:(kb + 1) * 128]
        sp, spn = b["sp"]
        if b["first"]:
            qstate[qt] = None
        carry = qstate[qt]
        a_ps, a_psn = b["z"]
        Sc.op("pe", lambda e: e.matmul(a_ps[:], cst["negtri"], sp[:], start=False, stop=(carry is None),
                                       skip_group_check=True),
              reads=["consts", spn], writes=[a_psn])
        if carry is not None:
            Sc.op("pe", lambda e: e.matmul(a_ps[:], cst["negones"], carry[0][:], start=False, stop=True,
                                           skip_group_check=True),
                  reads=["consts", carry[1]], writes=[a_psn])
        at, atn = R["a"].next()
        Sc.op("act", lambda e: e.activation(at[:], a_ps[:], AF.Exp), reads=[a_psn], writes=[atn])
        b["at"] = (at, atn)
        if kb > 0:
            if carry is None:
                qstate[qt] = (sp, spn)
            else:
                nc_, ncn = R["carry"].next()
                Sc.op("dve", lambda e: e.tensor_tensor(nc_[:], carry[0][:], sp[:], ALU.add),
                      reads=[carry[1], spn], writes=[ncn])
                qstate[qt] = (nc_, ncn)

    ops = {}

    def stage_av(b):
        qt, kb = b["qt"], b["kb"]
        at, atn = b["at"]
        if b["first"]:
            ops[qt] = PSO.next()
        o_ps, o_psn = ops[qt]
        Sc.op("pe", lambda e: e.matmul(o_ps[:], vtok[:, kb, :], at[:], start=b["first"], stop=b["last"]),
              reads=[(vtokn, kb // 4), atn], writes=[o_psn])
        if b["last"]:
            ot, otn = R["o"].next()
            Sc.op("dve", lambda e: e.tensor_tensor(ot[:], o_ps[:], gT[:, qt * 512:(qt + 1) * 512], ALU.mult),
                  reads=[o_psn, (hbn, 3)], writes=[otn])
            Sc.dma("sp", out_dram[:, qt * 512:(qt + 1) * 512], ot[:], reads=[otn], is_output=is_output)

    n = len(blocks)
    for k in range(n + 2):
        if k < n:
            stage_z(blocks[k])
        if 0 <= k - 1 < n:
            stage_arg(blocks[k - 1])
        if 0 <= k - 2 < n:
            stage_av(blocks[k - 2])
        if fill:
            fill(1)


def hgrn_head_v2(nc, Sc, PS, PSO, cst, R, T, hb, hbn, lbc, omlc, ogc, out_dram, is_output=True, fill=None):
    qT, fT, vT, gT = hb[:, 0, :], hb[:, 1, :], hb[:, 2, :], hb[:, 3, :]
    E, B2, CUM, KT, QT, EL = T["E"], T["B2"], T["CUM"], T["KT"], T["QT"], T["EL"]
    ktok, vtok, AM, ST = T["ktok"], T["vtok"], T["AM"], T["ST"]
    f_ = fill if fill else (lambda n: None)
    H = S // 2
    for hf in range(2):
        cs = slice(hf * H, (hf + 1) * H)
        Sc.op("act", lambda e: e.activation(E[:], fT[:, cs], AF.Exp, scale=-1.0), reads=[(hbn, 1)], writes=["E"])
        f_(5)
        Sc.op("dve", lambda e: e.tensor_scalar(B2[:], E[:], 1.0, None, ALU.add), reads=["E"], writes=["B2"])
        Sc.op("dve", lambda e: e.reciprocal(B2[:], B2[:]), reads=["B2"], writes=["B2"])
        f_(5)
        Sc.op("dve", lambda e: e.scalar_tensor_tensor(KT[:, cs], E[:], omlc, B2[:], ALU.mult, ALU.mult),
              reads=["E", "B2", "hconst"], writes=[("KT", hf)])
        Sc.op("dve", lambda e: e.tensor_scalar(B2[:], B2[:], omlc, lbc, ALU.mult, ALU.add),
              reads=["B2", "hconst"], writes=["B2"])
        f_(5)
        Sc.op("act", lambda e: e.activation(B2[:], B2[:], AF.Ln), reads=["B2"], writes=["B2"])
        Sc.op("dve", lambda e: e.tensor_tensor_scan(CUM[:], T["rmask"][:, cs], B2[:], 0.0, ALU.mult, ALU.add),
              reads=["B2", "consts2"], writes=["CUM"])
        f_(5)
        Sc.op("act", lambda e: e.activation(E[:], CUM[:], AF.Exp), reads=["CUM"], writes=["E"])
        Sc.op("dve", lambda e: e.tensor_tensor(QT[:, cs], qT[:, cs], E[:], ALU.mult),
              reads=["E", (hbn, 0)], writes=[("QT", hf)])
        f_(5)
        Sc.op("act", lambda e: e.activation(B2[:], CUM[:], AF.Exp, scale=-1.0), reads=["CUM"], writes=["B2"])
        Sc.op("dve", lambda e: e.tensor_tensor(KT[:, cs], KT[:, cs], B2[:], ALU.mult),
              reads=[("KT", hf), "B2"], writes=[("KT", hf)])
        Sc.op("act", lambda e: e.activation(EL[:, hf * 16:(hf + 1) * 16], CUM[:, 127:H:128], AF.Exp),
              reads=["CUM"], writes=[("EL", hf)])
        f_(5)
    make_vtok(nc, Sc, PS, cst, hb, hbn, ktok, "ktok", 1, srcT=KT[:], srckey="KT", fill=fill)
    make_vtok(nc, Sc, PS, cst, hb, hbn, vtok, "vtok", 1, fill=fill)
    for g in range(8):
        at_ps, at_psn = PS.next()
        for j in range(4):
            c = g * 4 + j
            tsl = slice(c * 128, (c + 1) * 128)
            Sc.op("pe", lambda e: e.matmul(at_ps[:, j * 128:(j + 1) * 128], KT[:, tsl], QT[:, tsl],
                                           start=True, stop=True), reads=["KT", "QT"], writes=[at_psn])
        Sc.op("dve", lambda e: e.tensor_tensor(AM[:, g * 4:(g + 1) * 4, :],
                                               at_ps[:].rearrange("p (j t) -> p j t", j=4),
                                               T["mask4"].rearrange("p (j t) -> p j t", j=4), ALU.mult),
              reads=[at_psn, "consts2"], writes=[("AM", g)])
        p2, p2n = PS.next()
        for j in range(4):
            c = g * 4 + j
            if c == 31:
                continue
            Sc.op("pe", lambda e: e.matmul(p2[:, j * 128:(j + 1) * 128], ktok[:, c, :], vtok[:, c, :],
                                           start=True, stop=True),
                  reads=[("ktok", c // 4), ("vtok", c // 4)], writes=[p2n])
        for j in range(4):
            c = g * 4 + j
            if c == 31:
                continue
            Sc.op("act", lambda e: e.mul(ST[:, c + 1, :], p2[:, j * 128:(j + 1) * 128], EL[:, c:c + 1]),
                  reads=[p2n, "EL"], writes=[("ST", c + 1)])
        f_(3)
    for c in range(1, 31):
        Sc.op("dve", lambda e: e.scalar_tensor_tensor(ST[:, c + 1, :], ST[:, c, :], EL[:, c:c + 1], ST[:, c + 1, :],
                                                      ALU.mult, ALU.add),
              reads=[("ST", c), "EL", ("ST", c + 1)], writes=[("ST", c + 1)])
        f_(1)
    for g in range(8):
        o_ps, o_psn = PSO.next()
        for j in range(4):
            c = g * 4 + j
            tsl = slice(c * 128, (c + 1) * 128)
            oc = slice(j * 128, (j + 1) * 128)
            if c > 0:
                Sc.op("pe", lambda e: e.matmul(o_ps[:, oc], ST[:, c, :], QT[:, tsl], start=True, stop=False),
                      reads=[("ST", c), "QT"], writes=[o_psn])
            Sc.op("pe", lambda e: e.matmul(o_ps[:, oc], vtok[:, c, :], AM[:, c, :], start=(c == 0), stop=True),
                  reads=[("vtok", c // 4), ("AM", g)], writes=[o_psn])
        cs = slice(g * 512, (g + 1) * 512)
        sq, sqn = R["sq"].next()
        Sc.op("act", lambda e: e.activation(sq[:], o_ps[:], AF.Square), reads=[o_psn], writes=[sqn])
        ss_ps, ss_psn = PS.next()
        Sc.op("pe", lambda e: e.matmul(ss_ps[:], cst["ones"], sq[:], start=True, stop=True),
              reads=[sqn, "consts"], writes=[ss_psn])
        rs, rsn = R["rs"].next()
        Sc.op("act", lambda e: e.activation(rs[:], ss_ps[:], AF.Ln, scale=1.0 / 128.0, bias=EPS),
              reads=[ss_psn], writes=[rsn])
        Sc.op("act", lambda e: e.activation(rs[:], rs[:], AF.Exp, scale=-0.5), reads=[rsn], writes=[rsn])
        Sc.op("dve", lambda e: e.tensor_tensor(rs[:], o_ps[:], rs[:], ALU.mult), reads=[o_psn, rsn], writes=[rsn])
        ot, otn = R["o"].next()
        Sc.op("dve", lambda e: e.scalar_tensor_tensor(ot[:], rs[:], ogc, gT[:, cs], ALU.mult, ALU.mult),
              reads=[rsn, "hconst", "ogt", (hbn, 3)], writes=[otn])
        Sc.dma("sp", out_dram[:, cs], ot[:], reads=[otn], is_output=is_output)
        f_(2)


def dil_head_v2(nc, Sc, PS, cst, R, hb, hbn, vtoks, vtokns, UZ, out_dram, is_output=True, fill=None):
    qT, kT, gT = hb[:, 0, :], hb[:, 1, :], hb[:, 3, :]
    sc = math.sqrt(128.0)
    for gi, r in enumerate((1, 4, 16)):
        make_vtok(nc, Sc, PS, cst, hb, hbn, vtoks[gi], vtokns[gi], r)
    blocks = []
    for gi, r in enumerate((1, 4, 16)):
        nb = 32 // r
        for p in range(r):
            for n in range(nb):
                blocks.append({"gi": gi, "r": r, "p": p, "n": n, "blk": p * nb + n})

    def stage_s(b):
        r, p, n = b["r"], b["p"], b["n"]
        cq = dil_cols(p, n, r)
        w = 256 if n >= 1 else 128
        s_ps, s_psn = PS.next()
        Sc.op("pe", lambda e: e.matmul(s_ps[:, 0:w], cst["ident"], cst["dmask"][:, 0:w], start=True, stop=False,
                                       skip_group_check=True),
              reads=["consts"], writes=[s_psn])
        Sc.op("pe", lambda e: e.matmul(s_ps[:, 0:128], kT[:, cq], qT[:, cq], start=False, stop=(n == 0),
                                       skip_group_check=True),
              reads=[(hbn, 0), (hbn, 1)], writes=[s_psn])
        if n >= 1:
            ck = dil_cols(p, n - 1, r)
            Sc.op("pe", lambda e: e.matmul(s_ps[:, 128:256], kT[:, ck], qT[:, cq], start=False, stop=True,
                                           skip_group_check=True),
                  reads=[(hbn, 0), (hbn, 1)], writes=[s_psn])
        pt, ptn = R["p"].next()
        Sc.op("act", lambda e: e.activation(pt[:, 0:w], s_ps[:, 0:w], AF.Exp, scale=sc),
              reads=[s_psn], writes=[ptn])
        b["pt"] = (pt, ptn)

    def stage_uz(b):
        gi, r, p, n, blk = b["gi"], b["r"], b["p"], b["n"], b["blk"]
        vt, vtn = vtoks[gi], vtokns[gi]
        pt, ptn = b["pt"]
        cq = dil_cols(p, n, r)
        uz_ps, uz_psn = PS.next()
        for half in range(2):
            o = uz_ps[:, half * 128:(half + 1) * 128]
            l0 = vt[:, blk, :] if half == 0 else cst["ones"]
            Sc.op("pe", lambda e: e.matmul(o, l0, pt[:, 0:128], start=True, stop=(n == 0)),
                  reads=[(vtn, blk // 4), ptn, "consts"], writes=[uz_psn])
            if n >= 1:
                l1 = vt[:, blk - 1, :] if half == 0 else cst["ones"]
                Sc.op("pe", lambda e: e.matmul(o, l1, pt[:, 128:256], start=False, stop=True),
                      reads=[(vtn, (blk - 1) // 4), ptn, "consts"], writes=[uz_psn])
        acc = UZ[:, :, cq]
        src = uz_ps[:, 0:256].rearrange("p (a t) -> p a t", a=2)
        if gi == 0:
            wr = [("uz0", n)]
            if n % 2 == 0:
                Sc.op("dve", lambda e: e.tensor_copy(acc, src), reads=[uz_psn], writes=wr)
            else:
                Sc.op("act", lambda e: e.copy(acc, src), reads=[uz_psn], writes=wr)
        else:
            if gi == 1:
                rd = [("uz0", 4 * n + i) for i in range(4)]
                wr = [("uz1", (n, p))]
            else:
                rd = [("uz1", (c, p % 4)) for c in range(4 * n, 4 * n + 4)]
                wr = [("uz2", (n, p))]
            Sc.op("dve", lambda e: e.tensor_tensor(acc, acc, src, ALU.add), reads=[uz_psn] + rd, writes=wr)

    nb_ = len(blocks)
    for k in range(nb_ + 1):
        if k < nb_:
            stage_s(blocks[k])
        if k >= 1:
            stage_uz(blocks[k - 1])
        if fill:
            fill(2 if k % 2 == 0 else 1)
    for ch in range(8):
        cs = slice(ch * 512, (ch + 1) * 512)
        rz, rzn = R["rz"].next()
        Sc.op("dve", lambda e: e.reciprocal(rz[:], UZ[:, 1, cs]), reads=["uz0", "uz1", "uz2"], writes=[rzn])
        Sc.op("pool", lambda e: e.tensor_tensor(rz[:], rz[:], UZ[:, 0, cs], ALU.mult),
              reads=[rzn, "uz0", "uz1", "uz2"], writes=[rzn])
        ot, otn = R["o"].next()
        Sc.op("pool", lambda e: e.tensor_tensor(ot[:], rz[:], gT[:, cs], ALU.mult),
              reads=[rzn, (hbn, 3)], writes=[otn])
        Sc.dma("sp", out_dram[:, cs], ot[:], reads=[otn], is_output=is_output)


def phase1_stream(nc, Sc, PS, x, gain_bc, hd_d, cst):
    with ExitStack() as e1:
        gbc = e1.enter_context(_sbt(nc, "gbc", [128, D], F32))
        ss = e1.enter_context(_sbt(nc, "ss", [128, 32], F32))
        rstd = e1.enter_context(_sbt(nc, "rstd", [128, 32], F32))
        junk = e1.enter_context(_sbt(nc, "junk", [128, D], BF16))
        xt_r = Rot(nc, e1, "xt", [128, D], F32, 4)
        xs_r = Rot(nc, e1, "xs", [128, D], BF16, 4)
        hs_r = Rot(nc, e1, "hs1_", [128, NKC, 512], BF16, 2)
        Sc.dma("sp", gbc[:], gain_bc, writes=["gbc"])
        Sc.op("dve", lambda e: e.memset(ss[:], 0.0), writes=["ss"])
        for t5 in range(8):
            hs, hsn = hs_r.next()
            for ti in range(4):
                tt = t5 * 4 + ti
                xt, xtn = xt_r.next()
                Sc.dma("pool", xt[:], x[tt * 128:(tt + 1) * 128, :], writes=[xtn])
                Sc.op("act", lambda e: e.activation(junk[:], xt[:], AF.Square, accum_out=ss[:, tt:tt + 1]),
                      reads=[xtn], writes=[("ss", tt)])
                Sc.op("act", lambda e: e.activation(rstd[:, tt:tt + 1], ss[:, tt:tt + 1], AF.Ln,
                                                    scale=1.0 / D, bias=EPS),
                      reads=[("ss", tt)], writes=[("rstd", tt)])
                Sc.op("act", lambda e: e.activation(rstd[:, tt:tt + 1], rstd[:, tt:tt + 1], AF.Exp, scale=-0.5),
                      reads=[("rstd", tt)], writes=[("rstd", tt)])
                xs, xsn = xs_r.next()
                Sc.op("dve", lambda e: e.scalar_tensor_tensor(xs[:], xt[:], rstd[:, tt:tt + 1], gbc[:],
                                                              ALU.mult, ALU.mult),
                      reads=[xtn, ("rstd", tt), "gbc"], writes=[xsn])
                for half in range(2):
                    ps, psn = PS.next()
                    psb = ps[:].bitcast(BF16)
                    for j in range(8):
                        kc = half * 8 + j
                        Sc.op("pe", lambda e: e.transpose(psb[:, j * 128:(j + 1) * 128],
                                                          xs[:, kc * 128:(kc + 1) * 128], cst["ident"]),
                              reads=[xsn, "consts"], writes=[psn])
                    src = psb.rearrange("p (j t) -> p j t", j=8)
                    dst = hs[:, half * 8:(half + 1) * 8, ti * 128:(ti + 1) * 128]
                    if half == 0:
                        Sc.op("act", lambda e: e.copy(dst, src), reads=[psn], writes=[(hsn, (ti, half))])
                    else:
                        Sc.op("dve", lambda e: e.tensor_copy(dst, src), reads=[psn], writes=[(hsn, (ti, half))])
            Sc.dma("sp", hd_d[t5], hs[:], reads=[hsn])
        Sc.barrier()


class ProjStream:
    def __init__(self, nc, es, Sc, PSP, cst, heads, kinds, W, hd_d, scr, nhs=3):
        self.nc, self.Sc, self.PSP, self.cst = nc, Sc, PSP, cst
        self.heads, self.kinds, self.W, self.hd_d, self.scr = heads, kinds, W, hd_d, scr
        self.wt_r = Rot(nc, es, "wt", [128, NKC, 512], BF16, 2)
        self.hs_r = Rot(nc, es, "hs", [128, NKC, 512], BF16, nhs)
        self.nhs = nhs
        self.st_r = Rot(nc, es, "st", [128, 4, 512], BF16, 2)
        if any(k == "dil" for k in kinds.values()):
            self.sq_r = Rot(nc, es, "sq", [128, 512], BF16, 2)
            self.rs_r = Rot(nc, es, "rs", [128, 512], F32, 2)
        self.eg_r = Rot(nc, es, "eg", [128, 512], F32, 2)
        self.pref = "dve"
        self.done = set()
        self.wts = {}
        self.hss = {}
        self.gen = self._run()
        self.exhausted = False

    def step(self, n=1):
        for _ in range(n):
            if self.exhausted:
                return
            try:
                next(self.gen)
            except StopIteration:
                self.exhausted = True

    def finish_head(self, hd):
        while hd not in self.done and not self.exhausted:
            self.step()

    def _load_w(self, hd):
        if hd in self.wts or hd not in self.kinds:
            return
        wt, wtn = self.wt_r.next()
        for g4 in range(4):
            self.Sc.dma("pool", wt[:, g4 * 4:(g4 + 1) * 4, :], self.W[hd, :, g4 * 4:(g4 + 1) * 4, :],
                        writes=[(wtn, g4)])
        self.wts[hd] = (wt, wtn)

    def _load_hs(self, i):
        if i in self.hss or i >= 8 * len(self.heads):
            return
        hs, hsn = self.hs_r.next()
        tt = i % 8
        for hf in range(2):
            self.Sc.dma("sp", hs[:, hf * 8:(hf + 1) * 8, :], self.hd_d[tt, :, hf * 8:(hf + 1) * 8, :],
                        writes=[(hsn, hf)])
        self.hss[i] = (hs, hsn)

    def _run(self):
        Sc, cst = self.Sc, self.cst
        self._load_w(self.heads[0])
        for j in range(self.nhs - 1):
            self._load_hs(j)
        for hi, hd in enumerate(self.heads):
            kind = self.kinds[hd]
            wt, wtn = self.wts[hd]
            if hi + 1 < len(self.heads):
                self._load_w(self.heads[hi + 1])
            for tt in range(NTT):
                i = hi * 8 + tt
                hs, hsn = self.hss.pop(i)
                self._load_hs(i + self.nhs - 1)
                st, stn = self.st_r.next()
                for c in range(4):
                    ps, psn = self.PSP.next()
                    for kc in range(NKC):
                        Sc.op("pe", lambda e: e.matmul(ps[:], wt[:, kc, c * 128:(c + 1) * 128], hs[:, kc, :],
                                                       start=(kc == 0), stop=(kc == NKC - 1)),
                              reads=[(wtn, kc // 4), (hsn, kc // 8)], writes=[psn])
                        if kc % 4 == 3 and kc != NKC - 1:
                            yield
                    self._evac(kind, c, ps, psn, st, stn)
                    yield
                dst = self.scr[hd, :, :, tt * 512:(tt + 1) * 512].rearrange("c p t -> p c t")
                Sc.dma("sp", dst, st[:], reads=[stn], writes=[("scr", hd)])
            self.done.add(hd)
            yield

    def _evac(self, kind, c, ps, psn, st, stn):
        Sc, cst = self.Sc, self.cst
        act_heavy = self.pref == "act"

        def cp(dst, key):
            if act_heavy:
                Sc.op("act", lambda e: e.copy(dst, ps[:]), reads=[psn], writes=[key])
            else:
                Sc.op("dve", lambda e: e.tensor_copy(dst, ps[:]), reads=[psn], writes=[key])

        if c == 3:
            eg, egn = self.eg_r.next()
            Sc.op("act", lambda e: e.activation(eg[:], ps[:], AF.Exp, scale=-1.0), reads=[psn], writes=[egn])
            if act_heavy:
                Sc.op("act", lambda e: e.activation(eg[:], eg[:], AF.Ln, bias=1.0), reads=[egn], writes=[egn])
                Sc.op("act", lambda e: e.activation(eg[:], eg[:], AF.Exp, scale=-1.0), reads=[egn], writes=[egn])
            else:
                Sc.op("dve", lambda e: e.tensor_scalar(eg[:], eg[:], 1.0, None, ALU.add), reads=[egn], writes=[egn])
                Sc.op("dve", lambda e: e.reciprocal(eg[:], eg[:]), reads=[egn], writes=[egn])
            Sc.op("dve", lambda e: e.tensor_tensor(st[:, 3, :], ps[:], eg[:], ALU.mult),
                  reads=[psn, egn], writes=[(stn, 3)])
        elif c == 2:
            cp(st[:, 2, :], (stn, 2))
        elif kind == "dil":
            gcol = cst["qg"] if c == 0 else cst["kg"]
            sq, sqn = self.sq_r.next()
            Sc.op("act", lambda e: e.activation(sq[:], ps[:], AF.Square), reads=[psn], writes=[sqn])
            p2, p2n = self.PSP.next()
            Sc.op("pe", lambda e: e.matmul(p2[:], cst["ones"], sq[:], start=True, stop=True),
                  reads=[sqn, "consts"], writes=[p2n])
            rs, rsn = self.rs_r.next()
            Sc.op("act", lambda e: e.activation(rs[:], p2[:], AF.Ln, bias=128.0 * EPS), reads=[p2n], writes=[rsn])
            Sc.op("act", lambda e: e.activation(rs[:], rs[:], AF.Exp, scale=-0.5), reads=[rsn], writes=[rsn])
            Sc.op("dve", lambda e: e.scalar_tensor_tensor(st[:, c, :], ps[:], gcol, rs[:], ALU.mult, ALU.mult),
                  reads=[psn, rsn, "gcols"], writes=[(stn, c)])
        elif c == 0:
            if kind == "sb":
                sc = 1.0 / math.sqrt(128.0)
                Sc.op("dve", lambda e: e.tensor_scalar(st[:, 0, :], ps[:], sc, None, ALU.mult),
                      reads=[psn], writes=[(stn, 0)])
            else:
                cp(st[:, 0, :], (stn, 0))
        else:
            cp(st[:, 1, :], (stn, 1))


def load_head_tracked(nc, Sc, hb, hbn, scr, hd):
    for c in range(4):
        Sc.dma("sp" if c % 2 == 0 else "pool", hb[:, c, :], scr[hd, c, :, :], reads=[("scr", hd)], writes=[(hbn, c)])


def mixer_stream(nc, es, Sc, PSA, layer, x, g, W, extra, hd_d, scr, out, nheads, c, is_output=False):
    PS, PSO, PSP = PSA.sub([0, 1, 2]), PSA.sub([3, 4]), PSA.sub([5, 6, 7])
    bank_split = {"sb": ([0, 1, 2, 3], [4, 5], [6, 7]),
                  "dil": ([0, 1, 2, 3], [4], [4, 5, 6, 7]),
                  "hgrn": ([0, 1, 2], [3, 4], [5, 6, 7])}
    heads = list(range(nheads))
    L = "L%d" % layer
    if layer == 0:
        gct = es.enter_context(_sbt(nc, L + "gct", [128, 2], F32))
        Sc.dma("sp", gct[:], extra, writes=["gcols"])
        c["qg"], c["kg"] = gct[:, 0:1], gct[:, 1:2]
        kinds = {hd: ("sb" if hd < nheads // 2 else "dil") for hd in heads}
    else:
        cst2, lbl, ogd = extra
        kinds = {hd: "hgrn" for hd in heads}
        c2 = es.enter_context(_sbt(nc, "c2", [128, 4608], BF16))
        Sc.dma("pool", c2[:], cst2, writes=["consts2"])
        hct = es.enter_context(_sbt(nc, "hct", [128, 2, nheads], F32))
        lbt = es.enter_context(_sbt(nc, "lbt", [128, nheads], F32))
        omlt = es.enter_context(_sbt(nc, "omlt", [128, nheads], F32))
        ogt = es.enter_context(_sbt(nc, "ogt", [128, 1], F32))
        Sc.dma("sp", hct[:], lbl, writes=["hct"])
        Sc.dma("sp", ogt[:], ogd, writes=["ogt"])
        Sc.op("dve", lambda e: e.tensor_tensor(omlt[:], hct[:, 0, :], hct[:, 1, :], ALU.subtract),
              reads=["hct"], writes=["omlt"])
        Sc.op("act", lambda e: e.activation(omlt[:], omlt[:], AF.Exp), reads=["omlt"], writes=["omlt"])
        Sc.op("dve", lambda e: e.tensor_scalar(lbt[:], omlt[:], 1.0, None, ALU.add), reads=["omlt"], writes=["lbt"])
        Sc.op("dve", lambda e: e.reciprocal(lbt[:], lbt[:]), reads=["lbt"], writes=["lbt"])
        Sc.op("dve", lambda e: e.tensor_tensor(omlt[:], omlt[:], lbt[:], ALU.mult),
              reads=["omlt", "lbt"], writes=["hconst"])
    phase1_stream(nc, Sc, PSA, x, g, hd_d, c)
    with ExitStack() as es3:
        P = ProjStream(nc, es3, Sc, PSP, c, heads, kinds, W, hd_d, scr, nhs=(3 if layer == 0 else 2))
        fill = P.step
        if layer == 0:
            groups = [[h for h in heads if kinds[h] == "sb"], [h for h in heads if kinds[h] == "dil"]]
        else:
            groups = [heads]
        for grp in groups:
            kind = kinds[grp[0]]
            P.pref = "act" if kind == "hgrn" else "dve"
            bs = bank_split[kind]
            PS, PSO = PSA.sub(bs[0]), PSA.sub(bs[1])
            P.PSP = PSA.sub(bs[2])
            with ExitStack() as es4:
                R = {"o": Rot(nc, es4, "o", [128, 512], BF16, 2)}
                nhb = 2 if kind == "sb" else 1
                hbs = [es4.enter_context(_sbt(nc, "hb%d" % i, [128, 4, S], BF16)) for i in range(nhb)]
                hbn = ["hb%d" % i for i in range(nhb)]
                if kind == "sb":
                    vtok0 = es4.enter_context(_sbt(nc, "vtok0", [128, 32, 128], BF16))
                    R.update({"e": Rot(nc, es4, "e", [128, 512], F32, 2), "sp": Rot(nc, es4, "sp", [128, 512], BF16, 5),
                              "a": Rot(nc, es4, "a", [128, 512], BF16, 4),
                              "carry": Rot(nc, es4, "carry", [128, 512], BF16, 4)})
                elif kind == "dil":
                    vtoks = [es4.enter_context(_sbt(nc, "vtok%d" % i, [128, 32, 128], BF16)) for i in range(3)]
                    vtokn = ["vtok0", "vtok1", "vtok2"]
                    UZ = es4.enter_context(_sbt(nc, "UZ", [128, 2, S], F32))
                    R.update({"p": Rot(nc, es4, "p", [128, 256], BF16, 4), "rz": Rot(nc, es4, "rz", [128, 512], F32, 2)})
                else:
                    T = {}
                    for nm in ("E", "B2", "CUM"):
                        T[nm] = es4.enter_context(_sbt(nc, nm, [128, S // 2], F32))
                    for nm in ("KT", "QT"):
                        T[nm] = es4.enter_context(_sbt(nc, nm, [128, S], BF16))
                    for nm in ("AM", "ST"):
                        T[nm] = es4.enter_context(_sbt(nc, nm, [128, 32, 128], BF16))
                    T["EL"] = es4.enter_context(_sbt(nc, "EL", [128, 32], F32))
                    T["ktok"] = es4.enter_context(_sbt(nc, "ktok", [128, 32, 128], BF16))
                    T["vtok"] = es4.enter_context(_sbt(nc, "vtok", [128, 32, 128], BF16))
                    T["rmask"], T["mask4"] = c2[:, 0:4096], c2[:, 4096:4608]
                    R.update({"sq": Rot(nc, es4, "sq3", [128, 512], BF16, 2), "rs": Rot(nc, es4, "rs3", [128, 512], F32, 2)})
                P.finish_head(grp[0])
                load_head_tracked(nc, Sc, hbs[0], hbn[0], scr, grp[0])
                if kind == "dil":
                    P.step(40)
                for i, hd in enumerate(grp):
                    hb, hn = hbs[i % nhb], hbn[i % nhb]
                    if kind == "sb":
                        make_vtok(nc, Sc, PS, c, hb, hn, vtok0, "vtok0", 1)
                        sb_head_v2(nc, Sc, PS, PSO, c, R, hb, hn, vtok0, "vtok0", out[hd], is_output, fill=fill)
                    elif kind == "dil":
                        dil_head_v2(nc, Sc, PS, c, R, hb, hn, vtoks, vtokn, UZ, out[hd], is_output, fill=fill)
                    else:
                        hgrn_head_v2(nc, Sc, PS, PSO, c, R, T, hb, hn, lbt[:, hd:hd + 1], omlt[:, hd:hd + 1],
                                     ogt[:, 0:1], out[hd], is_output, fill=fill)
                    if i + 1 < len(grp):
                        P.finish_head(grp[i + 1])
                        load_head_tracked(nc, Sc, hbs[(i + 1) % nhb], hbn[(i + 1) % nhb], scr, grp[i + 1])
                        if kind == "dil":
                            P.step(40)
                Sc.barrier()


def build_outproj():
    nc = bass.Bass("TRN2", target_bir_lowering=False)
    mixT = nc.dram_tensor("mixT", [32, 128, 2048], BF16, kind="ExternalInput").ap()
    w = nc.dram_tensor("w", [128, 32, 2048], F32, kind="ExternalInput").ap()
    x = nc.dram_tensor("x", [2048, 2048], F32, kind="ExternalInput").ap()
    y = nc.dram_tensor("y", [2048, 2048], F32, kind="ExternalOutput").ap()
    with ExitStack() as es:
        Sc = Sched(nc, es)
        PS = PsumPool(nc, es)
        outproj_body(nc, es, Sc, PS, mixT, w, x, y, 2048)
        Sc.finish()
    return nc


def outproj_body(nc, es_unused, Sc, PS, mixT, w, x, y, ntok, is_output=True):
    with ExitStack() as es:
        wsb = es.enter_context(_sbt(nc, "wsb", [128, 32, 2048], BF16))
        for g in range(8):
            Sc.dma("pool", wsb[:, g * 4:(g + 1) * 4, :], w[:, g * 4:(g + 1) * 4, :], writes=[("wsb", g)])
        mx_r = Rot(nc, es, "mx", [128, 32, 256], BF16, 2)
        xt_r = Rot(nc, es, "xo", [128, 2048], F32, 2)
        for t5 in range(ntok // 256):
            mx, mxn = mx_r.next()
            for g in range(4):
                src = mixT[g * 8:(g + 1) * 8, :, t5 * 256:(t5 + 1) * 256].rearrange("c p t -> p c t")
                Sc.dma("sp", mx[:, g * 8:(g + 1) * 8, :], src, writes=[(mxn, g)])
            for ti in range(2):
                tt = t5 * 2 + ti
                xt, xtn = xt_r.next()
                Sc.dma("sp", xt[:], x[tt * 128:(tt + 1) * 128, :], writes=[xtn])
                for n in range(4):
                    ps, psn = PS.next()
                    for kc in range(32):
                        Sc.op("pe", lambda e: e.matmul(ps[:], mx[:, kc, ti * 128:(ti + 1) * 128],
                                                       wsb[:, kc, n * 512:(n + 1) * 512],
                                                       start=(kc == 0), stop=(kc == 31)),
                              reads=[(mxn, kc // 8), ("wsb", kc // 4)], writes=[psn])
                    Sc.op("dve", lambda e: e.tensor_tensor(xt[:, n * 512:(n + 1) * 512], ps[:],
                                                           xt[:, n * 512:(n + 1) * 512], ALU.add),
                          reads=[psn, xtn], writes=[(xtn, n)])
                Sc.dma("sp", y[tt * 128:(tt + 1) * 128, :], xt[:], reads=[xtn], is_output=is_output)
        Sc.barrier()


def build_mixer(layer, nheads=16):
    nc = bass.Bass("TRN2", target_bir_lowering=False)
    x = nc.dram_tensor("x", [S, D], F32, kind="ExternalInput").ap()
    g = nc.dram_tensor("g", [128, D], F32, kind="ExternalInput").ap()
    W = nc.dram_tensor("W", [nheads, 128, NKC, 512], F32, kind="ExternalInput").ap()
    cst = nc.dram_tensor("cst", [128, 2816], F32, kind="ExternalInput").ap()
    if layer == 0:
        gc = nc.dram_tensor("gc", [128, 2], F32, kind="ExternalInput").ap()
    else:
        cst2 = nc.dram_tensor("cst2", [128, 4608], F32, kind="ExternalInput").ap()
        lbl = nc.dram_tensor("lbl", [128, 2, nheads], F32, kind="ExternalInput").ap()
        ogd = nc.dram_tensor("og", [128, 1], F32, kind="ExternalInput").ap()
    scr = nc.dram_tensor("scr", [nheads, 4, 128, S], BF16, kind="Internal").ap()
    out = nc.dram_tensor("mixT", [nheads, 128, S], BF16, kind="ExternalOutput").ap()
    with ExitStack() as es:
        Sc = Sched(nc, es)
        PSA = PsumPool(nc, es)
        mixer_body(nc, es, Sc, PSA, layer, x, g, W, cst, gc if layer == 0 else (cst2, lbl, ogd), scr, out, nheads)
        Sc.finish()
    return nc


def mixer_body(nc, es, Sc, PSA, layer, x, g, W, cst, extra, scr, out, nheads=16, c=None, is_output=True):
    PS, PSO = PSA.sub([0, 1, 2, 3, 4, 5]), PSA.sub([6, 7])
    if c is None:
        c = load_consts(nc, es, Sc, cst)
    heads = list(range(nheads))
    L = "L%d" % layer
    if layer == 0:
        gct = es.enter_context(_sbt(nc, L + "gct", [128, 2], F32))
        Sc.dma("sp", gct[:], extra, writes=["gcols"])
        c["qg"], c["kg"] = gct[:, 0:1], gct[:, 1:2]
        kinds = {hd: ("sb" if hd < nheads // 2 else "dil") for hd in heads}
    else:
        cst2, lbl, ogd = extra
        kinds = {hd: "hgrn" for hd in heads}
        c2 = es.enter_context(_sbt(nc, "c2", [128, 4608], BF16))
        Sc.dma("pool", c2[:], cst2, writes=["consts2"])
        hct = es.enter_context(_sbt(nc, "hct", [128, 2, nheads], F32))
        lbt = es.enter_context(_sbt(nc, "lbt", [128, nheads], F32))
        omlt = es.enter_context(_sbt(nc, "omlt", [128, nheads], F32))
        ogt = es.enter_context(_sbt(nc, "ogt", [128, 1], F32))
        Sc.dma("sp", hct[:], lbl, writes=["hct"])
        Sc.dma("sp", ogt[:], ogd, writes=["ogt"])
        Sc.op("dve", lambda e: e.tensor_tensor(omlt[:], hct[:, 0, :], hct[:, 1, :], ALU.subtract),
              reads=["hct"], writes=["omlt"])
        Sc.op("act", lambda e: e.activation(omlt[:], omlt[:], AF.Exp), reads=["omlt"], writes=["omlt"])
        Sc.op("dve", lambda e: e.tensor_scalar(lbt[:], omlt[:], 1.0, None, ALU.add), reads=["omlt"], writes=["lbt"])
        Sc.op("dve", lambda e: e.reciprocal(lbt[:], lbt[:]), reads=["lbt"], writes=["lbt"])
        Sc.op("dve", lambda e: e.tensor_tensor(omlt[:], omlt[:], lbt[:], ALU.mult),
              reads=["omlt", "lbt"], writes=["hconst"])
    phase12(nc, Sc, PSA, x, g, W, scr, heads, kinds, c)
    es = ExitStack()
    es.__enter__()
    hbs = [es.enter_context(_sbt(nc, "hb%d" % i, [128, 4, S], BF16)) for i in range(2)]
    hbn = ["hb0", "hb1"]
    R = {"o": Rot(nc, es, "o", [128, 512], BF16, 2)}
    if layer == 0:
        vtoks = [es.enter_context(_sbt(nc, "vtok%d" % i, [128, 32, 128], BF16)) for i in range(3)]
        vtokn = ["vtok0", "vtok1", "vtok2"]
        UZ = es.enter_context(_sbt(nc, "UZ", [128, 2, S], F32))
        R.update({"e": Rot(nc, es, "e", [128, 512], F32, 3), "sp": Rot(nc, es, "sp", [128, 512], BF16, 5),
                  "a": Rot(nc, es, "a", [128, 512], BF16, 4), "carry": Rot(nc, es, "carry", [128, 512], BF16, 4),
                  "p": Rot(nc, es, "p", [128, 256], BF16, 4), "rz": Rot(nc, es, "rz", [128, 512], F32, 2)})
    else:
        T = {}
        for nm in ("E", "B2", "CUM"):
            T[nm] = es.enter_context(_sbt(nc, nm, [128, S], F32))
        for nm in ("KT", "QT"):
            T[nm] = es.enter_context(_sbt(nc, nm, [128, S], BF16))
        for nm in ("AM", "PSE", "ST"):
            T[nm] = es.enter_context(_sbt(nc, nm, [128, 32, 128], BF16))
        T["EL"] = es.enter_context(_sbt(nc, "EL", [128, 32], F32))
        T["ktok"] = es.enter_context(_sbt(nc, "ktok", [128, 32, 128], BF16))
        T["vtok"] = es.enter_context(_sbt(nc, "vtok", [128, 32, 128], BF16))
        T["rmask"], T["mask4"] = c2[:, 0:4096], c2[:, 4096:4608]
        R.update({"am": Rot(nc, es, "am", [128, 128], BF16, 3), "state": Rot(nc, es, "state", [128, 128], BF16, 3),
                  "pse": Rot(nc, es, "pse", [128, 128], F32, 2), "sq": Rot(nc, es, "sq3", [128, 512], BF16, 2),
                  "rs": Rot(nc, es, "rs3", [128, 512], F32, 2)})
    load_head(nc, Sc, hbs[0], hbn[0], scr, heads[0])
    for i, hd in enumerate(heads):
        hb, hn = hbs[i % 2], hbn[i % 2]
        if i + 1 < len(heads):
            load_head(nc, Sc, hbs[(i + 1) % 2], hbn[(i + 1) % 2], scr, heads[i + 1])
        if kinds[hd] == "sb":
            make_vtok(nc, Sc, PS, c, hb, hn, vtoks[0], vtokn[0], 1)
            sb_head_v2(nc, Sc, PS, PSO, c, R, hb, hn, vtoks[0], vtokn[0], out[hd], is_output)
        elif kinds[hd] == "dil":
            dil_head_v2(nc, Sc, PS, c, R, hb, hn, vtoks, vtokn, UZ, out[hd], is_output)
        else:
            hgrn_head_v2(nc, Sc, PS, PSO, c, R, T, hb, hn, lbt[:, hd:hd + 1], omlt[:, hd:hd + 1], ogt[:, 0:1], out[hd], is_output)
    Sc.barrier()
    es.close()


def build_fused():
    nc = bass.Bass("TRN2", target_bir_lowering=False)
    x = nc.dram_tensor("x", [S, D], F32, kind="ExternalInput").ap()
    g0 = nc.dram_tensor("g0", [128, D], F32, kind="ExternalInput").ap()
    g1 = nc.dram_tensor("g1", [128, D], F32, kind="ExternalInput").ap()
    W0 = nc.dram_tensor("W0", [32, 128, NKC, 512], F32, kind="ExternalInput").ap()
    W1 = nc.dram_tensor("W1", [32, 128, NKC, 512], F32, kind="ExternalInput").ap()
    wo0 = nc.dram_tensor("wo0", [128, 32, 2048], F32, kind="ExternalInput").ap()
    wo1 = nc.dram_tensor("wo1", [128, 32, 2048], F32, kind="ExternalInput").ap()
    cst = nc.dram_tensor("cst", [128, 2816], F32, kind="ExternalInput").ap()
    gc = nc.dram_tensor("gc", [128, 2], F32, kind="ExternalInput").ap()
    cst2 = nc.dram_tensor("cst2", [128, 4608], F32, kind="ExternalInput").ap()
    lbl = nc.dram_tensor("lbl", [128, 2, 32], F32, kind="ExternalInput").ap()
    ogd = nc.dram_tensor("og", [128, 1], F32, kind="ExternalInput").ap()
    scr = nc.dram_tensor("scr", [32, 4, 128, S], BF16, kind="Internal").ap()
    hd_d = nc.dram_tensor("hdn_d", [8, 128, NKC, 512], BF16, kind="Internal").ap()
    mixT = nc.dram_tensor("mixT", [32, 128, S], BF16, kind="Internal").ap()
    x1 = nc.dram_tensor("x1", [S, D], F32, kind="Internal").ap()
    y = nc.dram_tensor("y", [S, D], F32, kind="ExternalOutput").ap()
    with ExitStack() as es:
        Sc = Sched(nc, es)
        PSA = PsumPool(nc, es)
        c = load_consts(nc, es, Sc, cst)
        mixer_stream(nc, es, Sc, PSA, 0, x, g0, W0, gc, hd_d, scr, mixT, 32, c)
        outproj_body(nc, None, Sc, PSA, mixT, wo0, x, x1, S, is_output=False)
        mixer_stream(nc, es, Sc, PSA, 1, x1, g1, W1, (cst2, lbl, ogd), hd_d, scr, mixT, 32, c)
        outproj_body(nc, None, Sc, PSA, mixT, wo1, x1, y, S, is_output=True)
        Sc.finish()
        print("instr counts", Sc.cnt, "dma", Sc.dcnt, "waits", Sc.nwaits)
    return nc


N_CORES = 4
ACTIVE = (0, 1, 2, 3)


def _prep_w(w_in, cols):
    return np.ascontiguousarray(w_in[:, cols].reshape(NKC, 128, 512).transpose(1, 0, 2))


def _cols_l0():
    out = []
    for hd in range(32):
        if hd < 16:
            h, base = hd, (0, 2048, 4096, 12288)
        else:
            h, base = hd - 16, (6144, 8192, 10240, 12288 + 2048)
        out.append(np.concatenate([b + np.arange(h * 128, (h + 1) * 128) for b in base]))
    return out


def _cols_l1():
    return [np.concatenate([b + np.arange(h * 128, (h + 1) * 128) for b in (0, 4096, 8192, 12288)])
            for h in range(32)]


def _make_cst2():
    c2 = np.zeros((128, 4608), np.float32)
    c2[:, :4096] = (np.arange(4096) % 128 != 0)[None, :]
    c2[:, 4096:] = np.tile(np.arange(128)[:, None] <= np.arange(128)[None, :], (1, 4))
    return c2


def kernel(x, norm_even, w_in_even, q_norm_even, k_norm_even, w_out_even,
           norm_odd, w_in_odd, lb_logits, o_norm_odd, w_out_odd):
    f32 = np.float32
    x = np.asarray(x, f32)
    w0 = np.asarray(w_in_even[0], f32)
    w1 = np.asarray(w_in_odd[0], f32)
    shared = {
        "g0": np.ascontiguousarray(np.broadcast_to(np.asarray(norm_even[0], f32), (128, D))),
        "g1": np.ascontiguousarray(np.broadcast_to(np.asarray(norm_odd[0], f32), (128, D))),
        "W0": np.stack([_prep_w(w0, cl) for cl in _cols_l0()]),
        "W1": np.stack([_prep_w(w1, cl) for cl in _cols_l1()]),
        "wo0": np.ascontiguousarray(np.asarray(w_out_even[0], f32).reshape(32, 128, 2048).transpose(1, 0, 2)),
        "wo1": np.ascontiguousarray(np.asarray(w_out_odd[0], f32).reshape(32, 128, 2048).transpose(1, 0, 2)),
        "cst": make_consts_np(),
        "gc": np.ascontiguousarray(np.stack([np.asarray(q_norm_even[0], f32), np.asarray(k_norm_even[0], f32)], 1)),
        "cst2": _make_cst2(),
        "lbl": np.ascontiguousarray(np.asarray(lb_logits, f32).reshape(2, 32, 128).transpose(2, 0, 1)),
        "og": np.ascontiguousarray(np.asarray(o_norm_odd[0], f32).reshape(128, 1)),
    }
    big = ("W0", "W1", "wo0", "wo1")
    idle = dict(shared, x=np.zeros((S, D), f32))
    for k in big:
        idle[k] = np.zeros_like(shared[k])
    in_maps = []
    for core in range(N_CORES):
        if core in ACTIVE:
            in_maps.append(dict(shared, x=np.ascontiguousarray(x[ACTIVE.index(core)])))
        else:
            in_maps.append(idle)
    nc = build_fused()
    res = run_bass_kernel_spmd(nc, in_maps, core_ids=list(range(N_CORES)))
    return np.stack([res.results[core]["y"] for core in ACTIVE])
```

```python
import math
import numpy as np
from contextlib import ExitStack
import concourse.bass as bass
import concourse.mybir as mybir
from concourse.bass_utils import run_bass_kernel_spmd


F32 = mybir.dt.float32
BF16 = mybir.dt.bfloat16
AF = mybir.ActivationFunctionType
ALU = mybir.AluOpType
AX = mybir.AxisListType

NDMA_SLOTS = 8
_UNIQ = [0]


def _sbt(nc, name, shape, dtype):
    _UNIQ[0] += 1
    return nc.sbuf_tensor("%s_u%d" % (name, _UNIQ[0]), shape, dtype)


class Sched:
    def __init__(self, nc, es):
        self.nc = nc
        self.eng = {"pe": nc.tensor, "act": nc.scalar, "dve": nc.vector,
                    "pool": nc.gpsimd, "sp": nc.sync}
        self.sem = {e: es.enter_context(nc.semaphore("s_" + e)) for e in self.eng}
        self.cnt = {e: 0 for e in self.eng}
        self.dq = {"sp": "sp", "pool": "pool", "act": "act"}
        self.dsem = {q: [es.enter_context(nc.semaphore("d_%s%d" % (q, i)))
                         for i in range(NDMA_SLOTS)] for q in self.dq}
        self.dcnt = {q: 0 for q in self.dq}
        self.waited = {e: {} for e in self.eng}
        self.state = {}
        self.out_tokens = []
        self.nwaits = 0

    def _need(self, e, tok):
        if tok is None:
            return
        sem, val, key = tok
        if e == "pe" and key == "pe":
            return
        w = self.waited[e]
        if w.get(key, 0) >= val:
            return
        w[key] = val
        self.eng[e].wait_ge(sem, val)
        self.nwaits += 1

    def _st(self, root):
        s = self.state.get(root)
        if s is None:
            s = {"w": None, "r": [], "subs": {}}
            self.state[root] = s
        return s

    @staticmethod
    def _split(res):
        if isinstance(res, tuple):
            return res[0], res[1]
        return res, None

    def _deps(self, e, reads, writes):
        for res in reads:
            root, sub = self._split(res)
            s = self._st(root)
            self._need(e, s["w"])
            if sub is None:
                for ss in s["subs"].values():
                    self._need(e, ss["w"])
            else:
                ss = s["subs"].get(sub)
                if ss:
                    self._need(e, ss["w"])
        for res in writes:
            root, sub = self._split(res)
            s = self._st(root)
            self._need(e, s["w"])
            for t in s["r"]:
                self._need(e, t)
            if sub is None:
                for ss in s["subs"].values():
                    self._need(e, ss["w"])
                    for t in ss["r"]:
                        self._need(e, t)
            else:
                ss = s["subs"].get(sub)
                if ss:
                    self._need(e, ss["w"])
                    for t in ss["r"]:
                        self._need(e, t)

    def _record(self, tok, reads, writes):
        for res in reads:
            root, sub = self._split(res)
            s = self._st(root)
            if sub is None:
                s["r"] = [t for t in s["r"] if t[2] != tok[2]] + [tok]
            else:
                ss = s["subs"].setdefault(sub, {"w": None, "r": []})
                ss["r"] = [t for t in ss["r"] if t[2] != tok[2]] + [tok]
        for res in writes:
            root, sub = self._split(res)
            s = self._st(root)
            if sub is None:
                s["w"] = tok
                s["r"] = []
                s["subs"] = {}
            else:
                s["subs"][sub] = {"w": tok, "r": []}

    def op(self, e, fn, reads=(), writes=()):
        self._deps(e, reads, writes)
        ins = fn(self.eng[e])
        self.cnt[e] += 1
        ins.then_inc(self.sem[e], 1)
        tok = (self.sem[e], self.cnt[e], e)
        self._record(tok, reads, writes)
        return tok

    def dma(self, q, out, in_, reads=(), writes=(), is_output=False, **kw):
        e = self.dq[q]
        i = self.dcnt[q]
        slot = i % NDMA_SLOTS
        rnd = i // NDMA_SLOTS
        key = "d_%s%d" % (q, slot)
        sem = self.dsem[q][slot]
        if rnd > 0:
            self._need(e, (sem, 16 * rnd, key))
        self._deps(e, reads, writes)
        ins = self.eng[e].dma_start(out=out, in_=in_, **kw)
        ins.then_inc(sem, 16)
        self.dcnt[q] += 1
        tok = (sem, 16 * (rnd + 1), key)
        self._record(tok, reads, writes)
        if is_output:
            self.out_tokens.append(tok)
        return tok

    def collective(self, kind, ins, outs, groups, reads=(), writes=()):
        q, e = "pool", "pool"
        i = self.dcnt[q]
        slot, rnd = i % NDMA_SLOTS, i // NDMA_SLOTS
        key = "d_%s%d" % (q, slot)
        sem = self.dsem[q][slot]
        if rnd > 0:
            self._need(e, (sem, 16 * rnd, key))
        self._deps(e, reads, writes)
        ins_ = self.eng[e].collective_compute(kind, ALU.bypass, replica_groups=groups, ins=ins, outs=outs)
        ins_.then_inc(sem, 16)
        self.dcnt[q] += 1
        tok = (sem, 16 * (rnd + 1), key)
        self._record(tok, reads, writes)
        return tok

    def barrier(self):
        toks = [(self.sem[e], self.cnt[e], e) for e in self.eng if self.cnt[e] > 0]
        for q in self.dq:
            n = self.dcnt[q]
            for slot in range(NDMA_SLOTS):
                if n > slot:
                    rounds = (n - 1 - slot) // NDMA_SLOTS + 1
                    toks.append((self.dsem[q][slot], 16 * rounds, "d_%s%d" % (q, slot)))
        for e in self.eng:
            for t in toks:
                if not (t[2] == e):
                    self._need(e, t)
                elif e != "pe":
                    self._need(e, t)

    def finish(self):
        for t in self.out_tokens:
            self._need("sp", t)
        self.barrier()


S = 4096
D = 2048
NKC = D // 128
NTT = S // 512
EPS = 1e-6
MASKV = -200.0


class PsumPool:
    def __init__(self, nc, es, n=8, tiles=None, names=None):
        if tiles is None:
            self.t = [es.enter_context(nc.psum_tensor("ps%d" % i, [128, 512], F32)) for i in range(n)]
            self.names = ["ps%d" % i for i in range(n)]
        else:
            self.t, self.names = tiles, names
        self.i = 0

    def sub(self, idxs):
        return PsumPool(None, None, tiles=[self.t[i] for i in idxs], names=[self.names[i] for i in idxs])

    def next(self):
        k = self.i % len(self.t)
        self.i += 1
        return self.t[k], self.names[k]


class Rot:
    def __init__(self, nc, es, name, shape, dtype, n):
        self.t = [es.enter_context(_sbt(nc, "%s%d" % (name, i), shape, dtype)) for i in range(n)]
        self.names = ["%s%d" % (name, i) for i in range(n)]
        self.i = 0

    def next(self):
        k = self.i % len(self.t)
        self.i += 1
        return self.t[k], self.names[k]


def load_consts(nc, es, Sc, cst):
    ncols = 128 * 4 + 2048 + 256
    ct = es.enter_context(_sbt(nc, "consts", [128, ncols], BF16))
    Sc.dma("pool", ct[:], cst, writes=["consts"])
    c = {}
    c["ident"] = ct[:, 0:128]
    c["negtri"] = ct[:, 128:256]
    c["negones"] = ct[:, 256:384]
    c["ones"] = ct[:, 384:512]
    c["sbmask"] = [ct[:, 512 + m * 512: 512 + (m + 1) * 512] for m in range(4)]
    c["dmask"] = ct[:, 2560:2816]
    return c


def make_consts_np():
    ncols = 128 * 4 + 2048 + 256
    c = np.zeros((128, ncols), np.float32)
    j = np.arange(128)[:, None]
    s = np.arange(128)[None, :]
    c[:, 0:128] = np.eye(128)
    c[:, 128:256] = -1.0 * (j >= s)
    c[:, 256:384] = -1.0
    c[:, 384:512] = 1.0
    col = np.arange(512)[None, :]
    for m in range(4):
        valid = col > (m * 128 + j)
        c[:, 512 + m * 512: 512 + (m + 1) * 512] = np.where(valid, 0.0, MASKV)
    c[:, 2560:2688] = np.where(j <= s, 0.0, MASKV)
    c[:, 2688:2816] = np.where(j >= s, 0.0, MASKV)
    return c


def phase12(nc, Sc, PS, x, gain_bc, W, scr, heads, head_kinds, cst_extra, ntt_tok=32):
    with ExitStack() as es:
        hdnT = es.enter_context(_sbt(nc, "hdnT", [128, NKC, S], BF16))
        with ExitStack() as e1:
            gbc = e1.enter_context(_sbt(nc, "gbc", [128, D], F32))
            ss = e1.enter_context(_sbt(nc, "ss", [128, 32], F32))
            rstd = e1.enter_context(_sbt(nc, "rstd", [128, 32], F32))
            junk = e1.enter_context(_sbt(nc, "junk", [128, D], BF16))
            xt_r = Rot(nc, e1, "xt", [128, D], F32, 2)
            xs_r = Rot(nc, e1, "xs", [128, D], BF16, 2)
            Sc.dma("sp", gbc[:], gain_bc, writes=["gbc"])
            Sc.op("dve", lambda e: e.memset(ss[:], 0.0), writes=["ss"])
            for tt in range(ntt_tok):
                xt, xtn = xt_r.next()
                Sc.dma("sp", xt[:], x[tt * 128:(tt + 1) * 128, :], writes=[xtn])
                Sc.op("act", lambda e: e.activation(junk[:], xt[:], AF.Square, accum_out=ss[:, tt:tt + 1]),
                      reads=[xtn], writes=[("ss", tt)])
                Sc.op("act", lambda e: e.activation(rstd[:, tt:tt + 1], ss[:, tt:tt + 1], AF.Ln,
                                                    scale=1.0 / D, bias=EPS),
                      reads=[("ss", tt)], writes=[("rstd", tt)])
                Sc.op("act", lambda e: e.activation(rstd[:, tt:tt + 1], rstd[:, tt:tt + 1], AF.Exp, scale=-0.5),
                      reads=[("rstd", tt)], writes=[("rstd", tt)])
                xs, xsn = xs_r.next()
                Sc.op("dve", lambda e: e.scalar_tensor_tensor(xs[:], xt[:], rstd[:, tt:tt + 1], gbc[:],
                                                              ALU.mult, ALU.mult),
                      reads=[xtn, ("rstd", tt), "gbc"], writes=[xsn])
                for half in range(2):
                    ps, psn = PS.next()
                    psb = ps[:].bitcast(BF16)
                    for j in range(8):
                        kc = half * 8 + j
                        Sc.op("pe", lambda e: e.transpose(psb[:, j * 128:(j + 1) * 128],
                                                          xs[:, kc * 128:(kc + 1) * 128], cst_extra["ident"]),
                              reads=[xsn, "consts"], writes=[psn])
                    eng = "act" if half == 0 else "dve"
                    src = psb.rearrange("p (j t) -> p j t", j=8)
                    dst = hdnT[:, half * 8:(half + 1) * 8, tt * 128:(tt + 1) * 128]
                    if eng == "act":
                        Sc.op("act", lambda e: e.copy(dst, src), reads=[psn], writes=[("hdnT", (tt, half))])
                    else:
                        Sc.op("dve", lambda e: e.tensor_copy(dst, src), reads=[psn], writes=[("hdnT", (tt, half))])
        Sc.barrier()
        w_r = Rot(nc, es, "wt", [128, NKC, 512], BF16, 2)
        st_r = Rot(nc, es, "st", [128, 4, 512], BF16, 3)
        sq_r = Rot(nc, es, "sq", [128, 512], BF16, 2)
        rs_r = Rot(nc, es, "rs", [128, 512], F32, 2)
        eg_r = Rot(nc, es, "eg", [128, 512], F32, 2)
        for hd in heads:
            kind = head_kinds[hd]
            wt, wtn = w_r.next()
            for g4 in range(4):
                Sc.dma("pool", wt[:, g4 * 4:(g4 + 1) * 4, :], W[hd, :, g4 * 4:(g4 + 1) * 4, :],
                       writes=[(wtn, g4)])
            for tt in range(NTT):
                st, stn = st_r.next()
                pss = []
                for c in range(4):
                    ps, psn = PS.next()
                    pss.append((ps, psn))
                    for kc in range(NKC):
                        Sc.op("pe", lambda e: e.matmul(ps[:], wt[:, kc, c * 128:(c + 1) * 128],
                                                       hdnT[:, kc, tt * 512:(tt + 1) * 512],
                                                       start=(kc == 0), stop=(kc == NKC - 1)),
                              reads=[(wtn, kc // 4)], writes=[psn])
                (pq, pqn), (pk, pkn), (pv, pvn), (pg, pgn) = pss
                if kind == "sb":
                    sc = 1.0 / math.sqrt(128.0)
                    Sc.op("act", lambda e: e.mul(st[:, 0, :], pq[:], sc), reads=[pqn], writes=[(stn, 0)])
                    Sc.op("dve", lambda e: e.tensor_copy(st[:, 1, :], pk[:]), reads=[pkn], writes=[(stn, 1)])
                elif kind == "dil":
                    for ci, (pp, ppn, gcol) in enumerate(((pq, pqn, cst_extra["qg"]), (pk, pkn, cst_extra["kg"]))):
                        sq, sqn = sq_r.next()
                        Sc.op("act", lambda e: e.activation(sq[:], pp[:], AF.Square), reads=[ppn], writes=[sqn])
                        p2, p2n = PS.next()
                        Sc.op("pe", lambda e: e.matmul(p2[:], cst_extra["ones"], sq[:], start=True, stop=True),
                              reads=[sqn, "consts"], writes=[p2n])
                        rs, rsn = rs_r.next()
                        Sc.op("act", lambda e: e.activation(rs[:], p2[:], AF.Ln, bias=128.0 * EPS),
                              reads=[p2n], writes=[rsn])
                        Sc.op("act", lambda e: e.activation(rs[:], rs[:], AF.Exp, scale=-0.5),
                              reads=[rsn], writes=[rsn])
                        Sc.op("dve", lambda e: e.scalar_tensor_tensor(st[:, ci, :], pp[:], gcol, rs[:],
                                                                      ALU.mult, ALU.mult),
                              reads=[ppn, rsn, "gcols"], writes=[(stn, ci)])
                else:
                    Sc.op("act", lambda e: e.copy(st[:, 0, :], pq[:]), reads=[pqn], writes=[(stn, 0)])
                    Sc.op("dve", lambda e: e.tensor_copy(st[:, 1, :], pk[:]), reads=[pkn], writes=[(stn, 1)])
                Sc.op("act", lambda e: e.copy(st[:, 2, :], pv[:]), reads=[pvn], writes=[(stn, 2)])
                eg, egn = eg_r.next()
                Sc.op("act", lambda e: e.activation(eg[:], pg[:], AF.Exp, scale=-1.0), reads=[pgn], writes=[egn])
                Sc.op("dve", lambda e: e.tensor_scalar(eg[:], eg[:], 1.0, None, ALU.add), reads=[egn], writes=[egn])
                Sc.op("dve", lambda e: e.reciprocal(eg[:], eg[:]), reads=[egn], writes=[egn])
                Sc.op("dve", lambda e: e.tensor_tensor(st[:, 3, :], pg[:], eg[:], ALU.mult),
                      reads=[pgn, egn], writes=[(stn, 3)])
                dst = scr[hd, :, :, tt * 512:(tt + 1) * 512].rearrange("c p t -> p c t")
                Sc.dma("sp", dst, st[:], reads=[stn])
    Sc.barrier()


def load_head(nc, Sc, hb, hbn, scr, hd):
    for c in range(4):
        Sc.dma("sp", hb[:, c, :], scr[hd, c, :, :], writes=[(hbn, c)])


def make_vtok(nc, Sc, PS, cst, hb, hbn, vtok, vtokn, dil, srcT=None, srckey=None, fill=None):
    nb = 32 // dil
    if srcT is None:
        srcT, srckey = hb[:, 2, :], (hbn, 2)
    for g in range(8):
        ps, psn = PS.next()
        psb = ps[:].bitcast(BF16)
        for j in range(4):
            blk = g * 4 + j
            p, n = blk // nb, blk % nb
            start = p + dil * 128 * n
            src = srcT[:, start:start + dil * 127 + 1:dil]
            Sc.op("pe", lambda e: e.transpose(psb[:, j * 128:(j + 1) * 128], src, cst["ident"]),
                  reads=[srckey, "consts"], writes=[psn])
        dst = vtok[:, g * 4:(g + 1) * 4, :]
        srcp = psb[:, 0:512].rearrange("p (j t) -> p j t", j=4)
        if g % 2 == 0:
            Sc.op("dve", lambda e: e.tensor_copy(dst, srcp), reads=[psn], writes=[(vtokn, g)])
        else:
            Sc.op("act", lambda e: e.copy(dst, srcp), reads=[psn], writes=[(vtokn, g)])
        if fill:
            fill(1)


def sb_head(nc, Sc, PS, PSO, cst, R, hb, hbn, vtok, vtokn, out_dram, is_output=True):
    qT, kT, gT = hb[:, 0, :], hb[:, 1, :], hb[:, 3, :]
    for qt in range(NTT):
        nkb = 4 * (qt + 1)
        qs = qT[:, qt * 512:(qt + 1) * 512]
        o_ps, o_psn = PSO.next()
        carry = None
        carryn = None
        for idx, kb in enumerate(range(nkb - 1, -1, -1)):
            m = kb - 4 * qt
            diag = m >= 0
            ks = kT[:, kb * 128:(kb + 1) * 128]
            z_ps, z_psn = PS.next()
            Sc.op("pe", lambda e: e.matmul(z_ps[:], ks, qs, start=True, stop=not diag),
                  reads=[(hbn, 0), (hbn, 1)], writes=[z_psn])
            if diag:
                Sc.op("pe", lambda e: e.matmul(z_ps[:], cst["ident"], cst["sbmask"][m], start=False, stop=True),
                      reads=["consts"], writes=[z_psn])
            ee, een = R["e"].next()
            Sc.op("act", lambda e: e.activation(ee[:], z_ps[:], AF.Exp), reads=[z_psn], writes=[een])
            sp, spn = R["sp"].next()
            Sc.op("act", lambda e: e.activation(sp[:], ee[:], AF.Ln, bias=1.0), reads=[een], writes=[spn])
            a_ps, a_psn = PS.next()
            Sc.op("pe", lambda e: e.matmul(a_ps[:], ks, qs, start=True, stop=False),
                  reads=[(hbn, 0), (hbn, 1)], writes=[a_psn])
            if diag:
                Sc.op("pe", lambda e: e.matmul(a_ps[:], cst["ident"], cst["sbmask"][m], start=False, stop=False),
                      reads=["consts"], writes=[a_psn])
            Sc.op("pe", lambda e: e.matmul(a_ps[:], cst["negtri"], sp[:], start=False, stop=(carry is None)),
                  reads=["consts", spn], writes=[a_psn])
            if carry is not None:
                Sc.op("pe", lambda e: e.matmul(a_ps[:], cst["negones"], carry[:], start=False, stop=True),
                      reads=["consts", carryn], writes=[a_psn])
            at, atn = R["a"].next()
            Sc.op("act", lambda e: e.activation(at[:], a_ps[:], AF.Exp), reads=[a_psn], writes=[atn])
            Sc.op("pe", lambda e: e.matmul(o_ps[:], vtok[:, kb, :], at[:], start=(idx == 0), stop=(kb == 0)),
                  reads=[(vtokn, kb // 4), atn], writes=[o_psn])
            if kb > 0:
                if carry is None:
                    carry, carryn = sp, spn
                else:
                    nc_, ncn = R["carry"].next()
                    Sc.op("dve", lambda e: e.tensor_tensor(nc_[:], carry[:], sp[:], ALU.add),
                          reads=[carryn, spn], writes=[ncn])
                    carry, carryn = nc_, ncn
        ot, otn = R["o"].next()
        Sc.op("dve", lambda e: e.tensor_tensor(ot[:], o_ps[:], gT[:, qt * 512:(qt + 1) * 512], ALU.mult),
              reads=[o_psn, (hbn, 3)], writes=[otn])
        Sc.dma("sp", out_dram[:, qt * 512:(qt + 1) * 512], ot[:], reads=[otn], is_output=is_output)


def dil_cols(p, n, r):
    start = p + r * 128 * n
    return slice(start, start + r * 127 + 1, r)


def dil_head(nc, Sc, PS, cst, R, hb, hbn, vtoks, vtokns, UZ, out_dram, is_output=True):
    qT, kT, gT = hb[:, 0, :], hb[:, 1, :], hb[:, 3, :]
    sc = math.sqrt(128.0)
    for gi, r in enumerate((1, 4, 16)):
        make_vtok(nc, Sc, PS, cst, hb, hbn, vtoks[gi], vtokns[gi], r)
    for gi, r in enumerate((1, 4, 16)):
        nb = 32 // r
        vt, vtn = vtoks[gi], vtokns[gi]
        for p in range(r):
            for n in range(nb):
                blk = p * nb + n
                cq = dil_cols(p, n, r)
                w = 256 if n >= 1 else 128
                s_ps, s_psn = PS.next()
                Sc.op("pe", lambda e: e.matmul(s_ps[:, 0:128], kT[:, cq], qT[:, cq], start=True, stop=False),
                      reads=[(hbn, 0), (hbn, 1)], writes=[s_psn])
                Sc.op("pe", lambda e: e.matmul(s_ps[:, 0:128], cst["ident"], cst["dmask"][:, 0:128],
                                               start=False, stop=True), reads=["consts"], writes=[s_psn])
                if n >= 1:
                    ck = dil_cols(p, n - 1, r)
                    Sc.op("pe", lambda e: e.matmul(s_ps[:, 128:256], kT[:, ck], qT[:, cq], start=True, stop=False),
                          reads=[(hbn, 0), (hbn, 1)], writes=[s_psn])
                    Sc.op("pe", lambda e: e.matmul(s_ps[:, 128:256], cst["ident"], cst["dmask"][:, 128:256],
                                                   start=False, stop=True), reads=["consts"], writes=[s_psn])
                pt, ptn = R["p"].next()
                Sc.op("act", lambda e: e.activation(pt[:, 0:w], s_ps[:, 0:w], AF.Exp, scale=sc),
                      reads=[s_psn], writes=[ptn])
                uz_ps, uz_psn = PS.next()
                for half, lhs in enumerate((None, cst["ones"])):
                    o = uz_ps[:, half * 128:(half + 1) * 128]
                    l0 = vt[:, blk, :] if half == 0 else lhs
                    Sc.op("pe", lambda e: e.matmul(o, l0, pt[:, 0:128], start=True, stop=(n == 0)),
                          reads=[(vtn, blk // 4), ptn, "consts"], writes=[uz_psn])
                    if n >= 1:
                        l1 = vt[:, blk - 1, :] if half == 0 else lhs
                        Sc.op("pe", lambda e: e.matmul(o, l1, pt[:, 128:256], start=False, stop=True),
                              reads=[(vtn, (blk - 1) // 4), ptn, "consts"], writes=[uz_psn])
                acc = UZ[:, :, cq]
                src = uz_ps[:, 0:256].rearrange("p (a t) -> p a t", a=2)
                if gi == 0:
                    wr = [("uz0", n)]
                    if n % 2 == 0:
                        Sc.op("dve", lambda e: e.tensor_copy(acc, src), reads=[uz_psn], writes=wr)
                    else:
                        Sc.op("act", lambda e: e.copy(acc, src), reads=[uz_psn], writes=wr)
                else:
                    if gi == 1:
                        rd = [("uz0", 4 * n + i) for i in range(4)]
                        wr = [("uz1", (n, p))]
                    else:
                        rd = [("uz1", (c, p % 4)) for c in range(4 * n, 4 * n + 4)]
                        wr = [("uz2", (n, p))]
                    Sc.op("dve", lambda e: e.tensor_tensor(acc, acc, src, ALU.add),
                          reads=[uz_psn] + rd, writes=wr)
    for ch in range(8):
        cs = slice(ch * 512, (ch + 1) * 512)
        rz, rzn = R["rz"].next()
        Sc.op("dve", lambda e: e.reciprocal(rz[:], UZ[:, 1, cs]), reads=["uz0", "uz1", "uz2"], writes=[rzn])
        Sc.op("pool", lambda e: e.tensor_tensor(rz[:], rz[:], UZ[:, 0, cs], ALU.mult),
              reads=[rzn, "uz0", "uz1", "uz2"], writes=[rzn])
        ot, otn = R["o"].next()
        Sc.op("pool", lambda e: e.tensor_tensor(ot[:], rz[:], gT[:, cs], ALU.mult),
              reads=[rzn, (hbn, 3)], writes=[otn])
        Sc.dma("sp", out_dram[:, cs], ot[:], reads=[otn], is_output=is_output)


def hgrn_head(nc, Sc, PS, PSO, cst, R, T, hb, hbn, lbc, omlc, ogc, out_dram, is_output=True):
    qT, fT, vT, gT = hb[:, 0, :], hb[:, 1, :], hb[:, 2, :], hb[:, 3, :]
    E, B2, CUM, KT, QT, KTt, EL = T["E"], T["B2"], T["CUM"], T["KT"], T["QT"], T["KTt"], T["EL"]
    ktok, vtok = T["ktok"], T["vtok"]
    Sc.op("act", lambda e: e.activation(E[:], fT, AF.Exp, scale=-1.0), reads=[(hbn, 1)], writes=["E"])
    Sc.op("dve", lambda e: e.tensor_scalar(B2[:], E[:], 1.0, None, ALU.add), reads=["E"], writes=["B2"])
    Sc.op("dve", lambda e: e.reciprocal(B2[:], B2[:]), reads=["B2"], writes=["B2"])
    Sc.op("dve", lambda e: e.scalar_tensor_tensor(KT[:], E[:], omlc, B2[:], ALU.mult, ALU.mult),
          reads=["E", "B2", "hconst"], writes=["KT"])
    Sc.op("dve", lambda e: e.tensor_scalar(B2[:], B2[:], omlc, lbc, ALU.mult, ALU.add),
          reads=["B2", "hconst"], writes=["B2"])
    Sc.op("act", lambda e: e.activation(B2[:], B2[:], AF.Ln), reads=["B2"], writes=["B2"])
    Sc.op("dve", lambda e: e.tensor_tensor_scan(CUM[:], T["rmask"], B2[:], 0.0, ALU.mult, ALU.add),
          reads=["B2", "consts2"], writes=["CUM"])
    Sc.op("act", lambda e: e.activation(E[:], CUM[:], AF.Exp), reads=["CUM"], writes=["E"])
    Sc.op("dve", lambda e: e.tensor_tensor(QT[:], qT, E[:], ALU.mult), reads=["E", (hbn, 0)], writes=["QT"])
    Sc.op("act", lambda e: e.activation(B2[:], CUM[:], AF.Exp, scale=-1.0), reads=["CUM"], writes=["B2"])
    Sc.op("dve", lambda e: e.tensor_tensor(KTt[:], KT[:], B2[:], ALU.mult), reads=["KT", "B2"], writes=["KTt"])
    Sc.op("act", lambda e: e.activation(EL[:], CUM[:, 127:4096:128], AF.Exp), reads=["CUM"], writes=["EL"])
    make_vtok(nc, Sc, PS, cst, hb, hbn, ktok, "ktok", 1, srcT=KTt[:], srckey="KTt")
    make_vtok(nc, Sc, PS, cst, hb, hbn, vtok, "vtok", 1)
    state = None
    for c in range(32):
        tsl = slice(c * 128, (c + 1) * 128)
        at_ps, at_psn = PS.next()
        Sc.op("pe", lambda e: e.matmul(at_ps[:, 0:128], KTt[:, tsl], QT[:, tsl], start=True, stop=True),
              reads=["KTt", "QT"], writes=[at_psn])
        am, amn = R["am"].next()
        Sc.op("dve", lambda e: e.tensor_tensor(am[:], at_ps[:, 0:128], T["mask01"], ALU.mult),
              reads=[at_psn, "consts2"], writes=[amn])
        if c % 4 == 0:
            o_ps, o_psn = PSO.next()
        oc = slice((c % 4) * 128, (c % 4 + 1) * 128)
        if state is not None:
            st_t, st_n = state
            Sc.op("pe", lambda e: e.matmul(o_ps[:, oc], st_t[:], QT[:, tsl], start=True, stop=False),
                  reads=[st_n, "QT"], writes=[o_psn])
        Sc.op("pe", lambda e: e.matmul(o_ps[:, oc], vtok[:, c, :], am[:], start=(state is None), stop=True),
              reads=[("vtok", c // 4), amn], writes=[o_psn])
        if c < 31:
            p2, p2n = PS.next()
            Sc.op("pe", lambda e: e.matmul(p2[:, 0:128], ktok[:, c, :], vtok[:, c, :], start=True, stop=True),
                  reads=[("ktok", c // 4), ("vtok", c // 4)], writes=[p2n])
            ns, nsn = R["state"].next()
            if state is None:
                Sc.op("act", lambda e: e.mul(ns[:], p2[:, 0:128], EL[:, c:c + 1]), reads=[p2n, "EL"], writes=[nsn])
            else:
                pe_, pen = R["pse"].next()
                Sc.op("act", lambda e: e.mul(pe_[:], p2[:, 0:128], EL[:, c:c + 1]), reads=[p2n, "EL"], writes=[pen])
                Sc.op("dve", lambda e: e.scalar_tensor_tensor(ns[:], st_t[:], EL[:, c:c + 1], pe_[:],
                                                              ALU.mult, ALU.add),
                      reads=[st_n, "EL", pen], writes=[nsn])
            state = (ns, nsn)
        if c % 4 == 3:
            cs = slice((c // 4) * 512, (c // 4 + 1) * 512)
            sq, sqn = R["sq"].next()
            Sc.op("act", lambda e: e.activation(sq[:], o_ps[:], AF.Square), reads=[o_psn], writes=[sqn])
            ss_ps, ss_psn = PS.next()
            Sc.op("pe", lambda e: e.matmul(ss_ps[:], cst["ones"], sq[:], start=True, stop=True),
                  reads=[sqn, "consts"], writes=[ss_psn])
            rs, rsn = R["rs"].next()
            Sc.op("act", lambda e: e.activation(rs[:], ss_ps[:], AF.Ln, scale=1.0 / 128.0, bias=EPS),
                  reads=[ss_psn], writes=[rsn])
            Sc.op("act", lambda e: e.activation(rs[:], rs[:], AF.Exp, scale=-0.5), reads=[rsn], writes=[rsn])
            Sc.op("dve", lambda e: e.tensor_tensor(rs[:], o_ps[:], rs[:], ALU.mult), reads=[o_psn, rsn], writes=[rsn])
            ot, otn = R["o"].next()
            Sc.op("dve", lambda e: e.scalar_tensor_tensor(ot[:], rs[:], ogc, gT[:, cs], ALU.mult, ALU.mult),
                  reads=[rsn, "hconst", "ogt", (hbn, 3)], writes=[otn])
            Sc.dma("sp", out_dram[:, cs], ot[:], reads=[otn], is_output=is_output)


def sb_head_v2(nc, Sc, PS, PSO, cst, R, hb, hbn, vtok, vtokn, out_dram, is_output=True, fill=None):
    qT, kT, gT = hb[:, 0, :], hb[:, 1, :], hb[:, 3, :]
    blocks = []
    for qt in range(NTT):
        nkb = 4 * (qt + 1)
        for idx, kb in enumerate(range(nkb - 1, -1, -1)):
            blocks.append({"qt": qt, "kb": kb, "first": idx == 0, "last": kb == 0})
    qstate = {}

    def stage_z(b):
        qt, kb = b["qt"], b["kb"]
        m = kb - 4 * qt
        qs = qT[:, qt * 512:(qt + 1) * 512]
        ks = kT[:, kb * 128:(kb + 1) * 128]
        z_ps, z_psn = PS.next()
        Sc.op("pe", lambda e: e.matmul(z_ps[:], ks, qs, start=True, stop=False, skip_group_check=True),
              reads=[(hbn, 0), (hbn, 1)], writes=[z_psn])
        if m >= 0:
            Sc.op("pe", lambda e: e.matmul(z_ps[:], cst["ident"], cst["sbmask"][m], start=False, stop=False,
                                           skip_group_check=True),
                  reads=["consts"], writes=[z_psn])
        b["z"] = (z_ps, z_psn)
        ee, een = R["e"].next()
        Sc.op("act", lambda e: e.activation(ee[:], z_ps[:], AF.Exp), reads=[z_psn], writes=[een])
        sp, spn = R["sp"].next()
        Sc.op("act", lambda e: e.activation(sp[:], ee[:], AF.Ln, bias=1.0), reads=[een], writes=[spn])
        b["sp"] = (sp, spn)

    def stage_arg(b):
        qt, kb = b["qt"], b["kb"]
        m = kb - 4 * qt
        qs = qT[:, qt * 512:(qt + 1) * 512]
        ks = kT[:, kb * 128:(kb + 1) * 128]
        sp, spn = b["sp"]
        if b["first"]:
            qstate[qt] = None
        carry = qstate[qt]
        a_ps, a_psn = b["z"]
        Sc.op("pe", lambda e: e.matmul(a_ps[:], cst["negtri"], sp[:], start=False, stop=(carry is None),
                                       skip_group_check=True),
              reads=["consts", spn], writes=[a_psn])
        if carry is not None:
            Sc.op("pe", lambda e: e.matmul(a_ps[:], cst["negones"], carry[0][:], start=False, stop=True,
                                           skip_group_check=True),
                  reads=["consts", carry[1]], writes=[a_psn])
        at, atn = R["a"].next()
        Sc.op("act", lambda e: e.activation(at[:], a_ps[:], AF.Exp), reads=[a_psn], writes=[atn])
        b["at"] = (at, atn)
        if kb > 0:
            if carry is None:
                qstate[qt] = (sp, spn)
            else:
                nc_, ncn = R["carry"].next()
                Sc.op("dve", lambda e: e.tensor_tensor(nc_[:], carry[0][:], sp[:], ALU.add),
                      reads=[carry[1], spn], writes=[ncn])
                qstate[qt] = (nc_, ncn)

    ops = {}

    def stage_av(b):
        qt, kb = b["qt"], b["kb"]
        at, atn = b["at"]
        if b["first"]:
            ops[qt] = PSO.next()
        o_ps, o_psn = ops[qt]
        Sc.op("pe", lambda e: e.matmul(o_ps[:], vtok[:, kb, :], at[:], start=b["first"], stop=b["last"]),
              reads=[(vtokn, kb // 4), atn], writes=[o_psn])
        if b["last"]:
            ot, otn = R["o"].next()
            Sc.op("dve", lambda e: e.tensor_tensor(ot[:], o_ps[:], gT[:, qt * 512:(qt + 1) * 512], ALU.mult),
                  reads=[o_psn, (hbn, 3)], writes=[otn])
            Sc.dma("sp", out_dram[:, qt * 512:(qt + 1) * 512], ot[:], reads=[otn], is_output=is_output)

    n = len(blocks)
    for k in range(n + 2):
        if k < n:
            stage_z(blocks[k])
        if 0 <= k - 1 < n:
            stage_arg(blocks[k - 1])
        if 0 <= k - 2 < n:
            stage_av(blocks[k - 2])
        if fill:
            fill(1)


def hgrn_head_v2(nc, Sc, PS, PSO, cst, R, T, hb, hbn, lbc, omlc, ogc, out_dram, is_output=True, fill=None):
    qT, fT, vT, gT = hb[:, 0, :], hb[:, 1, :], hb[:, 2, :], hb[:, 3, :]
    E, B2, CUM, KT, QT, EL = T["E"], T["B2"], T["CUM"], T["KT"], T["QT"], T["EL"]
    ktok, vtok, AM, ST = T["ktok"], T["vtok"], T["AM"], T["ST"]
    f_ = fill if fill else (lambda n: None)
    H = S // 2
    for hf in range(2):
        cs = slice(hf * H, (hf + 1) * H)
        Sc.op("act", lambda e: e.activation(E[:], fT[:, cs], AF.Exp, scale=-1.0), reads=[(hbn, 1)], writes=["E"])
        f_(5)
        Sc.op("dve", lambda e: e.tensor_scalar(B2[:], E[:], 1.0, None, ALU.add), reads=["E"], writes=["B2"])
        Sc.op("dve", lambda e: e.reciprocal(B2[:], B2[:]), reads=["B2"], writes=["B2"])
        f_(5)
        Sc.op("dve", lambda e: e.scalar_tensor_tensor(KT[:, cs], E[:], omlc, B2[:], ALU.mult, ALU.mult),
              reads=["E", "B2", "hconst"], writes=[("KT", hf)])
        Sc.op("dve", lambda e: e.tensor_scalar(B2[:], B2[:], omlc, lbc, ALU.mult, ALU.add),
              reads=["B2", "hconst"], writes=["B2"])
        f_(5)
        Sc.op("act", lambda e: e.activation(B2[:], B2[:], AF.Ln), reads=["B2"], writes=["B2"])
        Sc.op("dve", lambda e: e.tensor_tensor_scan(CUM[:], T["rmask"][:, cs], B2[:], 0.0, ALU.mult, ALU.add),
              reads=["B2", "consts2"], writes=["CUM"])
        f_(5)
        Sc.op("act", lambda e: e.activation(E[:], CUM[:], AF.Exp), reads=["CUM"], writes=["E"])
        Sc.op("dve", lambda e: e.tensor_tensor(QT[:, cs], qT[:, cs], E[:], ALU.mult),
              reads=["E", (hbn, 0)], writes=[("QT", hf)])
        f_(5)
        Sc.op("act", lambda e: e.activation(B2[:], CUM[:], AF.Exp, scale=-1.0), reads=["CUM"], writes=["B2"])
        Sc.op("dve", lambda e: e.tensor_tensor(KT[:, cs], KT[:, cs], B2[:], ALU.mult),
              reads=[("KT", hf), "B2"], writes=[("KT", hf)])
        Sc.op("act", lambda e: e.activation(EL[:, hf * 16:(hf + 1) * 16], CUM[:, 127:H:128], AF.Exp),
              reads=["CUM"], writes=[("EL", hf)])
        f_(5)
    make_vtok(nc, Sc, PS, cst, hb, hbn, ktok, "ktok", 1, srcT=KT[:], srckey="KT", fill=fill)
    make_vtok(nc, Sc, PS, cst, hb, hbn, vtok, "vtok", 1, fill=fill)
    for g in range(8):
        at_ps, at_psn = PS.next()
        for j in range(4):
            c = g * 4 + j
            tsl = slice(c * 128, (c + 1) * 128)
            Sc.op("pe", lambda e: e.matmul(at_ps[:, j * 128:(j + 1) * 128], KT[:, tsl], QT[:, tsl],
                                           start=True, stop=True), reads=["KT", "QT"], writes=[at_psn])
        Sc.op("dve", lambda e: e.tensor_tensor(AM[:, g * 4:(g + 1) * 4, :],
                                               at_ps[:].rearrange("p (j t) -> p j t", j=4),
                                               T["mask4"].rearrange("p (j t) -> p j t", j=4), ALU.mult),
              reads=[at_psn, "consts2"], writes=[("AM", g)])
        p2, p2n = PS.next()
        for j in range(4):
            c = g * 4 + j
            if c == 31:
                continue
            Sc.op("pe", lambda e: e.matmul(p2[:, j * 128:(j + 1) * 128], ktok[:, c, :], vtok[:, c, :],
                                           start=True, stop=True),
                  reads=[("ktok", c // 4), ("vtok", c // 4)], writes=[p2n])
        for j in range(4):
            c = g * 4 + j
            if c == 31:
                continue
            Sc.op("act", lambda e: e.mul(ST[:, c + 1, :], p2[:, j * 128:(j + 1) * 128], EL[:, c:c + 1]),
                  reads=[p2n, "EL"], writes=[("ST", c + 1)])
        f_(3)
    for c in range(1, 31):
        Sc.op("dve", lambda e: e.scalar_tensor_tensor(ST[:, c + 1, :], ST[:, c, :], EL[:, c:c + 1], ST[:, c + 1, :],
                                                      ALU.mult, ALU.add),
              reads=[("ST", c), "EL", ("ST", c + 1)], writes=[("ST", c + 1)])
        f_(1)
    for g in range(8):
        o_ps, o_psn = PSO.next()
        for j in range(4):
            c = g * 4 + j
            tsl = slice(c * 128, (c + 1) * 128)
            oc = slice(j * 128, (j + 1) * 128)
            if c > 0:
                Sc.op("pe", lambda e: e.matmul(o_ps[:, oc], ST[:, c, :], QT[:, tsl], start=True, stop=False),
                      reads=[("ST", c), "QT"], writes=[o_psn])
            Sc.op("pe", lambda e: e.matmul(o_ps[:, oc], vtok[:, c, :], AM[:, c, :], start=(c == 0), stop=True),
                  reads=[("vtok", c // 4), ("AM", g)], writes=[o_psn])
        cs = slice(g * 512, (g + 1) * 512)
        sq, sqn = R["sq"].next()
        Sc.op("act", lambda e: e.activation(sq[:], o_ps[:], AF.Square), reads=[o_psn], writes=[sqn])
        ss_ps, ss_psn = PS.next()
        Sc.op("pe", lambda e: e.matmul(ss_ps[:], cst["ones"], sq[:], start=True, stop=True),
              reads=[sqn, "consts"], writes=[ss_psn])
        rs, rsn = R["rs"].next()
        Sc.op("act", lambda e: e.activation(rs[:], ss_ps[:], AF.Ln, scale=1.0 / 128.0, bias=EPS),
              reads=[ss_psn], writes=[rsn])
        Sc.op("act", lambda e: e.activation(rs[:], rs[:], AF.Exp, scale=-0.5), reads=[rsn], writes=[rsn])
        Sc.op("dve", lambda e: e.tensor_tensor(rs[:], o_ps[:], rs[:], ALU.mult), reads=[o_psn, rsn], writes=[rsn])
        ot, otn = R["o"].next()
        Sc.op("dve", lambda e: e.scalar_tensor_tensor(ot[:], rs[:], ogc, gT[:, cs], ALU.mult, ALU.mult),
              reads=[rsn, "hconst", "ogt", (hbn, 3)], writes=[otn])
        Sc.dma("sp", out_dram[:, cs], ot[:], reads=[otn], is_output=is_output)
        f_(2)


def dil_head_v2(nc, Sc, PS, cst, R, hb, hbn, vtoks, vtokns, UZ, out_dram, is_output=True, fill=None):
    qT, kT, gT = hb[:, 0, :], hb[:, 1, :], hb[:, 3, :]
    sc = math.sqrt(128.0)
    for gi, r in enumerate((1, 4, 16)):
        make_vtok(nc, Sc, PS, cst, hb, hbn, vtoks[gi], vtokns[gi], r)
    blocks = []
    for gi, r in enumerate((1, 4, 16)):
        nb = 32 // r
        for p in range(r):
            for n in range(nb):
                blocks.append({"gi": gi, "r": r, "p": p, "n": n, "blk": p * nb + n})

    def stage_s(b):
        r, p, n = b["r"], b["p"], b["n"]
        cq = dil_cols(p, n, r)
        w = 256 if n >= 1 else 128
        s_ps, s_psn = PS.next()
        Sc.op("pe", lambda e: e.matmul(s_ps[:, 0:w], cst["ident"], cst["dmask"][:, 0:w], start=True, stop=False,
                                       skip_group_check=True),
              reads=["consts"], writes=[s_psn])
        Sc.op("pe", lambda e: e.matmul(s_ps[:, 0:128], kT[:, cq], qT[:, cq], start=False, stop=(n == 0),
                                       skip_group_check=True),
              reads=[(hbn, 0), (hbn, 1)], writes=[s_psn])
        if n >= 1:
            ck = dil_cols(p, n - 1, r)
            Sc.op("pe", lambda e: e.matmul(s_ps[:, 128:256], kT[:, ck], qT[:, cq], start=False, stop=True,
                                           skip_group_check=True),
                  reads=[(hbn, 0), (hbn, 1)], writes=[s_psn])
        pt, ptn = R["p"].next()
        Sc.op("act", lambda e: e.activation(pt[:, 0:w], s_ps[:, 0:w], AF.Exp, scale=sc),
              reads=[s_psn], writes=[ptn])
        b["pt"] = (pt, ptn)

    def stage_uz(b):
        gi, r, p, n, blk = b["gi"], b["r"], b["p"], b["n"], b["blk"]
        vt, vtn = vtoks[gi], vtokns[gi]
        pt, ptn = b["pt"]
        cq = dil_cols(p, n, r)
        uz_ps, uz_psn = PS.next()
        for half in range(2):
            o = uz_ps[:, half * 128:(half + 1) * 128]
            l0 = vt[:, blk, :] if half == 0 else cst["ones"]
            Sc.op("pe", lambda e: e.matmul(o, l0, pt[:, 0:128], start=True, stop=(n == 0)),
                  reads=[(vtn, blk // 4), ptn, "consts"], writes=[uz_psn])
            if n >= 1:
                l1 = vt[:, blk - 1, :] if half == 0 else cst["ones"]
                Sc.op("pe", lambda e: e.matmul(o, l1, pt[:, 128:256], start=False, stop=True),
                      reads=[(vtn, (blk - 1) // 4), ptn, "consts"], writes=[uz_psn])
        acc = UZ[:, :, cq]
        src = uz_ps[:, 0:256].rearrange("p (a t) -> p a t", a=2)
        if gi == 0:
            wr = [("uz0", n)]
            if n % 2 == 0:
                Sc.op("dve", lambda e: e.tensor_copy(acc, src), reads=[uz_psn], writes=wr)
            else:
                Sc.op("act", lambda e: e.copy(acc, src), reads=[uz_psn], writes=wr)
        else:
            if gi == 1:
                rd = [("uz0", 4 * n + i) for i in range(4)]
                wr = [("uz1", (n, p))]
            else:
                rd = [("uz1", (c, p % 4)) for c in range(4 * n, 4 * n + 4)]
                wr = [("uz2", (n, p))]
            Sc.op("dve", lambda e: e.tensor_tensor(acc, acc, src, ALU.add), reads=[uz_psn] + rd, writes=wr)

    nb_ = len(blocks)
    for k in range(nb_ + 1):
        if k < nb_:
            stage_s(blocks[k])
        if k >= 1:
            stage_uz(blocks[k - 1])
        if fill:
            fill(2 if k % 2 == 0 else 1)
    for ch in range(8):
        cs = slice(ch * 512, (ch + 1) * 512)
        rz, rzn = R["rz"].next()
        Sc.op("dve", lambda e: e.reciprocal(rz[:], UZ[:, 1, cs]), reads=["uz0", "uz1", "uz2"], writes=[rzn])
        Sc.op("pool", lambda e: e.tensor_tensor(rz[:], rz[:], UZ[:, 0, cs], ALU.mult),
              reads=[rzn, "uz0", "uz1", "uz2"], writes=[rzn])
        ot, otn = R["o"].next()
        Sc.op("pool", lambda e: e.tensor_tensor(ot[:], rz[:], gT[:, cs], ALU.mult),
              reads=[rzn, (hbn, 3)], writes=[otn])
        Sc.dma("sp", out_dram[:, cs], ot[:], reads=[otn], is_output=is_output)


def phase1_stream(nc, Sc, PS, x, gain_bc, hd_d, cst):
    with ExitStack() as e1:
        gbc = e1.enter_context(_sbt(nc, "gbc", [128, D], F32))
        ss = e1.enter_context(_sbt(nc, "ss", [128, 32], F32))
        rstd = e1.enter_context(_sbt(nc, "rstd", [128, 32], F32))
        junk = e1.enter_context(_sbt(nc, "junk", [128, D], BF16))
        xt_r = Rot(nc, e1, "xt", [128, D], F32, 4)
        xs_r = Rot(nc, e1, "xs", [128, D], BF16, 4)
        hs_r = Rot(nc, e1, "hs1_", [128, NKC, 512], BF16, 2)
        Sc.dma("sp", gbc[:], gain_bc, writes=["gbc"])
        Sc.op("dve", lambda e: e.memset(ss[:], 0.0), writes=["ss"])
        for t5 in range(8):
            hs, hsn = hs_r.next()
            for ti in range(4):
                tt = t5 * 4 + ti
                xt, xtn = xt_r.next()
                Sc.dma("pool", xt[:], x[tt * 128:(tt + 1) * 128, :], writes=[xtn])
                Sc.op("act", lambda e: e.activation(junk[:], xt[:], AF.Square, accum_out=ss[:, tt:tt + 1]),
                      reads=[xtn], writes=[("ss", tt)])
                Sc.op("act", lambda e: e.activation(rstd[:, tt:tt + 1], ss[:, tt:tt + 1], AF.Ln,
                                                    scale=1.0 / D, bias=EPS),
                      reads=[("ss", tt)], writes=[("rstd", tt)])
                Sc.op("act", lambda e: e.activation(rstd[:, tt:tt + 1], rstd[:, tt:tt + 1], AF.Exp, scale=-0.5),
                      reads=[("rstd", tt)], writes=[("rstd", tt)])
                xs, xsn = xs_r.next()
                Sc.op("dve", lambda e: e.scalar_tensor_tensor(xs[:], xt[:], rstd[:, tt:tt + 1], gbc[:],
                                                              ALU.mult, ALU.mult),
                      reads=[xtn, ("rstd", tt), "gbc"], writes=[xsn])
                for half in range(2):
                    ps, psn = PS.next()
                    psb = ps[:].bitcast(BF16)
                    for j in range(8):
                        kc = half * 8 + j
                        Sc.op("pe", lambda e: e.transpose(psb[:, j * 128:(j + 1) * 128],
                                                          xs[:, kc * 128:(kc + 1) * 128], cst["ident"]),
                              reads=[xsn, "consts"], writes=[psn])
                    src = psb.rearrange("p (j t) -> p j t", j=8)
                    dst = hs[:, half * 8:(half + 1) * 8, ti * 128:(ti + 1) * 128]
                    if half == 0:
                        Sc.op("act", lambda e: e.copy(dst, src), reads=[psn], writes=[(hsn, (ti, half))])
                    else:
                        Sc.op("dve", lambda e: e.tensor_copy(dst, src), reads=[psn], writes=[(hsn, (ti, half))])
            Sc.dma("sp", hd_d[t5], hs[:], reads=[hsn])
        Sc.barrier()


class ProjStream:
    def __init__(self, nc, es, Sc, PSP, cst, heads, kinds, W, hd_d, scr, nhs=3):
        self.nc, self.Sc, self.PSP, self.cst = nc, Sc, PSP, cst
        self.heads, self.kinds, self.W, self.hd_d, self.scr = heads, kinds, W, hd_d, scr
        self.wt_r = Rot(nc, es, "wt", [128, NKC, 512], BF16, 2)
        self.hs_r = Rot(nc, es, "hs", [128, NKC, 512], BF16, nhs)
        self.nhs = nhs
        self.st_r = Rot(nc, es, "st", [128, 4, 512], BF16, 2)
        if any(k == "dil" for k in kinds.values()):
            self.sq_r = Rot(nc, es, "sq", [128, 512], BF16, 2)
            self.rs_r = Rot(nc, es, "rs", [128, 512], F32, 2)
        self.eg_r = Rot(nc, es, "eg", [128, 512], F32, 2)
        self.pref = "dve"
        self.done = set()
        self.wts = {}
        self.hss = {}
        self.gen = self._run()
        self.exhausted = False

    def step(self, n=1):
        for _ in range(n):
            if self.exhausted:
                return
            try:
                next(self.gen)
            except StopIteration:
                self.exhausted = True

    def finish_head(self, hd):
        while hd not in self.done and not self.exhausted:
            self.step()

    def _load_w(self, hd):
        if hd in self.wts or hd not in self.kinds:
            return
        wt, wtn = self.wt_r.next()
        for g4 in range(4):
            self.Sc.dma("pool", wt[:, g4 * 4:(g4 + 1) * 4, :], self.W[hd, :, g4 * 4:(g4 + 1) * 4, :],
                        writes=[(wtn, g4)])
        self.wts[hd] = (wt, wtn)

    def _load_hs(self, i):
        if i in self.hss or i >= 8 * len(self.heads):
            return
        hs, hsn = self.hs_r.next()
        tt = i % 8
        for hf in range(2):
            self.Sc.dma("sp", hs[:, hf * 8:(hf + 1) * 8, :], self.hd_d[tt, :, hf * 8:(hf + 1) * 8, :],
                        writes=[(hsn, hf)])
        self.hss[i] = (hs, hsn)

    def _run(self):
        Sc, cst = self.Sc, self.cst
        self._load_w(self.heads[0])
        for j in range(self.nhs - 1):
            self._load_hs(j)
        for hi, hd in enumerate(self.heads):
            kind = self.kinds[hd]
            wt, wtn = self.wts[hd]
            if hi + 1 < len(self.heads):
                self._load_w(self.heads[hi + 1])
            for tt in range(NTT):
                i = hi * 8 + tt
                hs, hsn = self.hss.pop(i)
                self._load_hs(i + self.nhs - 1)
                st, stn = self.st_r.next()
                for c in range(4):
                    ps, psn = self.PSP.next()
                    for kc in range(NKC):
                        Sc.op("pe", lambda e: e.matmul(ps[:], wt[:, kc, c * 128:(c + 1) * 128], hs[:, kc, :],
                                                       start=(kc == 0), stop=(kc == NKC - 1)),
                              reads=[(wtn, kc // 4), (hsn, kc // 8)], writes=[psn])
                        if kc % 4 == 3 and kc != NKC - 1:
                            yield
                    self._evac(kind, c, ps, psn, st, stn)
                    yield
                dst = self.scr[hd, :, :, tt * 512:(tt + 1) * 512].rearrange("c p t -> p c t")
                Sc.dma("sp", dst, st[:], reads=[stn], writes=[("scr", hd)])
            self.done.add(hd)
            yield

    def _evac(self, kind, c, ps, psn, st, stn):
        Sc, cst = self.Sc, self.cst
        act_heavy = self.pref == "act"

        def cp(dst, key):
            if act_heavy:
                Sc.op("act", lambda e: e.copy(dst, ps[:]), reads=[psn], writes=[key])
            else:
                Sc.op("dve", lambda e: e.tensor_copy(dst, ps[:]), reads=[psn], writes=[key])

        if c == 3:
            eg, egn = self.eg_r.next()
            Sc.op("act", lambda e: e.activation(eg[:], ps[:], AF.Exp, scale=-1.0), reads=[psn], writes=[egn])
            if act_heavy:
                Sc.op("act", lambda e: e.activation(eg[:], eg[:], AF.Ln, bias=1.0), reads=[egn], writes=[egn])
                Sc.op("act", lambda e: e.activation(eg[:], eg[:], AF.Exp, scale=-1.0), reads=[egn], writes=[egn])
            else:
                Sc.op("dve", lambda e: e.tensor_scalar(eg[:], eg[:], 1.0, None, ALU.add), reads=[egn], writes=[egn])
                Sc.op("dve", lambda e: e.reciprocal(eg[:], eg[:]), reads=[egn], writes=[egn])
            Sc.op("dve", lambda e: e.tensor_tensor(st[:, 3, :], ps[:], eg[:], ALU.mult),
                  reads=[psn, egn], writes=[(stn, 3)])
        elif c == 2:
            cp(st[:, 2, :], (stn, 2))
        elif kind == "dil":
            gcol = cst["qg"] if c == 0 else cst["kg"]
            sq, sqn = self.sq_r.next()
            Sc.op("act", lambda e: e.activation(sq[:], ps[:], AF.Square), reads=[psn], writes=[sqn])
            p2, p2n = self.PSP.next()
            Sc.op("pe", lambda e: e.matmul(p2[:], cst["ones"], sq[:], start=True, stop=True),
                  reads=[sqn, "consts"], writes=[p2n])
            rs, rsn = self.rs_r.next()
            Sc.op("act", lambda e: e.activation(rs[:], p2[:], AF.Ln, bias=128.0 * EPS), reads=[p2n], writes=[rsn])
            Sc.op("act", lambda e: e.activation(rs[:], rs[:], AF.Exp, scale=-0.5), reads=[rsn], writes=[rsn])
            Sc.op("dve", lambda e: e.scalar_tensor_tensor(st[:, c, :], ps[:], gcol, rs[:], ALU.mult, ALU.mult),
                  reads=[psn, rsn, "gcols"], writes=[(stn, c)])
        elif c == 0:
            if kind == "sb":
                sc = 1.0 / math.sqrt(128.0)
                Sc.op("dve", lambda e: e.tensor_scalar(st[:, 0, :], ps[:], sc, None, ALU.mult),
                      reads=[psn], writes=[(stn, 0)])
            else:
                cp(st[:, 0, :], (stn, 0))
        else:
            cp(st[:, 1, :], (stn, 1))


def load_head_tracked(nc, Sc, hb, hbn, scr, hd):
    for c in range(4):
        Sc.dma("sp", hb[:, c, :], scr[hd, c, :, :], reads=[("scr", hd)], writes=[(hbn, c)])


def mixer_stream(nc, es, Sc, PSA, layer, x, g, W, extra, hd_d, scr, out, nheads, c, is_output=False):
    PS, PSO, PSP = PSA.sub([0, 1, 2]), PSA.sub([3, 4]), PSA.sub([5, 6, 7])
    bank_split = {"sb": ([0, 1, 2, 3], [4, 5], [6, 7]),
                  "dil": ([0, 1, 2, 3], [4], [4, 5, 6, 7]),
                  "hgrn": ([0, 1, 2], [3, 4], [5, 6, 7])}
    heads = list(range(nheads))
    L = "L%d" % layer
    if layer == 0:
        gct = es.enter_context(_sbt(nc, L + "gct", [128, 2], F32))
        Sc.dma("sp", gct[:], extra, writes=["gcols"])
        c["qg"], c["kg"] = gct[:, 0:1], gct[:, 1:2]
        kinds = {hd: ("sb" if hd < nheads // 2 else "dil") for hd in heads}
    else:
        cst2, lbl, ogd = extra
        kinds = {hd: "hgrn" for hd in heads}
        c2 = es.enter_context(_sbt(nc, "c2", [128, 4608], BF16))
        Sc.dma("pool", c2[:], cst2, writes=["consts2"])
        hct = es.enter_context(_sbt(nc, "hct", [128, 2, nheads], F32))
        lbt = es.enter_context(_sbt(nc, "lbt", [128, nheads], F32))
        omlt = es.enter_context(_sbt(nc, "omlt", [128, nheads], F32))
        ogt = es.enter_context(_sbt(nc, "ogt", [128, 1], F32))
        Sc.dma("sp", hct[:], lbl, writes=["hct"])
        Sc.dma("sp", ogt[:], ogd, writes=["ogt"])
        Sc.op("dve", lambda e: e.tensor_tensor(omlt[:], hct[:, 0, :], hct[:, 1, :], ALU.subtract),
              reads=["hct"], writes=["omlt"])
        Sc.op("act", lambda e: e.activation(omlt[:], omlt[:], AF.Exp), reads=["omlt"], writes=["omlt"])
        Sc.op("dve", lambda e: e.tensor_scalar(lbt[:], omlt[:], 1.0, None, ALU.add), reads=["omlt"], writes=["lbt"])
        Sc.op("dve", lambda e: e.reciprocal(lbt[:], lbt[:]), reads=["lbt"], writes=["lbt"])
        Sc.op("dve", lambda e: e.tensor_tensor(omlt[:], omlt[:], lbt[:], ALU.mult),
              reads=["omlt", "lbt"], writes=["hconst"])
    phase1_stream(nc, Sc, PSA, x, g, hd_d, c)
    with ExitStack() as es3:
        P = ProjStream(nc, es3, Sc, PSP, c, heads, kinds, W, hd_d, scr, nhs=(3 if layer == 0 else 2))
        fill = P.step
        if layer == 0:
            groups = [[h for h in heads if kinds[h] == "sb"], [h for h in heads if kinds[h] == "dil"]]
        else:
            groups = [heads]
        for grp in groups:
            kind = kinds[grp[0]]
            P.pref = "act" if kind == "hgrn" else "dve"
            bs = bank_split[kind]
            PS, PSO = PSA.sub(bs[0]), PSA.sub(bs[1])
            P.PSP = PSA.sub(bs[2])
            with ExitStack() as es4:
                R = {"o": Rot(nc, es4, "o", [128, 512], BF16, 2)}
                nhb = 2 if kind == "sb" else 1
                hbs = [es4.enter_context(_sbt(nc, "hb%d" % i, [128, 4, S], BF16)) for i in range(nhb)]
                hbn = ["hb%d" % i for i in range(nhb)]
                if kind == "sb":
                    vtok0 = es4.enter_context(_sbt(nc, "vtok0", [128, 32, 128], BF16))
                    R.update({"e": Rot(nc, es4, "e", [128, 512], F32, 2), "sp": Rot(nc, es4, "sp", [128, 512], BF16, 5),
                              "a": Rot(nc, es4, "a", [128, 512], BF16, 4),
                              "carry": Rot(nc, es4, "carry", [128, 512], BF16, 4)})
                elif kind == "dil":
                    vtoks = [es4.enter_context(_sbt(nc, "vtok%d" % i, [128, 32, 128], BF16)) for i in range(3)]
                    vtokn = ["vtok0", "vtok1", "vtok2"]
                    UZ = es4.enter_context(_sbt(nc, "UZ", [128, 2, S], F32))
                    R.update({"p": Rot(nc, es4, "p", [128, 256], BF16, 4), "rz": Rot(nc, es4, "rz", [128, 512], F32, 2)})
                else:
                    T = {}
                    for nm in ("E", "B2", "CUM"):
                        T[nm] = es4.enter_context(_sbt(nc, nm, [128, S // 2], F32))
                    for nm in ("KT", "QT"):
                        T[nm] = es4.enter_context(_sbt(nc, nm, [128, S], BF16))
                    for nm in ("AM", "ST"):
                        T[nm] = es4.enter_context(_sbt(nc, nm, [128, 32, 128], BF16))
                    T["EL"] = es4.enter_context(_sbt(nc, "EL", [128, 32], F32))
                    T["ktok"] = es4.enter_context(_sbt(nc, "ktok", [128, 32, 128], BF16))
                    T["vtok"] = es4.enter_context(_sbt(nc, "vtok", [128, 32, 128], BF16))
                    T["rmask"], T["mask4"] = c2[:, 0:4096], c2[:, 4096:4608]
                    R.update({"sq": Rot(nc, es4, "sq3", [128, 512], BF16, 2), "rs": Rot(nc, es4, "rs3", [128, 512], F32, 2)})
                P.finish_head(grp[0])
                load_head_tracked(nc, Sc, hbs[0], hbn[0], scr, grp[0])
                if kind == "dil":
                    P.step(40)
                for i, hd in enumerate(grp):
                    hb, hn = hbs[i % nhb], hbn[i % nhb]
                    if kind == "sb":
                        make_vtok(nc, Sc, PS, c, hb, hn, vtok0, "vtok0", 1)
                        sb_head_v2(nc, Sc, PS, PSO, c, R, hb, hn, vtok0, "vtok0", out[hd], is_output, fill=fill)
                    elif kind == "dil":
                        dil_head_v2(nc, Sc, PS, c, R, hb, hn, vtoks, vtokn, UZ, out[hd], is_output, fill=fill)
                    else:
                        hgrn_head_v2(nc, Sc, PS, PSO, c, R, T, hb, hn, lbt[:, hd:hd + 1], omlt[:, hd:hd + 1],
                                     ogt[:, 0:1], out[hd], is_output, fill=fill)
                    if i + 1 < len(grp):
                        P.finish_head(grp[i + 1])
                        load_head_tracked(nc, Sc, hbs[(i + 1) % nhb], hbn[(i + 1) % nhb], scr, grp[i + 1])
                        if kind == "dil":
                            P.step(40)
                Sc.barrier()


def build_outproj():
    nc = bass.Bass("TRN2", target_bir_lowering=False)
    mixT = nc.dram_tensor("mixT", [32, 128, 2048], BF16, kind="ExternalInput").ap()
    w = nc.dram_tensor("w", [128, 32, 2048], F32, kind="ExternalInput").ap()
    x = nc.dram_tensor("x", [2048, 2048], F32, kind="ExternalInput").ap()
    y = nc.dram_tensor("y", [2048, 2048], F32, kind="ExternalOutput").ap()
    with ExitStack() as es:
        Sc = Sched(nc, es)
        PS = PsumPool(nc, es)
        outproj_body(nc, es, Sc, PS, mixT, w, x, y, 2048)
        Sc.finish()
    return nc


def outproj_body(nc, es_unused, Sc, PS, mixT, w, x, y, ntok, is_output=True):
    with ExitStack() as es:
        wsb = es.enter_context(_sbt(nc, "wsb", [128, 32, 2048], BF16))
        for g in range(8):
            Sc.dma("pool", wsb[:, g * 4:(g + 1) * 4, :], w[:, g * 4:(g + 1) * 4, :], writes=[("wsb", g)])
        mx_r = Rot(nc, es, "mx", [128, 32, 256], BF16, 2)
        xt_r = Rot(nc, es, "xo", [128, 2048], F32, 2)
        for t5 in range(ntok // 256):
            mx, mxn = mx_r.next()
            for g in range(4):
                src = mixT[g * 8:(g + 1) * 8, :, t5 * 256:(t5 + 1) * 256].rearrange("c p t -> p c t")
                Sc.dma("act", mx[:, g * 8:(g + 1) * 8, :], src, writes=[(mxn, g)])
            for ti in range(2):
                tt = t5 * 2 + ti
                xt, xtn = xt_r.next()
                Sc.dma("pool", xt[:], x[tt * 128:(tt + 1) * 128, :], writes=[xtn])
                for n in range(4):
                    ps, psn = PS.next()
                    for kc in range(32):
                        Sc.op("pe", lambda e: e.matmul(ps[:], mx[:, kc, ti * 128:(ti + 1) * 128],
                                                       wsb[:, kc, n * 512:(n + 1) * 512],
                                                       start=(kc == 0), stop=(kc == 31)),
                              reads=[(mxn, kc // 8), ("wsb", kc // 4)], writes=[psn])
                    Sc.op("dve", lambda e: e.tensor_tensor(xt[:, n * 512:(n + 1) * 512], ps[:],
                                                           xt[:, n * 512:(n + 1) * 512], ALU.add),
                          reads=[psn, xtn], writes=[(xtn, n)])
                Sc.dma("sp", y[tt * 128:(tt + 1) * 128, :], xt[:], reads=[xtn], is_output=is_output)
        Sc.barrier()


def build_mixer(layer, nheads=16):
    nc = bass.Bass("TRN2", target_bir_lowering=False)
    x = nc.dram_tensor("x", [S, D], F32, kind="ExternalInput").ap()
    g = nc.dram_tensor("g", [128, D], F32, kind="ExternalInput").ap()
    W = nc.dram_tensor("W", [nheads, 128, NKC, 512], F32, kind="ExternalInput").ap()
    cst = nc.dram_tensor("cst", [128, 2816], F32, kind="ExternalInput").ap()
    if layer == 0:
        gc = nc.dram_tensor("gc", [128, 2], F32, kind="ExternalInput").ap()
    else:
        cst2 = nc.dram_tensor("cst2", [128, 4608], F32, kind="ExternalInput").ap()
        lbl = nc.dram_tensor("lbl", [128, 2, nheads], F32, kind="ExternalInput").ap()
        ogd = nc.dram_tensor("og", [128, 1], F32, kind="ExternalInput").ap()
    scr = nc.dram_tensor("scr", [nheads, 4, 128, S], BF16, kind="Internal").ap()
    out = nc.dram_tensor("mixT", [nheads, 128, S], BF16, kind="ExternalOutput").ap()
    with ExitStack() as es:
        Sc = Sched(nc, es)
        PSA = PsumPool(nc, es)
        mixer_body(nc, es, Sc, PSA, layer, x, g, W, cst, gc if layer == 0 else (cst2, lbl, ogd), scr, out, nheads)
        Sc.finish()
    return nc


def mixer_body(nc, es, Sc, PSA, layer, x, g, W, cst, extra, scr, out, nheads=16, c=None, is_output=True):
    PS, PSO = PSA.sub([0, 1, 2, 3, 4, 5]), PSA.sub([6, 7])
    if c is None:
        c = load_consts(nc, es, Sc, cst)
    heads = list(range(nheads))
    L = "L%d" % layer
    if layer == 0:
        gct = es.enter_context(_sbt(nc, L + "gct", [128, 2], F32))
        Sc.dma("sp", gct[:], extra, writes=["gcols"])
        c["qg"], c["kg"] = gct[:, 0:1], gct[:, 1:2]
        kinds = {hd: ("sb" if hd < nheads // 2 else "dil") for hd in heads}
    else:
        cst2, lbl, ogd = extra
        kinds = {hd: "hgrn" for hd in heads}
        c2 = es.enter_context(_sbt(nc, "c2", [128, 4608], BF16))
        Sc.dma("pool", c2[:], cst2, writes=["consts2"])
        hct = es.enter_context(_sbt(nc, "hct", [128, 2, nheads], F32))
        lbt = es.enter_context(_sbt(nc, "lbt", [128, nheads], F32))
        omlt = es.enter_context(_sbt(nc, "omlt", [128, nheads], F32))
        ogt = es.enter_context(_sbt(nc, "ogt", [128, 1], F32))
        Sc.dma("sp", hct[:], lbl, writes=["hct"])
        Sc.dma("sp", ogt[:], ogd, writes=["ogt"])
        Sc.op("dve", lambda e: e.tensor_tensor(omlt[:], hct[:, 0, :], hct[:, 1, :], ALU.subtract),
              reads=["hct"], writes=["omlt"])
        Sc.op("act", lambda e: e.activation(omlt[:], omlt[:], AF.Exp), reads=["omlt"], writes=["omlt"])
        Sc.op("dve", lambda e: e.tensor_scalar(lbt[:], omlt[:], 1.0, None, ALU.add), reads=["omlt"], writes=["lbt"])
        Sc.op("dve", lambda e: e.reciprocal(lbt[:], lbt[:]), reads=["lbt"], writes=["lbt"])
        Sc.op("dve", lambda e: e.tensor_tensor(omlt[:], omlt[:], lbt[:], ALU.mult),
              reads=["omlt", "lbt"], writes=["hconst"])
    phase12(nc, Sc, PSA, x, g, W, scr, heads, kinds, c)
    es = ExitStack()
    es.__enter__()
    hbs = [es.enter_context(_sbt(nc, "hb%d" % i, [128, 4, S], BF16)) for i in range(2)]
    hbn = ["hb0", "hb1"]
    R = {"o": Rot(nc, es, "o", [128, 512], BF16, 2)}
    if layer == 0:
        vtoks = [es.enter_context(_sbt(nc, "vtok%d" % i, [128, 32, 128], BF16)) for i in range(3)]
        vtokn = ["vtok0", "vtok1", "vtok2"]
        UZ = es.enter_context(_sbt(nc, "UZ", [128, 2, S], F32))
        R.update({"e": Rot(nc, es, "e", [128, 512], F32, 3), "sp": Rot(nc, es, "sp", [128, 512], BF16, 5),
                  "a": Rot(nc, es, "a", [128, 512], BF16, 4), "carry": Rot(nc, es, "carry", [128, 512], BF16, 4),
                  "p": Rot(nc, es, "p", [128, 256], BF16, 4), "rz": Rot(nc, es, "rz", [128, 512], F32, 2)})
    else:
        T = {}
        for nm in ("E", "B2", "CUM"):
            T[nm] = es.enter_context(_sbt(nc, nm, [128, S], F32))
        for nm in ("KT", "QT"):
            T[nm] = es.enter_context(_sbt(nc, nm, [128, S], BF16))
        for nm in ("AM", "PSE", "ST"):
            T[nm] = es.enter_context(_sbt(nc, nm, [128, 32, 128], BF16))
        T["EL"] = es.enter_context(_sbt(nc, "EL", [128, 32], F32))
        T["ktok"] = es.enter_context(_sbt(nc, "ktok", [128, 32, 128], BF16))
        T["vtok"] = es.enter_context(_sbt(nc, "vtok", [128, 32, 128], BF16))
        T["rmask"], T["mask4"] = c2[:, 0:4096], c2[:, 4096:4608]
        R.update({"am": Rot(nc, es, "am", [128, 128], BF16, 3), "state": Rot(nc, es, "state", [128, 128], BF16, 3),
                  "pse": Rot(nc, es, "pse", [128, 128], F32, 2), "sq": Rot(nc, es, "sq3", [128, 512], BF16, 2),
                  "rs": Rot(nc, es, "rs3", [128, 512], F32, 2)})
    load_head(nc, Sc, hbs[0], hbn[0], scr, heads[0])
    for i, hd in enumerate(heads):
        hb, hn = hbs[i % 2], hbn[i % 2]
        if i + 1 < len(heads):
            load_head(nc, Sc, hbs[(i + 1) % 2], hbn[(i + 1) % 2], scr, heads[i + 1])
        if kinds[hd] == "sb":
            make_vtok(nc, Sc, PS, c, hb, hn, vtoks[0], vtokn[0], 1)
            sb_head_v2(nc, Sc, PS, PSO, c, R, hb, hn, vtoks[0], vtokn[0], out[hd], is_output)
        elif kinds[hd] == "dil":
            dil_head_v2(nc, Sc, PS, c, R, hb, hn, vtoks, vtokn, UZ, out[hd], is_output)
        else:
            hgrn_head_v2(nc, Sc, PS, PSO, c, R, T, hb, hn, lbt[:, hd:hd + 1], omlt[:, hd:hd + 1], ogt[:, 0:1], out[hd], is_output)
    Sc.barrier()
    es.close()


def build_fused():
    nc = bass.Bass("TRN2", target_bir_lowering=False)
    x = nc.dram_tensor("x", [S, D], F32, kind="ExternalInput").ap()
    g0 = nc.dram_tensor("g0", [128, D], F32, kind="ExternalInput").ap()
    g1 = nc.dram_tensor("g1", [128, D], F32, kind="ExternalInput").ap()
    W0 = nc.dram_tensor("W0", [32, 128, NKC, 512], F32, kind="ExternalInput").ap()
    W1 = nc.dram_tensor("W1", [32, 128, NKC, 512], F32, kind="ExternalInput").ap()
    wo0 = nc.dram_tensor("wo0", [128, 32, 2048], F32, kind="ExternalInput").ap()
    wo1 = nc.dram_tensor("wo1", [128, 32, 2048], F32, kind="ExternalInput").ap()
    cst = nc.dram_tensor("cst", [128, 2816], F32, kind="ExternalInput").ap()
    gc = nc.dram_tensor("gc", [128, 2], F32, kind="ExternalInput").ap()
    cst2 = nc.dram_tensor("cst2", [128, 4608], F32, kind="ExternalInput").ap()
    lbl = nc.dram_tensor("lbl", [128, 2, 32], F32, kind="ExternalInput").ap()
    ogd = nc.dram_tensor("og", [128, 1], F32, kind="ExternalInput").ap()
    scr = nc.dram_tensor("scr", [32, 4, 128, S], BF16, kind="Internal").ap()
    hd_d = nc.dram_tensor("hdn_d", [8, 128, NKC, 512], BF16, kind="Internal").ap()
    mixT = nc.dram_tensor("mixT", [32, 128, S], BF16, kind="Internal").ap()
    x1 = nc.dram_tensor("x1", [S, D], F32, kind="Internal").ap()
    y = nc.dram_tensor("y", [S, D], F32, kind="ExternalOutput").ap()
    with ExitStack() as es:
        Sc = Sched(nc, es)
        PSA = PsumPool(nc, es)
        c = load_consts(nc, es, Sc, cst)
        mixer_stream(nc, es, Sc, PSA, 0, x, g0, W0, gc, hd_d, scr, mixT, 32, c)
        outproj_body(nc, None, Sc, PSA, mixT, wo0, x, x1, S, is_output=False)
        mixer_stream(nc, es, Sc, PSA, 1, x1, g1, W1, (cst2, lbl, ogd), hd_d, scr, mixT, 32, c)
        outproj_body(nc, None, Sc, PSA, mixT, wo1, x1, y, S, is_output=True)
        Sc.finish()
        print("instr counts", Sc.cnt, "dma", Sc.dcnt, "waits", Sc.nwaits)
    return nc


N_CORES = 4
ACTIVE = (0, 1, 2, 3)


def _prep_w(w_in, cols):
    return np.ascontiguousarray(w_in[:, cols].reshape(NKC, 128, 512).transpose(1, 0, 2))


def _cols_l0():
    out = []
    for hd in range(32):
        if hd < 16:
            h, base = hd, (0, 2048, 4096, 12288)
        else:
            h, base = hd - 16, (6144, 8192, 10240, 12288 + 2048)
        out.append(np.concatenate([b + np.arange(h * 128, (h + 1) * 128) for b in base]))
    return out


def _cols_l1():
    return [np.concatenate([b + np.arange(h * 128, (h + 1) * 128) for b in (0, 4096, 8192, 12288)])
            for h in range(32)]


def _make_cst2():
    c2 = np.zeros((128, 4608), np.float32)
    c2[:, :4096] = (np.arange(4096) % 128 != 0)[None, :]
    c2[:, 4096:] = np.tile(np.arange(128)[:, None] <= np.arange(128)[None, :], (1, 4))
    return c2


def kernel(x, norm_even, w_in_even, q_norm_even, k_norm_even, w_out_even,
           norm_odd, w_in_odd, lb_logits, o_norm_odd, w_out_odd):
    f32 = np.float32
    x = np.asarray(x, f32)
    w0 = np.asarray(w_in_even[0], f32)
    w1 = np.asarray(w_in_odd[0], f32)
    shared = {
        "g0": np.ascontiguousarray(np.broadcast_to(np.asarray(norm_even[0], f32), (128, D))),
        "g1": np.ascontiguousarray(np.broadcast_to(np.asarray(norm_odd[0], f32), (128, D))),
        "W0": np.stack([_prep_w(w0, cl) for cl in _cols_l0()]),
        "W1": np.stack([_prep_w(w1, cl) for cl in _cols_l1()]),
        "wo0": np.ascontiguousarray(np.asarray(w_out_even[0], f32).reshape(32, 128, 2048).transpose(1, 0, 2)),
        "wo1": np.ascontiguousarray(np.asarray(w_out_odd[0], f32).reshape(32, 128, 2048).transpose(1, 0, 2)),
        "cst": make_consts_np(),
        "gc": np.ascontiguousarray(np.stack([np.asarray(q_norm_even[0], f32), np.asarray(k_norm_even[0], f32)], 1)),
        "cst2": _make_cst2(),
        "lbl": np.ascontiguousarray(np.asarray(lb_logits, f32).reshape(2, 32, 128).transpose(2, 0, 1)),
        "og": np.ascontiguousarray(np.asarray(o_norm_odd[0], f32).reshape(128, 1)),
    }
    big = ("W0", "W1", "wo0", "wo1")
    idle = dict(shared, x=np.zeros((S, D), f32))
    for k in big:
        idle[k] = np.zeros_like(shared[k])
    in_maps = []
    for core in range(N_CORES):
        if core in ACTIVE:
            in_maps.append(dict(shared, x=np.ascontiguousarray(x[ACTIVE.index(core)])))
        else:
            in_maps.append(idle)
    nc = build_fused()
    res = run_bass_kernel_spmd(nc, in_maps, core_ids=list(range(N_CORES)))
    return np.stack([res.results[core]["y"] for core in ACTIVE])
```
